# Optimizing a Trainium2 kernel written in Bass

```python
import functools
import jax, jax.numpy as jnp
from jax import lax
import numpy as np

D_MODEL = 1024
BATCH = 2
SEQ = 8192
DEPTH = 1
DEC_BATCH = 128
DEC_SEQ = 8
PAST_LEN = 16384
PAGE_SIZE = 128

HEAD_DIM = 64
ROT_DIM = HEAD_DIM // 4
ROPE_THETA = 500000.0
ATTN_SCALE = HEAD_DIM ** -0.5
BLOCK = 128
N_HEADS_A = 8
N_KV_A = 2
GROUP_A = N_HEADS_A // N_KV_A
WINDOW_A = 128
DIL_GROUPS = ((128, 1), (512, 4), (2048, 16))
N_DIL = 3
N_KV_B = 4
N_HEADS_B = N_DIL * N_KV_B
WINDOW_B = 2048
Q_A_W = N_HEADS_A * HEAD_DIM
KV_A_W = N_KV_A * HEAD_DIM
Q_B_W = N_HEADS_B * HEAD_DIM
KV_B_W = N_KV_B * HEAD_DIM
IN_COLS = Q_A_W + 2 * KV_A_W + Q_B_W + 2 * KV_B_W + 2 * D_MODEL
N_MEM = 256
MEM_HEADS = 4
MEM_HEAD_DIM = 128
MEM_WIDTH = MEM_HEADS * MEM_HEAD_DIM
MEM_SCALE = MEM_HEAD_DIM ** -0.5
D_FF = 2816
RMS_EPS = 1e-6

kernel_name = 'hybrid_swa_dilated_memory_decoder_step'


def rmsnorm(x, g):
    xf = x.astype(jnp.float32)
    xf = xf * lax.rsqrt(jnp.mean(xf * xf, axis=-1, keepdims=True) + RMS_EPS)
    return (xf * g.astype(jnp.float32)).astype(x.dtype)


def swiglu_half(x, norm_g, w_gate, w_up, w_down):
    u = rmsnorm(x, norm_g)
    return x + 0.5 * ((jax.nn.silu(u @ w_gate) * (u @ w_up)) @ w_down)


def rope(x, pos):
    half = ROT_DIM // 2
    inv_freq = jnp.power(jnp.float32(ROPE_THETA), -jnp.arange(half, dtype=jnp.float32) / half)
    ang = pos.astype(jnp.float32)[:, None] * inv_freq[None, :]
    cos = jnp.cos(ang)[:, None, :].astype(x.dtype)
    sin = jnp.sin(ang)[:, None, :].astype(x.dtype)
    x1, x2, rest = x[..., :half], x[..., half:ROT_DIM], x[..., ROT_DIM:]
    return jnp.concatenate([x1 * cos - x2 * sin, x2 * cos + x1 * sin, rest], axis=-1)


def band_attn(q, k, v, steps):
    n, s, kv, g, dh = q.shape
    pad = (-s) % BLOCK
    if pad:
        q = jnp.pad(q, ((0, 0), (0, pad), (0, 0), (0, 0), (0, 0)))
        k = jnp.pad(k, ((0, 0), (0, pad), (0, 0), (0, 0)))
        v = jnp.pad(v, ((0, 0), (0, pad), (0, 0), (0, 0)))
    nb = (s + pad) // BLOCK
    qb = q.reshape(n, nb, BLOCK, kv, g, dh)

    def with_prev(x):
        xb = x.reshape(n, nb, BLOCK, kv, dh)
        prev = jnp.concatenate([jnp.zeros_like(xb[:, :1]), xb[:, :-1]], axis=1)
        return jnp.concatenate([prev, xb], axis=2)

    kk, vv = with_prev(k), with_prev(v)
    sc = jnp.einsum('nbqkgd,nbjkd->nbkgqj', qb, kk, preferred_element_type=jnp.float32) * ATTN_SCALE
    qi = jnp.arange(BLOCK)[:, None]
    kj = jnp.arange(2 * BLOCK)[None, :]
    dist = qi + BLOCK - kj
    band = (dist >= 0) & (dist <= steps)
    has_prev = (jnp.arange(nb) > 0)[:, None, None] | (kj >= BLOCK)[None]
    mask = band[None] & has_prev
    sc = jnp.where(mask[None, :, None, None], sc, -jnp.inf)
    lse = jax.nn.logsumexp(sc, axis=-1)
    p = jnp.exp(sc - lse[..., None]).astype(v.dtype)
    o = jnp.einsum('nbkgqj,nbjkd->nbqkgd', p, vv, preferred_element_type=jnp.float32)
    o = o.reshape(n, nb * BLOCK, kv, g, dh)[:, :s]
    lse = jnp.transpose(lse, (0, 1, 4, 2, 3)).reshape(n, nb * BLOCK, kv, g)[:, :s]
    return o, lse


def dilated_band_attn(q, k, v, dilation, steps):
    n, s = q.shape[:2]
    sd = s // dilation

    def split(x):
        rest = x.shape[2:]
        x = jnp.moveaxis(x.reshape(n, sd, dilation, *rest), 2, 1)
        return x.reshape(n * dilation, sd, *rest)

    def merge(x):
        rest = x.shape[2:]
        x = jnp.moveaxis(x.reshape(n, dilation, sd, *rest), 1, 2)
        return x.reshape(n, s, *rest)

    o, lse = band_attn(split(q), split(k), split(v), steps)
    return merge(o), merge(lse)


def gather_attn(q, k_all, v_all, dilation, steps):
    t = q.shape[1]
    n_past = k_all.shape[1] - t
    idx = (n_past + jnp.arange(t))[:, None] - dilation * jnp.arange(steps + 1)[None, :]
    valid = idx >= 0
    idx = jnp.maximum(idx, 0)
    kg = jnp.take(k_all, idx, axis=1)
    vg = jnp.take(v_all, idx, axis=1)
    sc = jnp.einsum('ntkgd,ntjkd->ntkgj', q, kg, preferred_element_type=jnp.float32) * ATTN_SCALE
    sc = jnp.where(valid[None, :, None, None, :], sc, -jnp.inf)
    lse = jax.nn.logsumexp(sc, axis=-1)
    p = jnp.exp(sc - lse[..., None]).astype(v_all.dtype)
    o = jnp.einsum('ntkgj,ntjkd->ntkgd', p, vg, preferred_element_type=jnp.float32)
    return o, lse


def cached_attn(q, k, v, dilation, steps, k_past, v_past):
    return gather_attn(q, jnp.concatenate([k_past, k], axis=1), jnp.concatenate([v_past, v], axis=1), dilation, steps)


def project_mixers(h, pos, mix_norm, w_in):
    n, s, _ = h.shape
    z = rmsnorm(h, mix_norm) @ w_in
    sizes = (Q_A_W, KV_A_W, KV_A_W, Q_B_W, KV_B_W, KV_B_W, D_MODEL, D_MODEL)
    q_a, k_a, v_a, q_b, k_b, v_b, g_a, g_b = jnp.split(z, np.cumsum(sizes)[:-1].tolist(), axis=-1)
    q_a = rope(q_a.reshape(n, s, N_HEADS_A, HEAD_DIM), pos).reshape(n, s, N_KV_A, GROUP_A, HEAD_DIM)
    k_a = rope(k_a.reshape(n, s, N_KV_A, HEAD_DIM), pos)
    v_a = v_a.reshape(n, s, N_KV_A, HEAD_DIM)
    q_b = rope(q_b.reshape(n, s, N_HEADS_B, HEAD_DIM), pos).reshape(n, s, N_DIL, N_KV_B, HEAD_DIM)
    k_b = rope(k_b.reshape(n, s, N_KV_B, HEAD_DIM), pos)
    v_b = v_b.reshape(n, s, N_KV_B, HEAD_DIM)
    return q_a, k_a, v_a, q_b, k_b, v_b, g_a, g_b


def memory_kv(mem, norm_g, w_k, w_v):
    n, m, _ = mem.shape
    u = rmsnorm(mem, norm_g)
    return ((u @ w_k).reshape(n, m, MEM_HEADS, MEM_HEAD_DIM), (u @ w_v).reshape(n, m, MEM_HEADS, MEM_HEAD_DIM))


def cross_attn(h, mem_k, mem_v, norm_g, w_q, w_o):
    n, s, _ = h.shape
    q = (rmsnorm(h, norm_g) @ w_q).reshape(n, s, MEM_HEADS, MEM_HEAD_DIM)
    sc = jnp.einsum('nshd,nmhd->nhsm', q, mem_k, preferred_element_type=jnp.float32) * MEM_SCALE
    p = jax.nn.softmax(sc, axis=-1).astype(mem_v.dtype)
    o = jnp.einsum('nhsm,nmhd->nshd', p, mem_v, preferred_element_type=jnp.float32)
    return o.astype(h.dtype).reshape(n, s, MEM_WIDTH) @ w_o


def decoder_layer(x, pos, attn_a, attn_b, mem_k, mem_v, ffn1, mix, cross, ffn2):
    n, s, _ = x.shape
    h = swiglu_half(x, *ffn1)
    mix_norm, w_in, sink, w_branch_a, w_branch_b, w_out = mix
    q_a, k_a, v_a, q_b, k_b, v_b, g_a, g_b = project_mixers(h, pos, mix_norm, w_in)
    o_a, lse_a = attn_a(q_a, k_a, v_a)
    o_a = o_a * jax.nn.sigmoid(lse_a - sink.astype(jnp.float32).reshape(N_KV_A, GROUP_A))[..., None]
    outs, lses = [], []
    for g, (window, dilation) in enumerate(DIL_GROUPS):
        o, lse = attn_b(q_b[:, :, g, :, None], k_b, v_b, dilation, window // dilation)
        outs.append(o)
        lses.append(lse)
    wts = jax.nn.softmax(jnp.stack(lses), axis=0)
    o_b = jnp.sum(wts[..., None] * jnp.stack(outs), axis=0)
    branch_a = o_a.astype(h.dtype).reshape(n, s, Q_A_W) @ w_branch_a
    branch_b = o_b.astype(h.dtype).reshape(n, s, KV_B_W) @ w_branch_b
    h = h + (jax.nn.sigmoid(g_a) * branch_a + jax.nn.sigmoid(g_b) * branch_b) @ w_out
    h = h + cross_attn(h, mem_k, mem_v, *cross)
    h = swiglu_half(h, *ffn2)
    return h, k_a, v_a, k_b, v_b


def setup_inputs(seed: int = 0) -> dict:
    key = jax.random.key(seed)
    keys = iter(jax.random.split(key, 40))

    def normal(shape, scale=1.0):
        return scale * jax.random.normal(next(keys), shape, jnp.float32)

    def gain(shape):
        return 1.0 + 0.1 * normal(shape)

    l_a = min(WINDOW_A, PAST_LEN)
    l_b = min(WINDOW_B, PAST_LEN)
    d = D_MODEL
    return {
        'x_prompt': normal((BATCH, SEQ, d)),
        'x_sample': normal((DEC_BATCH, DEC_SEQ, d)),
        'cache_swa_k': normal((DEPTH, DEC_BATCH, l_a, N_KV_A, HEAD_DIM)),
        'cache_swa_v': normal((DEPTH, DEC_BATCH, l_a, N_KV_A, HEAD_DIM)),
        'cache_dil_k': normal((DEPTH, DEC_BATCH, l_b, N_KV_B, HEAD_DIM)),
        'cache_dil_v': normal((DEPTH, DEC_BATCH, l_b, N_KV_B, HEAD_DIM)),
        'cache_mem_k': normal((DEPTH, DEC_BATCH, N_MEM, MEM_HEADS, MEM_HEAD_DIM)),
        'cache_mem_v': normal((DEPTH, DEC_BATCH, N_MEM, MEM_HEADS, MEM_HEAD_DIM)),
        'mem_prompt': normal((BATCH, N_MEM, d)),
        'ffn1_norm': gain((DEPTH, d)),
        'ffn1_w_gate': normal((DEPTH, d, D_FF), d ** -0.5),
        'ffn1_w_up': normal((DEPTH, d, D_FF), d ** -0.5),
        'ffn1_w_down': normal((DEPTH, D_FF, d), D_FF ** -0.5),
        'mix_norm': gain((DEPTH, d)),
        'w_in': normal((DEPTH, d, IN_COLS), d ** -0.5),
        'attn_sink': normal((DEPTH, N_HEADS_A)),
        'w_branch_a': normal((DEPTH, Q_A_W, d), Q_A_W ** -0.5),
        'w_branch_b': normal((DEPTH, KV_B_W, d), KV_B_W ** -0.5),
        'w_out': normal((DEPTH, d, d), d ** -0.5),
        'mem_q_norm': gain((DEPTH, d)),
        'mem_kv_norm': gain((DEPTH, d)),
        'w_mem_q': normal((DEPTH, d, MEM_WIDTH), d ** -0.5),
        'w_mem_k': normal((DEPTH, d, MEM_WIDTH), d ** -0.5),
        'w_mem_v': normal((DEPTH, d, MEM_WIDTH), d ** -0.5),
        'w_mem_o': normal((DEPTH, MEM_WIDTH, d), MEM_WIDTH ** -0.5),
        'ffn2_norm': gain((DEPTH, d)),
        'ffn2_w_gate': normal((DEPTH, d, D_FF), d ** -0.5),
        'ffn2_w_up': normal((DEPTH, d, D_FF), d ** -0.5),
        'ffn2_w_down': normal((DEPTH, D_FF, d), D_FF ** -0.5),
        'final_norm': gain((d,)),
    }


def reference(x_prompt, x_sample, cache_swa_k, cache_swa_v, cache_dil_k, cache_dil_v, cache_mem_k, cache_mem_v,
              mem_prompt, ffn1_norm, ffn1_w_gate, ffn1_w_up, ffn1_w_down, mix_norm, w_in, attn_sink, w_branch_a,
              w_branch_b, w_out, mem_q_norm, mem_kv_norm, w_mem_q, w_mem_k, w_mem_v, w_mem_o, ffn2_norm,
              ffn2_w_gate, ffn2_w_up, ffn2_w_down, final_norm):
    s = x_prompt.shape[1]
    t = x_sample.shape[1]
    pos_p = jnp.arange(s, dtype=jnp.int32)
    pos_s = PAST_LEN + jnp.arange(t, dtype=jnp.int32)
    keep_a = min(WINDOW_A, s)
    keep_b = min(WINDOW_B, s)
    prompt_attn_a = functools.partial(dilated_band_attn, dilation=1, steps=WINDOW_A)
    hp, hs = x_prompt, x_sample
    rows = []
    for l in range(DEPTH):
        ffn1 = (ffn1_norm[l], ffn1_w_gate[l], ffn1_w_up[l], ffn1_w_down[l])
        mix = (mix_norm[l], w_in[l], attn_sink[l], w_branch_a[l], w_branch_b[l], w_out[l])
        cross = (mem_q_norm[l], w_mem_q[l], w_mem_o[l])
        ffn2 = (ffn2_norm[l], ffn2_w_gate[l], ffn2_w_up[l], ffn2_w_down[l])
        mk_p, mv_p = memory_kv(mem_prompt, mem_kv_norm[l], w_mem_k[l], w_mem_v[l])
        hp, ka_p, va_p, kb_p, vb_p = decoder_layer(hp, pos_p, prompt_attn_a, dilated_band_attn,
                                                   mk_p, mv_p, ffn1, mix, cross, ffn2)
        sample_attn_a = functools.partial(cached_attn, dilation=1, steps=WINDOW_A,
                                          k_past=cache_swa_k[l], v_past=cache_swa_v[l])
        sample_attn_b = functools.partial(cached_attn, k_past=cache_dil_k[l], v_past=cache_dil_v[l])
        hs, ka_s, va_s, kb_s, vb_s = decoder_layer(hs, pos_s, sample_attn_a, sample_attn_b,
                                                   cache_mem_k[l], cache_mem_v[l], ffn1, mix, cross, ffn2)
        rows.append((ka_p[:, s - keep_a:], va_p[:, s - keep_a:], kb_p[:, s - keep_b:], vb_p[:, s - keep_b:],
                     mk_p, mv_p, ka_s, va_s, kb_s, vb_s))
    (swa_k_p, swa_v_p, dil_k_p, dil_v_p, mem_k_p, mem_v_p,
     swa_k_s, swa_v_s, dil_k_s, dil_v_s) = [jnp.stack(c) for c in zip(*rows)]
    y_prompt = rmsnorm(hp, final_norm)
    y_sample = rmsnorm(hs, final_norm)
    return (y_prompt, y_sample, swa_k_p, swa_v_p, dil_k_p, dil_v_p, mem_k_p, mem_v_p,
            swa_k_s, swa_v_s, dil_k_s, dil_v_s)
```

```python
from contextlib import ExitStack
import numpy as np
import concourse.bass as bass
import concourse.mybir as mybir
from concourse.bass_utils import run_bass_kernel_spmd

F32 = mybir.dt.float32
BF16 = mybir.dt.bfloat16
AF = mybir.ActivationFunctionType
ALU = mybir.AluOpType

P2 = 9
P2S = 9
P2X = 9
P2Y = 0
PBG = 3
HALF_W = 0
MKV = 1
P3 = 9
PBX = 9
SKIP_AS = False
PBN = 1
STAGE = 9

D = 1024
DFF = 2816
NT = 17
T_OWN = NT * 128
T_HALO = 2048
TBMAX = 640
OWN_BLOCKS = [(0, 512), (512, 512), (1024, 512), (1536, 640)]
HALO_BLOCKS = [(0, 512), (512, 512), (1024, 512), (1536, 512)]
ATTN_SCALE = 0.125
MEM_SCALE = 128 ** -0.5
EPS = 1e-6
ENGS = ("pe", "act", "dve", "pool", "sp")


class Sch:
    def __init__(self, nc, es, dummy=False):
        self.nc, self.es, self.dummy = nc, es, dummy
        self.cnt = {e: 0 for e in ENGS}
        self.ops = {e: [] for e in ENGS}
        self.seen = {e: {} for e in ENGS}
        self.res = {}
        self.dsem = {}
        if not dummy:
            self.sem = {e: es.enter_context(nc.semaphore("sem_" + e)) for e in ENGS}

    def dma_sem(self, name):
        if name not in self.dsem:
            h = None if self.dummy else self.es.enter_context(self.nc.semaphore("d_" + name))
            self.dsem[name] = [h, 0]
        return name

    @staticmethod
    def _flat(keys):
        out = []
        for k in keys:
            if isinstance(k, (tuple, list)):
                out.extend(Sch._flat(k))
            else:
                out.append(k)
        return out

    def _collect(self, eng, reads, writes):
        deps = {}

        def add(d):
            if d is not None and deps.get(d[0], 0) < d[1]:
                deps[d[0]] = d[1]
        for k in reads:
            r = self.res.get(k)
            if r is not None:
                add(r[0])
        for k in writes:
            r = self.res.get(k)
            if r is not None:
                add(r[0])
                for d in r[1]:
                    add(d)
        waits = []
        seen = self.seen[eng]
        for k, v in deps.items():
            if k == eng and eng in ("pe", "sp", "pool"):
                continue
            if seen.get(k, 0) >= v:
                continue
            seen[k] = v
            waits.append((k, v))
        return waits

    def _record(self, token, reads, writes):
        for k in reads:
            self.res.setdefault(k, [None, []])[1].append(token)
        for k in writes:
            self.res[k] = [token, []]

    def op(self, eng, fn, reads=(), writes=(), drain=False):
        if self.dummy:
            return
        reads, writes = self._flat(reads), self._flat(writes)
        waits = self._collect(eng, reads, writes)
        if drain and self.cnt[eng] > 0:
            waits.append((eng, self.cnt[eng]))
        self.cnt[eng] += 1
        self.ops[eng].append((waits, fn, ("e", eng)))
        self._record((eng, self.cnt[eng]), reads, writes)

    def dma(self, queue, fn, reads=(), writes=(), sem=None):
        if self.dummy:
            return
        reads, writes = self._flat(reads), self._flat(writes)
        self.dma_sem(sem)
        waits = self._collect(queue, reads, writes)
        d = self.dsem[sem]
        d[1] += 16
        self.ops[queue].append((waits, fn, ("d", sem)))
        self._record(("D:" + sem, d[1]), reads, writes)

    def barrier(self):
        if self.dummy:
            return
        allv = [(e, self.cnt[e]) for e in ENGS if self.cnt[e] > 0]
        allv += [("D:" + n, c) for n, (h, c) in self.dsem.items() if c > 0]
        for e in ENGS:
            waits = []
            for k, v in allv:
                if k == e or self.seen[e].get(k, 0) >= v:
                    continue
                self.seen[e][k] = v
                waits.append((k, v))
            if waits:
                self.ops[e].append((waits, None, None))
        self.res = {}

    def _semh(self, k):
        return self.dsem[k[2:]][0] if k.startswith("D:") else self.sem[k]

    def finalize(self):
        fin = [(e, self.cnt[e]) for e in ENGS if e != "sp" and self.cnt[e] > 0]
        fin += [("D:" + n, c) for n, (h, c) in self.dsem.items() if c > 0]
        for k, v in fin:
            assert v < 65000, (k, v)

        def run(e, eng):
            for waits, fn, inc in self.ops[e]:
                for k, v in waits:
                    eng.wait_ge(self._semh(k), v)
                if fn is None:
                    continue
                ins = fn(eng)
                if inc[0] == "e":
                    ins.then_inc(self.sem[inc[1]], 1)
                else:
                    ins.then_inc(self.dsem[inc[1]][0], 16)
            if e == "sp":
                for k, v in fin:
                    eng.wait_ge(self._semh(k), v)

        with self.nc.Block() as block:
            @block.tensor
            def _(eng):
                run("pe", eng)

            @block.scalar
            def _(eng):
                run("act", eng)

            @block.vector
            def _(eng):
                run("dve", eng)

            @block.gpsimd
            def _(eng):
                run("pool", eng)

            @block.sync
            def _(eng):
                run("sp", eng)


def nslices(n, step=512):
    return [(a, min(step, n - a)) for a in range(0, n, step)]


class Prog:
    def __init__(self, nc, es, S, wjobs=None):
        self.nc, self.es, self.S = nc, es, S
        self.recording = wjobs is None
        self.wjobs = [] if wjobs is None else wjobs
        self.wi = 0
        self.wloaded = 0
        self.psi = 0
        self.tmpi = 0
        self.uid = 0
        self.held = set()
        self.half_ok = False
        self.PTi = 0
        self.wscr = {}
        self.next_x = None
        self.mid_hook = None

    def alloc(self):
        nc, es = self.nc, self.es
        dt = lambda n, s, d, k="ExternalInput": nc.dram_tensor(n, s, d, kind=k).ap()
        self.xo = dt("xo", [T_OWN, D], F32)
        self.xh = dt("xh", [T_HALO, D], F32)
        self.xm = dt("xm", [256, D], F32)
        self.rope_o = dt("rope_o", [2, 128, T_OWN], F32)
        self.rope_h = dt("rope_h", [2, 128, T_HALO], F32)
        self.gn = dt("gn", [128, 48], F32)
        self.ident_d = dt("ident", [128, 128], F32)
        self.masks_d = dt("masks", [128, 8, 128], F32)
        self.sinkb = dt("sinkb", [2, 512], F32)
        self.maskS_d = dt("masks_s", [128, 408], F32)
        self.gfin_d = dt("gfin_row", [128, 1024], F32)
        self.w = {}
        for n, s in [("g1", [D, DFF]), ("u1", [D, DFF]), ("d1", [DFF, D]), ("g2", [D, DFF]), ("u2", [D, DFF]),
                     ("d2", [DFF, D]), ("q", [D, 2560]), ("kv", [D, 1152]), ("gate", [D, 2048]),
                     ("ba", [512, D]), ("bb", [256, D]), ("out", [D, D]), ("mq", [D, 512]), ("mk", [D, 512]),
                     ("mv", [D, 512]), ("mo", [512, D])]:
            self.w[n] = dt("w_" + n, s, F32)
        self.cswk = dt("cswk", [16, 128, 128], F32)
        self.cswv = dt("cswv", [16, 128, 128], F32)
        self.cdk = dt("cdk", [16, 2048, 256], F32)
        self.cdv = dt("cdv", [16, 2048, 256], F32)
        self.cmk = dt("cmk", [16, 256, 512], F32)
        self.cmv = dt("cmv", [16, 256, 512], F32)
        self.y = dt("y", [T_OWN, D], F32, "ExternalOutput")
        self.o_kv = dt("o_kv", [T_OWN, 768], F32, "ExternalOutput")
        self.o_mem = dt("o_mem", [256, 1024], F32, "ExternalOutput")
        self.hs = dt("hs", [128, 8, T_OWN], F32, "Internal")

        sb = lambda n, s, d: es.enter_context(nc.sbuf_tensor(n, s, d))
        self.ident = sb("identf", [128, 128], F32)
        self.identb = sb("identb", [128, 128], BF16)
        self.ones = sb("ones", [128, 128], BF16)
        self.ones1 = sb("ones1", [128, 128], BF16)
        self.onesf = sb("onesf", [128, 64], F32)
        self.epsb = sb("epsb", [128, 1], F32)
        self.gnt = sb("gnt", [128, 48], F32)
        self.masks = sb("masksb", [128, 8, 128], BF16)
        self.esink = sb("esink", [128, 2, 512], F32)
        self.QA = sb("QA", [128, 4, T_OWN], BF16)
        self.QB = sb("QB", [128, 6, T_OWN], BF16)
        self.KA = sb("KA", [128, 128 + T_OWN], BF16)
        self.KB = sb("KB", [128, 2, T_HALO + T_OWN], BF16)
        self.VAT = sb("VAT", [128, 128 + T_OWN], BF16)
        self.VBT = sb("VBT", [128, 2, T_HALO + T_OWN], BF16)
        self.ARENA = 57000
        self.arena = sb("arena", [128, self.ARENA], BF16)
        self.ps = [es.enter_context(nc.psum_tensor("ps%d" % i, [128, 1024], F32)) for i in range(4)]
        o = 0
        def carve(nelem_bf16, dtype=BF16):
            nonlocal o
            v = self.arena[:, o:o + nelem_bf16]
            o += nelem_bf16
            return v.bitcast(F32) if dtype == F32 else v
        self.xT = carve(8 * TBMAX * 2, F32).rearrange("p (c t) -> p c t", c=8)
        self.masksf = self.xT[:, 0:2, 0:512].rearrange("p c (m t) -> p c m t", m=4)
        self.uT = carve(8 * TBMAX).rearrange("p (c t) -> p c t", c=8)
        self.hT = carve(22 * TBMAX).rearrange("p (c t) -> p c t", c=22)
        self.tmp = [carve(TBMAX * 2, F32) for _ in range(3)]
        self.o_ropet = o
        self.ropet = carve(2 * TBMAX * 2, F32).rearrange("p (c t) -> p c t", c=2)
        self.stg = [carve(TBMAX * 2, F32) for _ in range(2)]
        self.NW = 3
        self.WCAP = 5632
        self.wslot = [carve(self.WCAP) for _ in range(self.NW)]
        self.o_p13 = o
        self.rstd = carve(TBMAX * 2, F32)
        assert o <= self.ARENA, o
        self.xstage = self.hT.rearrange("p c t -> p (c t)")[:, 0:5 * 2048].bitcast(F32).rearrange("p (n d) -> p n d", n=5)
        self.sq = self.hT

    def nps(self, hold=False, half=False):
        if half and self.half_ok:
            while True:
                b = self.psi % 8
                self.psi += 1
                if ("bk%d" % b) not in self.held:
                    break
            return self.ps[b // 2][:, (b % 2) * 512:(b % 2) * 512 + 512], ("bk%d" % b,)
        while True:
            if self.psi % 2:
                self.psi += 1
            b = self.psi % 8
            self.psi += 2
            if ("bk%d" % b) not in self.held and ("bk%d" % (b + 1)) not in self.held:
                break
        key = ("bk%d" % b, "bk%d" % (b + 1))
        if hold:
            self.held.update(key)
        return self.ps[b // 2], key

    def unhold(self, key):
        for k in key:
            self.held.discard(k)

    def ntmp(self):
        i = self.tmpi % 3
        self.tmpi += 1
        return self.tmp[i], "tmp%d" % i

    def key(self, s):
        self.uid += 1
        return "%s#%d" % (s, self.uid)

    def wget(self, *parts):
        desc = tuple(parts)
        if not self.recording:
            i = self.wi
            assert self.wjobs[i] == desc, (i, self.wjobs[i], desc)
            self.wi += 1
            while (self.wloaded < len(self.wjobs) and self.wloaded < i + self.NW
                   and self.wjobs[self.wloaded] != ("FENCE",)):
                self._wload(self.wloaded)
                self.wloaded += 1
            s = i % self.NW
        else:
            self.wjobs.append(desc)
            s = 0
        return self._wviews(s, desc), "w%d" % s

    def wfence(self):
        if self.recording:
            self.wjobs.append(("FENCE",))
            return
        assert self.wjobs[self.wi] == ("FENCE",) and self.wloaded == self.wi
        self.wi += 1
        self.wloaded = self.wi

    def _wviews(self, s, desc):
        views, o = [], 0
        for (name, kc0, kcn, c0, ncols) in desc:
            views.append(self.wslot[s][:, o:o + kcn * ncols].rearrange("p (k f) -> p k f", k=kcn))
            o += kcn * ncols
        assert o <= self.WCAP
        return views

    def _wload(self, j):
        desc = self.wjobs[j]
        s = j % self.NW
        n = sum(kcn * ncols for (_, _, kcn, _, ncols) in desc)
        reuse = sum(1 for d in self.wjobs if d == desc) > 1
        if desc in self.wscr:
            scr, skey = self.wscr[desc]
            dst = self.wslot[s][:, 0:n]
            self.S.dma("sp", lambda e, dst=dst, scr=scr: e.dma_start(out=dst, in_=scr[:, :]), reads=[skey], writes=["w%d" % s], sem="wh%d" % s)
            return
        for dst, (name, kc0, kcn, c0, ncols) in zip(self._wviews(s, desc), desc):
            src = self.w[name][kc0 * 128:(kc0 + kcn) * 128, c0:c0 + ncols].rearrange("(k p) f -> p k f", p=128)
            self.S.dma("pool", lambda e, dst=dst, src=src: e.dma_start(out=dst, in_=src), writes=["w%d" % s], sem="w%d" % s)
        if reuse:
            scr = self.nc.dram_tensor("wscr%d" % len(self.wscr), [128, n], BF16, kind="Internal").ap()
            skey = "wscr%d" % len(self.wscr)
            self.wscr[desc] = (scr, skey)
            srcv = self.wslot[s][:, 0:n]
            self.S.dma("sp", lambda e, scr=scr, srcv=srcv: e.dma_start(out=scr[:, :], in_=srcv), reads=["w%d" % s], writes=[skey], sem="ws%d" % s)

    def proj(self, wname, KC, act, actkeys, TB, c0, ncols, epi, after_first=None, after_group0=None):
        S = self.S
        gw = 512 if KC <= 8 else 256
        for g0 in range(0, ncols, gw):
            gn = min(gw, ncols - g0)
            (wv,), wkey = self.wget((wname, 0, KC, c0 + g0, gn))
            for j in range(gn // 128):
                ps, pk = self.nps(half=True)

                def mm(e, wv=wv, j=j, ps=ps):
                    ins = None
                    for (a, n) in nslices(TB):
                        for k in range(KC):
                            ins = e.matmul(ps[:, a:a + n], lhsT=wv[:, k, j * 128:(j + 1) * 128], rhs=act[:, k, a:a + n],
                                           start=(k == 0), stop=(k == KC - 1))
                    return ins
                S.op("pe", mm, reads=[wkey] + list(actkeys), writes=[pk])
                if after_first is not None:
                    after_first()
                    after_first = None
                epi((g0 // 128) + j, ps, pk)
            if after_group0 is not None:
                after_group0()
                after_group0 = None

    def load_x_dma(self, src, r0, TB):
        nt = TB // 128
        xs = self.xstage
        self.S.dma("sp", lambda e: e.dma_start(out=xs[:, 0:nt, :], in_=src[r0:r0 + TB, :].rearrange("(n p) d -> p n d", p=128)),
                   writes=["hT"], sem="xin")

    def load_x(self, src, r0, TB, tag, dma=True):
        S = self.S
        nt = TB // 128
        xs = self.xstage
        if dma:
            self.load_x_dma(src, r0, TB)
        for t in range(nt):
            ps, pk = self.nps()

            def tr(e, t=t, ps=ps):
                ins = None
                for c in range(8):
                    ins = e.transpose(out=ps[:, c * 128:(c + 1) * 128], in_=xs[:, t, c * 128:(c + 1) * 128], identity=self.ident[:])
                return ins
            S.op("pe", tr, reads=["hT", "ident"], writes=[pk])
            eng = "dve" if t % 2 == 0 else "act"
            if eng == "dve":
                S.op("dve", lambda e, t=t, ps=ps: e.tensor_copy(out=self.xT[:, :, t * 128:(t + 1) * 128],
                                                                 in_=ps[:].rearrange("p (c t) -> p c t", c=8)),
                     reads=[pk], writes=["xT"])
            else:
                S.op("act", lambda e, t=t, ps=ps: e.activation(out=self.xT[:, :, t * 128:(t + 1) * 128],
                                                                in_=ps[:].rearrange("p (c t) -> p c t", c=8), func=AF.Copy),
                     reads=[pk], writes=["xT"])

    def rmsnorm(self, TB, gi, out=None, outkey="uT", out32=None):
        S = self.S
        out = self.uT if out is None else out
        S.op("act", lambda e: e.activation(out=self.sq[:, 0:8, 0:TB], in_=self.xT[:, :, 0:TB], func=AF.Square),
             reads=["xT"], writes=["hT"])
        ps, pk = self.nps()

        def mm(e):
            ins = None
            for (a, n) in nslices(TB):
                for c in range(8):
                    ins = e.matmul(ps[:, a:a + n], lhsT=self.ones[:], rhs=self.sq[:, c, a:a + n], start=(c == 0), stop=(c == 7))
            return ins
        S.op("pe", mm, reads=["hT", "ones"], writes=[pk])
        rs, rk = self.ntmp()
        S.op("act", lambda e: e.activation(out=rs[:, 0:TB], in_=ps[:, 0:TB], func=AF.Sqrt, bias=self.epsb[:, 0:1], scale=1.0),
             reads=[pk, "consts"], writes=[rk])
        S.op("dve", lambda e: e.reciprocal(out=rs[:, 0:TB], in_=rs[:, 0:TB]), reads=[rk], writes=[rk])
        for c in range(8):
            S.op("dve", lambda e, c=c: e.scalar_tensor_tensor(out=out[:, c, 0:TB], in0=self.xT[:, c, 0:TB],
                                                             scalar=self.gnt[:, gi * 8 + c:gi * 8 + c + 1], in1=rs[:, 0:TB],
                                                             op0=ALU.mult, op1=ALU.mult),
                 reads=["xT", rk, "consts"], writes=[outkey])

    def norm_begin(self, TB, gi):
        S = self.S
        for c in range(8):
            sc_ap = self.gnt[:, gi * 8 + c:gi * 8 + c + 1]
            if c < 2:
                S.op("act", lambda e, c=c, sc_ap=sc_ap: e.activation(out=self.uT[:, c, 0:TB], in_=self.xT[:, c, 0:TB], func=AF.Copy, scale=sc_ap),
                     reads=["xT", "consts"], writes=["uT"])
            else:
                S.op("dve", lambda e, c=c, sc_ap=sc_ap: e.tensor_scalar(out=self.uT[:, c, 0:TB], in0=self.xT[:, c, 0:TB], scalar1=sc_ap, scalar2=None,
                                                                       op0=ALU.mult),
                     reads=["xT", "consts"], writes=["uT"])
        S.op("act", lambda e: e.activation(out=self.sq[:, 0:8, 0:TB], in_=self.xT[:, :, 0:TB], func=AF.Square),
             reads=["xT"], writes=["hT"])
        state = {"done": False}

        def finish():
            if state["done"]:
                return
            state["done"] = True
            ps, pk = self.nps(half=True)

            def mm(e):
                ins = None
                for (a, n) in nslices(TB):
                    for c in range(8):
                        ins = e.matmul(ps[:, a:a + n], lhsT=self.ones[:], rhs=self.sq[:, c, a:a + n], start=(c == 0), stop=(c == 7))
                return ins
            S.op("pe", mm, reads=["hT", "ones"], writes=[pk])
            S.op("act", lambda e: e.activation(out=self.rstd[:, 0:TB], in_=ps[:, 0:TB], func=AF.Sqrt, bias=self.epsb[:, 0:1], scale=1.0),
                 reads=[pk, "consts"], writes=["rstd"])
            S.op("dve", lambda e: e.reciprocal(out=self.rstd[:, 0:TB], in_=self.rstd[:, 0:TB]), reads=["rstd"], writes=["rstd"])
        return finish

    def ffn(self, TB, wg, wu, wd, nfin=None):
        S = self.S
        rs = self.rstd
        for g0 in range(0, DFF, 256):
            gn = 256
            (wgv, wuv), gk = self.wget((wg, 0, 8, g0, gn), (wu, 0, 8, g0, gn))
            uk = gk
            for j in range(gn // 128):
                f = g0 // 128 + j
                psg, pgk = self.nps(half=True)
                psu, puk = self.nps(half=True)

                def mm(e, wv, ps, j=j):
                    ins = None
                    for (a, n) in nslices(TB):
                        for k in range(8):
                            ins = e.matmul(ps[:, a:a + n], lhsT=wv[:, k, j * 128:(j + 1) * 128], rhs=self.uT[:, k, a:a + n],
                                           start=(k == 0), stop=(k == 7))
                    return ins
                S.op("pe", lambda e, wv=wgv, ps=psg, j=j: mm(e, wv, ps, j), reads=[gk, "uT"], writes=[pgk])
                S.op("pe", lambda e, wv=wuv, ps=psu, j=j: mm(e, wv, ps, j), reads=[uk, "uT"], writes=[puk])
                sg, sk = self.ntmp()
                if nfin is None:
                    S.op("act", lambda e, ps=psg, sg=sg: e.activation(out=sg[:, 0:TB], in_=ps[:, 0:TB], func=AF.Silu),
                         reads=[pgk], writes=[sk])
                else:
                    nfin()
                    S.op("dve", lambda e, ps=psg, sg=sg: e.tensor_tensor(out=sg[:, 0:TB], in0=ps[:, 0:TB], in1=rs[:, 0:TB], op=ALU.mult),
                         reads=[pgk, "rstd"], writes=[sk])
                    S.op("act", lambda e, sg=sg: e.activation(out=sg[:, 0:TB], in_=sg[:, 0:TB], func=AF.Silu), reads=[sk], writes=[sk])
                    S.op("dve", lambda e, sg=sg: e.tensor_tensor(out=sg[:, 0:TB], in0=sg[:, 0:TB], in1=rs[:, 0:TB], op=ALU.mult),
                         reads=[sk, "rstd"], writes=[sk])
                S.op("dve", lambda e, ps=psu, sg=sg, f=f: e.tensor_tensor(out=self.hT[:, f, 0:TB], in0=ps[:, 0:TB], in1=sg[:, 0:TB],
                                                                         op=ALU.mult),
                     reads=[puk, sk], writes=["hT"])

        def epi(j, ps, pk):
            S.op("dve", lambda e: e.scalar_tensor_tensor(out=self.xT[:, j, 0:TB], in0=ps[:, 0:TB], scalar=0.5,
                                                         in1=self.xT[:, j, 0:TB], op0=ALU.mult, op1=ALU.add),
                 reads=[pk, "xT"], writes=["xT"])
        self.proj(wd, 22, self.hT, ["hT"], TB, 0, D, epi)

    def consts(self):
        S = self.S
        S.dma("sp", lambda e: e.dma_start(out=self.ident[:], in_=self.ident_d[:, :]), writes=["ident"], sem="c0")
        S.dma("sp", lambda e: e.dma_start(out=self.gnt[:], in_=self.gn[:, :]), writes=["consts"], sem="c1")
        S.dma("sp", lambda e: e.dma_start(out=self.masksf, in_=self.masks_d[:, :, :].rearrange("p (c m) t -> p c m t", c=2)), writes=["xT"], sem="c2")
        S.dma("sp", lambda e: e.dma_start(out=self.esink[64:65, :, :], in_=self.sinkb[:, :].rearrange("(o g) n -> o g n", o=1)),
              writes=["esink"], sem="c3")
        S.op("dve", lambda e: e.memset(self.ones[:], 1.0 / 1024), writes=["ones"])
        S.op("dve", lambda e: e.memset(self.ones1[:], 1.0), writes=["ones1"])
        S.op("dve", lambda e: e.memset(self.onesf[:], 1.0), writes=["onesf"])
        S.op("dve", lambda e: e.memset(self.onesf[:], 1.0), writes=["onesf"])
        S.op("dve", lambda e: e.memset(self.epsb[:], EPS), writes=["consts"])
        S.op("dve", lambda e: e.tensor_copy(out=self.identb[:], in_=self.ident[:]), reads=["ident"], writes=["identb"])
        S.op("dve", lambda e: e.tensor_copy(out=self.masks[:].rearrange("p (c m) t -> p c m t", c=2), in_=self.masksf), reads=["xT"], writes=["masks"])
        S.op("act", lambda e: e.activation(out=self.esink[64:65, :, :], in_=self.esink[64:65, :, :], func=AF.Exp),
             reads=["esink"], writes=["esink"])

    def qkv(self, TB, own, t0, full_kv, nfin):
        S = self.S
        rs = self.rstd

        def hook():
            nfin()
            for i in range(2):
                S.op("dve", lambda e, i=i: e.tensor_tensor(out=self.ropet[:, i, 0:TB], in0=self.ropet[:, i, 0:TB], in1=rs[:, 0:TB], op=ALU.mult),
                     reads=["ropet", "rstd"], writes=["ropet"])
            if self.next_x is not None:
                self.load_x_dma(*self.next_x)
                self.next_x = None
        ropesrc = self.rope_o if own else self.rope_h
        S.dma("sp", lambda e: e.dma_start(out=self.ropet[:, :, 0:TB], in_=ropesrc[:, :, t0:t0 + TB].rearrange("c p t -> p c t")),
              writes=["ropet"], sem="rope")
        kcol = (T_HALO + t0) if own else t0
        acol = 128 + t0

        def finish(val, vk, dest, emit_out, ocol):
            if dest is not None:
                S.op("act", lambda e: e.activation(out=dest, in_=val[:, 0:TB], func=AF.Copy), reads=[vk], writes=["qkv"])
            if emit_out:
                nt = TB // 128
                ps, pk = self.nps()

                def tr(e):
                    ins = None
                    for t in range(nt):
                        ins = e.transpose(out=ps[:, t * 128:(t + 1) * 128], in_=val[:, t * 128:(t + 1) * 128], identity=self.ident[:])
                    return ins
                S.op("pe", tr, reads=[vk, "ident"], writes=[pk])
                i = self.uid % 2
                self.uid += 1
                st, sk = self.stg[i], "stg%d" % i
                S.op("dve", lambda e: e.tensor_copy(out=st[:, 0:TB], in_=ps[:, 0:TB]), reads=[pk], writes=[sk])
                S.dma("pool", lambda e: e.dma_start(out=self.o_kv[t0:t0 + TB, ocol:ocol + 128].rearrange("(n p) c -> p n c", p=128),
                                                  in_=st[:, 0:TB].rearrange("p (n c) -> p n c", c=128)),
                      reads=[sk], sem="okv%d" % i)

        pend = {}

        def epi_factory(dests):
            def epi(j, ps, pk):
                kind, dest, emit_out, ocol = dests[j]
                if kind == "z":
                    t1, k1 = self.ntmp()
                    S.op("dve", lambda e: e.tensor_tensor(out=t1[:, 0:TB], in0=ps[:, 0:TB], in1=self.ropet[:, 0, 0:TB], op=ALU.mult),
                         reads=[pk, "ropet"], writes=[k1])
                    pend["z"] = (t1, k1, dest, emit_out, ocol)
                elif kind == "s":
                    t1, k1, dest, emit_out, ocol = pend.pop("z")
                    t2, k2 = self.ntmp()
                    S.op("dve", lambda e: e.tensor_tensor(out=t2[:, 0:TB], in0=ps[:, 0:TB], in1=self.ropet[:, 1, 0:TB], op=ALU.mult),
                         reads=[pk, "ropet"], writes=[k2])
                    S.op("dve", lambda e: e.tensor_tensor(out=t1[:, 0:TB], in0=t1[:, 0:TB], in1=t2[:, 0:TB], op=ALU.add),
                         reads=[k1, k2], writes=[k1])
                    finish(t1, k1, dest, emit_out, ocol)
                else:
                    t1, k1 = self.ntmp()
                    S.op("dve", lambda e: e.tensor_tensor(out=t1[:, 0:TB], in0=ps[:, 0:TB], in1=rs[:, 0:TB], op=ALU.mult),
                         reads=[pk, "rstd"], writes=[k1])
                    finish(t1, k1, dest, emit_out, ocol)
            return epi

        if own:
            d = []
            for c in range(4):
                d += [("z", self.QA[:, c, t0:t0 + TB], False, 0), ("s", None, False, 0)]
            for c in range(6):
                d += [("z", self.QB[:, c, t0:t0 + TB], False, 0), ("s", None, False, 0)]
            self.proj("q", 8, self.uT, ["uT"], TB, 0, 2560, epi_factory(d), after_first=hook)
        d = [("z", self.KB[:, 0, kcol:kcol + TB], own, 128), ("s", None, False, 0),
             ("z", self.KB[:, 1, kcol:kcol + TB], own, 256), ("s", None, False, 0),
             ("v", self.VBT[:, 0, kcol:kcol + TB], own, 512), ("v", self.VBT[:, 1, kcol:kcol + TB], own, 640)]
        ncols = 768
        if full_kv:
            if own:
                d += [("v", self.VAT[:, acol:acol + TB], True, 384), ("z", self.KA[:, acol:acol + TB], True, 0), ("s", None, False, 0)]
            else:
                d += [("v", None, False, 0), ("z", None, False, 0), ("s", None, False, 0)]
            ncols = 1152
        if full_kv and not own:
            base = epi_factory(d)

            def epi2(j, ps, pk):
                if j < 6:
                    return base(j, ps, pk)
                lo = TB - 128
                if j == 6:
                    S.op("dve", lambda e: e.tensor_tensor(out=self.VAT[:, 0:128], in0=ps[:, lo:TB], in1=rs[:, lo:TB], op=ALU.mult),
                         reads=[pk, "rstd"], writes=["qkv"])
                elif j == 7:
                    t1, k1 = self.ntmp()
                    S.op("dve", lambda e: e.tensor_tensor(out=t1[:, 0:TB], in0=ps[:, 0:TB], in1=self.ropet[:, 0, 0:TB], op=ALU.mult),
                         reads=[pk, "ropet"], writes=[k1])
                    pend["z"] = (t1, k1)
                else:
                    t1, k1 = pend.pop("z")
                    t2, k2 = self.ntmp()
                    S.op("dve", lambda e: e.tensor_tensor(out=t2[:, 0:TB], in0=ps[:, 0:TB], in1=self.ropet[:, 1, 0:TB], op=ALU.mult),
                         reads=[pk, "ropet"], writes=[k2])
                    S.op("dve", lambda e: e.tensor_tensor(out=self.KA[:, 0:128], in0=t1[:, lo:TB], in1=t2[:, lo:TB], op=ALU.add),
                         reads=[k1, k2], writes=["qkv"])
            self.proj("kv", 8, self.uT, ["uT"], TB, 0, ncols, epi2, after_first=hook, after_group0=self.mid_hook)
        else:
            self.proj("kv", 8, self.uT, ["uT"], TB, 0, ncols, epi_factory(d), after_first=(None if own else hook), after_group0=self.mid_hook)
        self.mid_hook = None

    def phase1(self):
        S = self.S
        blocks = [(self.xh, t0, TB) for (t0, TB) in HALO_BLOCKS] + [(self.xo, t0, TB) for (t0, TB) in OWN_BLOCKS]
        self.load_x_dma(*blocks[0])
        self.load_x(*blocks[0], "b0", dma=False)

        def mk_mid(nb):
            return (lambda: self.load_x(*nb, "nx", dma=False)) if nb is not None else None
        for bi, (t0, TB) in enumerate(HALO_BLOCKS):
            self.half_ok = TB <= 512
            self.next_x = blocks[bi + 1]
            self.mid_hook = mk_mid(blocks[bi + 1])
            self.ffn(TB, "g1", "u1", "d1", self.norm_begin(TB, 0))
            self.qkv(TB, False, t0, bi == len(HALO_BLOCKS) - 1, self.norm_begin(TB, 1))
        for bi, (t0, TB) in enumerate(OWN_BLOCKS):
            self.half_ok = TB <= 512
            self.next_x = blocks[4 + bi + 1] if bi + 1 < len(OWN_BLOCKS) else None
            self.mid_hook = mk_mid(self.next_x)
            self.ffn(TB, "g1", "u1", "d1", self.norm_begin(TB, 0))
            S.dma("sp", lambda e, t0=t0, TB=TB: e.dma_start(out=self.hs[:, :, t0:t0 + TB], in_=self.xT[:, :, 0:TB]),
                  reads=["xT"], writes=["hs"], sem="hs")
            self.qkv(TB, True, t0, True, self.norm_begin(TB, 1))

    def mem_kv(self):
        S = self.S
        TB = 256
        self.load_x(self.xm, 0, TB, "m")
        self.rmsnorm(TB, 3)
        for wi, wn in enumerate(("mk", "mv")):
            def epi(j, ps, pk, wi=wi):
                t1, k1 = self.ntmp()
                S.op("act", lambda e: e.activation(out=t1[:, 0:TB], in_=ps[:, 0:TB], func=AF.Copy), reads=[pk], writes=[k1])
                ps2, pk2 = self.nps()

                def tr(e):
                    ins = None
                    for t in range(2):
                        ins = e.transpose(out=ps2[:, t * 128:(t + 1) * 128], in_=t1[:, t * 128:(t + 1) * 128], identity=self.ident[:])
                    return ins
                S.op("pe", tr, reads=[k1, "ident"], writes=[pk2])
                i = self.uid % 2
                self.uid += 1
                st, sk = self.stg[i], "stg%d" % i
                S.op("dve", lambda e: e.tensor_copy(out=st[:, 0:TB], in_=ps2[:, 0:TB]), reads=[pk2], writes=[sk])
                oc = wi * 512 + j * 128
                S.dma("pool", lambda e: e.dma_start(out=self.o_mem[:, oc:oc + 128].rearrange("(n p) c -> p n c", p=128),
                                                  in_=st[:, 0:TB].rearrange("p (n c) -> p n c", c=128)),
                      reads=[sk], sem="okv%d" % i)
            self.proj(wn, 8, self.uT, ["uT"], TB, 0, 512, epi)


    def psb(self, ps, i):
        return ps[:, i * 64:(i + 1) * 64].bitcast(BF16)

    def carve2(self):
        o = 0
        A = self.arena

        def carve(n, dtype=BF16):
            nonlocal o
            v = A[:, o:o + n]
            o += (n + 15) // 16 * 16
            return v.bitcast(F32) if dtype == F32 else v
        self.PT = [carve(1024) for _ in range(2)]
        AS = 68
        self.VAaug = [carve(2 * AS).rearrange("p (h d) -> p h d", h=2)[:, :, 0:65] for _ in range(4)]
        self.NVB = 10
        self.VBaug = [carve(4 * AS).rearrange("p (h d) -> p h d", h=4)[:, :, 0:65] for _ in range(self.NVB)]
        self.ot = [carve(1024, F32) for _ in range(2)]
        self.rr = carve(4096, F32)
        self.maskS = carve(408)
        self.o2_fixed = o
        self.accB = carve(4 * 2048 * 2, F32).rearrange("p (k t) -> p k t", k=4)
        self.craw_a = carve(2048).rearrange("p (s f) -> p s f", s=16)
        self.CKA = carve(2048).rearrange("p (s f) -> p s f", s=16)
        self.CVAaug = carve(16 * 2 * AS).rearrange("p (s h d) -> p s h d", s=16, h=2)[:, :, :, 0:65]
        self.VnAaug = carve(16 * 2 * AS).rearrange("p (s h d) -> p s h d", s=16, h=2)[:, :, :, 0:65]
        assert o <= 38400, o
        o = self.o2_fixed
        self.rawK = [carve(4096).rearrange("p (n f) -> p n f", n=16) for _ in range(2)]
        self.rawV = carve(4096).rearrange("p (n f) -> p n f", n=16)
        self.CKB = carve(4096).rearrange("p (c t) -> p c t", c=2)
        self.CVBaug = carve(16 * 4 * AS).rearrange("p (n h d) -> p n h d", n=16, h=4)[:, :, :, 0:65]
        self.VnBaug = carve(16 * 4 * AS).rearrange("p (s h d) -> p s h d", s=16, h=4)[:, :, :, 0:65]
        self.PTs = [carve(408) for _ in range(2)]
        self.Psum_s = [carve(136).rearrange("p (n t) -> p n t", t=8) for _ in range(2)]
        assert o <= 38400, o
        ow = 38400
        self.OA = A[:, ow:ow + 4 * T_OWN].rearrange("p (c t) -> p c t", c=4)
        self.OB = A[:, ow + 4 * T_OWN:ow + 6 * T_OWN].rearrange("p (c t) -> p c t", c=2)

    def attn_A(self):
        S = self.S
        vslot = {}
        vcount = [0]

        def vtileA(kt):
            if kt in vslot:
                return vslot[kt]
            i = vcount[0] % 4
            vcount[0] += 1
            for k in [k for k, v in vslot.items() if v[2] == i]:
                del vslot[k]
            va, key = self.VAaug[i], "VAaug%d" % i
            ps, pk = self.nps()
            col = 128 + kt * 128
            S.op("pe", lambda e: e.transpose(out=self.psb(ps, 0), in_=self.VAT[:, col:col + 128], identity=self.identb[:]),
                 reads=["identb"], writes=[pk])
            S.op("dve", lambda e: e.tensor_copy(out=va[:, :, 0:64], in_=self.psb(ps, 0).rearrange("p (h d) -> p h d", h=2)),
                 reads=[pk], writes=[key])
            vslot[kt] = (va, key, i)
            return vslot[kt]
        for i in range(4):
            S.op("dve", lambda e, i=i: e.memset(self.VAaug[i][:, :, 64:65], 1.0), writes=["VAaug%d" % i])
        pending = None
        pend3 = [None]
        rri = [0]

        def stage2(pd):
            PT, ptk, vc, vp, g, hp, q0 = pd
            po, pok = self.nps()

            def pv(e):
                e.matmul(po[0:65, 0:512], lhsT=vc[0][:, g, :], rhs=PT[:, 0:512], start=True, stop=False)
                return e.matmul(po[0:65, 0:512], lhsT=vp[0][:, g, :], rhs=PT[:, 512:1024], start=False, stop=True)
            S.op("pe", pv, reads=[ptk, vc[1], vp[1]], writes=[pok])
            oi = rri[0] % 2
            rri[0] += 1
            ot, otk = self.ot[oi], "ot%d" % oi
            rr, rrk = self.rr[:, oi * 512:(oi + 1) * 512], "rrA%d" % oi
            S.op("act", lambda e: e.activation(out=ot[0:65, 0:512], in_=po[0:65, 0:512], func=AF.Copy), reads=[pok], writes=[otk])
            S.op("dve", lambda e: e.tensor_tensor(out=rr[64:65, :], in0=ot[64:65, 0:512], in1=self.esink[64:65, g, 0:512], op=ALU.add),
                 reads=[otk, "esink"], writes=[rrk])
            S.op("act", lambda e: e.activation(out=rr[64:65, :], in_=rr[64:65, :], func=AF.Ln), reads=[rrk], writes=[rrk])
            S.op("act", lambda e: e.activation(out=rr[64:65, :], in_=rr[64:65, :], func=AF.Exp, scale=-1.0), reads=[rrk], writes=[rrk])
            if pend3[0] is not None:
                pend3[0]()

            def st3():
                ps, pk = self.nps()
                S.op("pe", lambda e: e.matmul(ps[0:64, 0:512], lhsT=self.onesf[64:65, 0:64], rhs=rr[64:65, :], start=True, stop=True),
                     reads=[rrk, "onesf"], writes=[pk], drain=True)
                S.op("dve", lambda e: e.tensor_tensor(out=self.OA[hp, 0:4, q0:q0 + 128], in0=ot[0:64, 0:512].rearrange("p (h q) -> p h q", h=4),
                                                     in1=ps[0:64, 0:512].rearrange("p (h q) -> p h q", h=4), op=ALU.mult),
                     reads=[pk, otk], writes=["OAB"])
            pend3[0] = st3

        for qt in range(16):
            vc = vtileA(qt)
            vp = vtileA(qt - 1)
            for g in range(2):
                hp = slice(g * 64, (g + 1) * 64)
                ps, pk = self.nps()
                q0 = qt * 128

                def sc(e, ps=ps, hp=hp, q0=q0, qt=qt):
                    r4 = lambda ap: ap.rearrange("p (h q) -> p h q", h=4)
                    e.matmul(r4(ps[:, 0:512]), lhsT=self.KA[hp, 128 + q0:128 + q0 + 128], rhs=self.QA[hp, 0:4, q0:q0 + 128], start=True, stop=True)
                    return e.matmul(r4(ps[:, 512:1024]), lhsT=self.KA[hp, q0:q0 + 128], rhs=self.QA[hp, 0:4, q0:q0 + 128], start=True, stop=True)
                S.op("pe", sc, writes=[pk], drain=True)
                pi = self.PTi % 2
                self.PTi += 1
                PT, ptk = self.PT[pi], "PT%d" % pi
                S.op("act", lambda e, ps=ps, PT=PT: e.activation(out=PT[:, :], in_=ps[:, :], func=AF.Exp, scale=ATTN_SCALE), reads=[pk], writes=[ptk])
                m0 = 2 if qt == 0 else 0
                S.op("dve", lambda e, PT=PT, m0=m0: e.tensor_tensor(
                    out=PT[:, :].rearrange("p (m h q) -> p m h q", m=2, h=4), in0=PT[:, :].rearrange("p (m h q) -> p m h q", m=2, h=4),
                    in1=self.masks[:, m0:m0 + 2, :].unsqueeze(2).to_broadcast([128, 2, 4, 128]), op=ALU.mult),
                    reads=[ptk, "masks"], writes=[ptk])
                if pending is not None:
                    stage2(pending)
                pending = (PT, ptk, vc, vp, g, hp, q0)
        stage2(pending)
        pend3[0]()

    def _normA(self, ot, otk, g, hp, q0):
        S = self.S
        rr = self.rr
        n = 512
        S.op("dve", lambda e: e.tensor_tensor(out=rr[64:65, 0:n], in0=ot[64:65, 0:n], in1=self.esink[64:65, g, 0:n], op=ALU.add),
             reads=[otk, "esink"], writes=["rr", "rrA0", "rrA1"])
        S.op("dve", lambda e: e.reciprocal(out=rr[64:65, 0:n], in_=rr[64:65, 0:n]), reads=["rr"], writes=["rr", "rrA0", "rrA1"])
        ps, pk = self.nps()
        S.op("pe", lambda e: e.matmul(ps[0:64, 0:n], lhsT=self.onesf[64:65, 0:64], rhs=rr[64:65, 0:n], start=True, stop=True),
             reads=["rr", "onesf"], writes=[pk], drain=True)
        S.op("dve", lambda e: e.tensor_tensor(out=self.OA[hp, 0:4, q0:q0 + 128], in0=ot[0:64, 0:n].rearrange("p (h q) -> p h q", h=4),
                                             in1=ps[0:64, 0:n].rearrange("p (h q) -> p h q", h=4), op=ALU.mult),
             reads=[pk, otk], writes=["OAB"])

    def _normA_s(self, ot, otk, g, hp, c0):
        S = self.S
        rr = self.rr
        n = 512
        v4 = lambda ap: ap.rearrange("p (s h t) -> p s h t", s=16, h=4)
        S.op("dve", lambda e: e.tensor_tensor(out=v4(rr[64:65, 0:n]), in0=v4(ot[64:65, 0:n]),
                                             in1=self.esink[64:65, g, :].rearrange("p (h q) -> p h q", h=4)[:, :, 0:8].unsqueeze(1).to_broadcast([1, 16, 4, 8]),
                                             op=ALU.add),
             reads=[otk, "esink"], writes=["rr", "rrA0", "rrA1"])
        S.op("act", lambda e: e.activation(out=rr[64:65, 0:n], in_=rr[64:65, 0:n], func=AF.Ln), reads=["rr"], writes=["rr", "rrA0", "rrA1"])
        S.op("act", lambda e: e.activation(out=rr[64:65, 0:n], in_=rr[64:65, 0:n], func=AF.Exp, scale=-1.0), reads=["rr"], writes=["rr", "rrA0", "rrA1"])
        ps, pk = self.nps()
        S.op("pe", lambda e: e.matmul(ps[0:64, 0:n], lhsT=self.onesf[64:65, 0:64], rhs=rr[64:65, 0:n], start=True, stop=True),
             reads=["rr", "onesf"], writes=[pk], drain=True)
        p4 = lambda ap: ap.rearrange("p (s h t) -> p h s t", s=16, h=4)
        S.op("dve", lambda e: e.tensor_tensor(out=self.OA[hp, 0:4, c0:c0 + 128].rearrange("p h (s t) -> p h s t", s=16),
                                             in0=p4(ot[0:64, 0:n]), in1=p4(ps[0:64, 0:n]), op=ALU.mult),
             reads=[pk, otk], writes=["OAB"])

    def attn_A_sample(self):
        S = self.S
        c0 = 2048
        S.dma("pool", lambda e: e.dma_start(out=self.craw_a, in_=self.cswk.rearrange("s p f -> p s f")), writes=["craw_a"], sem="ca0")
        S.op("dve", lambda e: e.memset(self.CVAaug[:, :, :, 64:65], 1.0), writes=["CVAaug"])
        S.op("dve", lambda e: e.memset(self.VnAaug[:, :, :, 64:65], 1.0), writes=["VnAaug"])
        for h in range(2):
            S.dma("pool", lambda e, h=h: e.dma_start(out=self.CVAaug[:, :, h, 0:64], in_=self.cswv[:, :, h * 64:(h + 1) * 64].rearrange("s p d -> p s d")),
                  writes=["CVAaug"], sem="ca1")
        if P2S < 2:
            return
        ps, pk = self.nps()

        def tr(e):
            ins = None
            for s_ in range(16):
                ins = e.transpose(out=self.psb(ps, s_), in_=self.craw_a[:, s_, :], identity=self.identb[:])
            return ins
        for q4 in range(4):
            psq, pkq = self.nps()

            def trq(e, psq=psq, q4=q4):
                ins = None
                for i in range(4):
                    ins = e.transpose(out=self.psb(psq, i), in_=self.craw_a[:, q4 * 4 + i, :], identity=self.identb[:])
                return ins
            S.op("pe", trq, reads=["craw_a", "identb"], writes=[pkq])
            S.op("dve", lambda e, psq=psq, q4=q4: e.tensor_copy(out=self.CKA[:, q4 * 4:q4 * 4 + 4, :],
                                                              in_=psq[:, 0:256].bitcast(BF16).rearrange("p (s f) -> p s f", s=4)),
                 reads=[pkq], writes=["CKA"])
        if P2S < 3:
            return
        for h8 in range(2):
            ps2, pk2 = self.nps()

            def tr2(e, ps2=ps2, h8=h8):
                ins = None
                for i in range(8):
                    s_ = h8 * 8 + i
                    ins = e.transpose(out=self.psb(ps2, i)[0:8, :], in_=self.VAT[:, 128 + c0 + s_ * 8:128 + c0 + s_ * 8 + 8], identity=self.identb[:])
                return ins
            S.op("pe", tr2, reads=["identb"], writes=[pk2])
            S.op("dve", lambda e, ps2=ps2, h8=h8: e.tensor_copy(out=self.VnAaug[0:8, h8 * 8:h8 * 8 + 8, :, 0:64],
                                                               in_=ps2[0:8, 0:512].bitcast(BF16).rearrange("p (s h d) -> p s h d", s=8, h=2)),
                 reads=[pk2], writes=["VnAaug"])
        if P2S < 4:
            return
        for g in range(2):
            hp = slice(g * 64, (g + 1) * 64)
            ps, pk = self.nps()

            def sc(e, ps=ps, hp=hp):
                ins = None
                for s_ in range(16):
                    q = self.QA[hp, 0:4, c0 + s_ * 8:c0 + s_ * 8 + 8]
                    r4 = lambda ap: ap.rearrange("p (h q) -> p h q", h=4)
                    lh = self.CKA[hp, s_, :] if P2Y == 0 else self.KA[hp, 0:128]
                    if P2Y in (4, 5):
                        if s_ < 2:
                            ins = e.matmul(ps[:, s_ * 512:(s_ + 1) * 512].rearrange("p (h q) -> p h q", h=4), lhsT=self.KA[hp, 128:256],
                                           rhs=self.QA[hp, 0:4, 0:128], start=True, stop=True)
                        continue
                    if P2Y == 2:
                        q = self.QA[hp, 0, 0:32]
                        ins = e.matmul(ps[:, s_ * 32:(s_ + 1) * 32], lhsT=lh, rhs=q, start=True, stop=True)
                    else:
                        ins = e.matmul(r4(ps[:, s_ * 32:(s_ + 1) * 32]), lhsT=lh, rhs=q, start=True, stop=True)
                    if P2X >= 1:
                        ins = e.matmul(r4(ps[0:8, 512 + s_ * 32:512 + (s_ + 1) * 32]), lhsT=self.KA[hp, 128 + c0 + s_ * 8:128 + c0 + s_ * 8 + 8], rhs=q,
                                       start=True, stop=True)
                return ins
            S.op("pe", sc, reads=["CKA"], writes=[pk], drain=True)
            pi = self.uid % 2
            self.uid += 1
            PT, ptk = self.PT[pi], "PT%d" % pi
            if P2X < 2:
                continue
            S.op("act", lambda e, ps=ps, PT=PT: e.activation(out=PT[:, 0:512], in_=ps[:, 0:512], func=AF.Exp, scale=ATTN_SCALE), reads=[pk], writes=[ptk])
            S.op("act", lambda e, ps=ps, PT=PT: e.activation(out=PT[0:8, 512:1024], in_=ps[0:8, 512:1024], func=AF.Exp, scale=ATTN_SCALE),
                 reads=[pk], writes=[ptk])
            S.op("dve", lambda e, PT=PT: e.tensor_tensor(out=PT[:, 0:512].rearrange("p (a t) -> p a t", t=8),
                                                         in0=PT[:, 0:512].rearrange("p (a t) -> p a t", t=8),
                                                         in1=self.masks[:, 4, 0:8].unsqueeze(1).to_broadcast([128, 64, 8]), op=ALU.mult),
                 reads=[ptk, "masks"], writes=[ptk])
            S.op("dve", lambda e, PT=PT: e.tensor_tensor(out=PT[0:8, 512:1024].rearrange("p (a t) -> p a t", t=8),
                                                         in0=PT[0:8, 512:1024].rearrange("p (a t) -> p a t", t=8),
                                                         in1=self.masks[0:8, 5, 0:8].unsqueeze(1).to_broadcast([8, 64, 8]), op=ALU.mult),
                 reads=[ptk, "masks"], writes=[ptk])
            if P2S < 5:
                continue
            po, pok = self.nps()

            def pv(e, po=po, PT=PT, g=g):
                ins = None
                for s_ in range(16):
                    o = po[0:65, s_ * 32:(s_ + 1) * 32]
                    e.matmul(o, lhsT=self.CVAaug[:, s_, g, :], rhs=PT[:, s_ * 32:(s_ + 1) * 32], start=True, stop=False)
                    ins = e.matmul(o, lhsT=self.VnAaug[0:8, s_, g, :], rhs=PT[0:8, 512 + s_ * 32:512 + (s_ + 1) * 32], start=False, stop=True)
                return ins
            S.op("pe", pv, reads=[ptk, "CVAaug", "VnAaug"], writes=[pok])
            oi = self.uid % 2
            self.uid += 1
            ot, otk = self.ot[oi], "ot%d" % oi
            S.op("act", lambda e, po=po, ot=ot: e.activation(out=ot[0:65, 0:512], in_=po[0:65, 0:512], func=AF.Copy), reads=[pok], writes=[otk])
            self._normA_s(ot, otk, g, hp, c0)

    def attn_B(self):
        S = self.S
        vslot = {}
        vcount = [0]
        for i in range(self.NVB):
            S.op("dve", lambda e, i=i: e.memset(self.VBaug[i][:, :, 64:65], 1.0), writes=["VBaug%d" % i])

        def cols(start, step):
            c = T_HALO + start
            return slice(c, c + 127 * step + 1, step)

        def vtileB(start, step, protect):
            kk = (start, step)
            if kk in vslot:
                return vslot[kk]
            while True:
                i = vcount[0] % self.NVB
                vcount[0] += 1
                owner = [k for k, v in vslot.items() if v[2] == i]
                if owner and owner[0] in protect:
                    continue
                for k in owner:
                    del vslot[k]
                break
            va, key = self.VBaug[i], "VBaug%d" % i
            ps, pk = self.nps()
            cs = cols(start, step)

            def tr(e):
                e.transpose(out=self.psb(ps, 0), in_=self.VBT[:, 0, cs], identity=self.identb[:])
                return e.transpose(out=self.psb(ps, 1), in_=self.VBT[:, 1, cs], identity=self.identb[:])
            S.op("pe", tr, reads=["identb"], writes=[pk])
            S.op("dve", lambda e: e.tensor_copy(out=va[:, :, 0:64], in_=ps[:, 0:128].bitcast(BF16).rearrange("p (h d) -> p h d", h=4)),
                 reads=[pk], writes=[key])
            vslot[kk] = (va, key, i, kk)
            return vslot[kk]

        pending = [None]

        def stage2(pd, half):
            PT, ptk, vc, vp, ti, ctx = pd
            if ctx["po"] is None:
                ctx["po"] = (self.nps(hold=True), self.nps(hold=True))
            (po0, pok0), (po1, pok1) = ctx["po"]

            def pv(e, ks):
                ins = None
                for k in ks:
                    po = (po0, po1)[k // 2]
                    o = po[0:65, (k % 2) * 512 + ti * 128:(k % 2) * 512 + ti * 128 + 128]
                    e.matmul(o, lhsT=vc[0][:, k, :], rhs=PT[:, k * 256:k * 256 + 128], start=True, stop=False)
                    ins = e.matmul(o, lhsT=vp[0][:, k, :], rhs=PT[:, k * 256 + 128:k * 256 + 256], start=False, stop=True)
                return ins
            if half == 0:
                S.op("pe", lambda e: pv(e, (0, 1)), reads=[ptk, vc[1], vp[1]], writes=[pok0])
                return
            S.op("pe", lambda e: pv(e, (2, 3)), reads=[ptk, vc[1], vp[1]], writes=[pok1])
            if ti == 3:
                self.unhold(pok0)
                self.unhold(pok1)
                g, d, accf = ctx["g"], ctx["d"], ctx["accf"]
                for k in range(4):
                    po, pok = ((po0, pok0), (po1, pok1))[k // 2]
                    src = po[0:65, (k % 2) * 512:(k % 2) * 512 + 512]
                    dst = accf(k)
                    if d == 16:
                        src = src.rearrange("p (r j) -> p r j", r=4)
                    if g == 0:
                        S.op("act", lambda e, src=src, dst=dst: e.activation(out=dst, in_=src, func=AF.Copy), reads=[pok], writes=["accB"])
                    else:
                        S.op("dve", lambda e, src=src, dst=dst: e.tensor_tensor(out=dst, in0=src, in1=dst, op=ALU.add), reads=[pok, "accB"], writes=["accB"])

        for g, d in enumerate((1, 4, 16)):
            if d == 1:
                batches = [[(128 * (b0 + i), 1) for i in range(4)] for b0 in range(0, 16, 4)]
                accv = [lambda k, b=b: self.accB[0:65, k, b * 512:(b + 1) * 512] for b in range(4)]
            elif d == 4:
                batches = [[(r + 512 * b, 4) for b in range(4)] for r in range(4)]
                accv = [lambda k, r=r: self.accB[0:65, k, r:r + 2045:4] for r in range(4)]
            else:
                batches = [[(r0 + i, 16) for i in range(4)] for r0 in range(0, 16, 4)]
                accv = [lambda k, r0=r0: self.accB[0:65, k, :].rearrange("p (j r) -> p r j", r=16)[:, r0:r0 + 4, :] for r0 in range(0, 16, 4)]
            for bi, batch in enumerate(batches):
                need = [(st, sp) for (st, sp) in batch] + [(st - 128 * d, sp) for (st, sp) in batch]
                ctx = {"po": None, "g": g, "d": d, "accf": accv[bi]}
                for ti, (st, sp) in enumerate(batch):
                    prot = list(need)
                    if pending[0] is not None:
                        prot += [pending[0][2][3], pending[0][3][3]]
                    vc = vtileB(st, sp, prot)
                    vp = vtileB(st - 128 * d, sp, prot)
                    ps, pk = self.nps()
                    cq, cp = cols(st, sp), cols(st - 128 * d, sp)

                    def sc(e, ks, ps=ps, cq=cq, cp=cp, g=g):
                        ins = None
                        for k in ks:
                            hp = slice((k % 2) * 64, (k % 2) * 64 + 64)
                            q = self.QB[hp, g * 2 + k // 2, cq.start - T_HALO:cq.stop - T_HALO:cq.step]
                            e.matmul(ps[:, k * 256:k * 256 + 128], lhsT=self.KB[hp, k // 2, cq], rhs=q, start=True, stop=True)
                            ins = e.matmul(ps[:, k * 256 + 128:k * 256 + 256], lhsT=self.KB[hp, k // 2, cp], rhs=q, start=True, stop=True)
                        return ins
                    S.op("pe", lambda e, sc=sc: sc(e, (0, 2)), writes=[pk], drain=(pending[0] is None))
                    if pending[0] is not None:
                        stage2(pending[0], 0)
                    S.op("pe", lambda e, sc=sc: sc(e, (1, 3)), writes=[pk], drain=(pending[0] is None))
                    if pending[0] is not None:
                        stage2(pending[0], 1)
                    pi = self.PTi % 2
                    self.PTi += 1
                    PT, ptk = self.PT[pi], "PT%d" % pi
                    S.op("act", lambda e, ps=ps, PT=PT: e.activation(out=PT[:, :], in_=ps[:, :], func=AF.Exp, scale=ATTN_SCALE), reads=[pk], writes=[ptk])
                    m0 = 2 if st - 128 * d < 0 else 0
                    S.op("dve", lambda e, PT=PT, m0=m0: e.tensor_tensor(
                        out=PT[:, :].rearrange("p (k m q) -> p k m q", k=4, m=2), in0=PT[:, :].rearrange("p (k m q) -> p k m q", k=4, m=2),
                        in1=self.masks[:, m0:m0 + 2, :].unsqueeze(1).to_broadcast([128, 4, 2, 128]), op=ALU.mult),
                        reads=[ptk, "masks"], writes=[ptk])
                    pending[0] = (PT, ptk, vc, vp, ti, ctx)
        stage2(pending[0], 0)
        stage2(pending[0], 1)
        for k in range(4):
            S.op("act", lambda e, k=k: e.activation(out=self.accB[64:65, k, :], in_=self.accB[64:65, k, :], func=AF.Ln), reads=["accB"], writes=["accB"])
            S.op("act", lambda e, k=k: e.activation(out=self.accB[64:65, k, :], in_=self.accB[64:65, k, :], func=AF.Exp, scale=-1.0),
                 reads=["accB"], writes=["accB"])
        for k in range(4):
            hp = slice((k % 2) * 64, (k % 2) * 64 + 64)
            acc = self.accB[:, k, :]
            for (a, m) in nslices(2048):
                ps, pk = self.nps()
                S.op("pe", lambda e, ps=ps, a=a, m=m, acc=acc: e.matmul(ps[0:64, 0:m], lhsT=self.onesf[64:65, 0:64], rhs=acc[64:65, a:a + m],
                                                                      start=True, stop=True),
                     reads=["accB", "onesf"], writes=[pk], drain=True)
                S.op("dve", lambda e, ps=ps, a=a, m=m, acc=acc, hp=hp, k=k: e.tensor_tensor(out=self.OB[hp, k // 2, a:a + m], in0=acc[0:64, a:a + m],
                                                                                          in1=ps[0:64, 0:m], op=ALU.mult),
                     reads=[pk, "accB"], writes=["OAB"])

    def _normB(self, acc, acck, hp, chunk, c0, n):
        S = self.S
        rr = self.rr
        S.op("act", lambda e: e.activation(out=rr[64:65, 0:n], in_=acc[64:65, 0:n], func=AF.Ln), reads=[acck], writes=["rr", "rrA0", "rrA1"])
        S.op("act", lambda e: e.activation(out=rr[64:65, 0:n], in_=rr[64:65, 0:n], func=AF.Exp, scale=-1.0), reads=["rr"], writes=["rr", "rrA0", "rrA1"])
        for (a, m) in nslices(n):
            ps, pk = self.nps()
            S.op("pe", lambda e, ps=ps, a=a, m=m: e.matmul(ps[0:64, 0:m], lhsT=self.onesf[64:65, 0:64], rhs=rr[64:65, a:a + m], start=True, stop=True),
                 reads=["rr", "onesf"], writes=[pk], drain=True)
            S.op("dve", lambda e, ps=ps, a=a, m=m: e.tensor_tensor(out=self.OB[hp, chunk, c0 + a:c0 + a + m], in0=acc[0:64, a:a + m],
                                                                 in1=ps[0:64, 0:m], op=ALU.mult),
                 reads=[pk, acck], writes=["OAB"])

    def attn_B_sample(self):
        S = self.S
        c0 = 2048
        kc0 = T_HALO + c0
        S.dma("sp", lambda e: e.dma_start(out=self.ot[1][:, 0:408], in_=self.maskS_d[:, :]), writes=["ot1"], sem="ms")
        S.op("dve", lambda e: e.tensor_copy(out=self.maskS[:, 0:408], in_=self.ot[1][:, 0:408]), reads=["ot1"], writes=["maskS"])
        S.op("dve", lambda e: e.memset(self.CVBaug[:, :, :, 64:65], 1.0), writes=["CVBaug"])
        S.op("dve", lambda e: e.memset(self.VnBaug[:, :, :, 64:65], 1.0), writes=["VnBaug"])
        for c in range(2):
            for h8 in range(2):
                ps2, pk2 = self.nps()

                def tr2(e, ps2=ps2, c=c, h8=h8):
                    ins = None
                    for i in range(8):
                        s_ = h8 * 8 + i
                        ins = e.transpose(out=self.psb(ps2, i)[0:8, :], in_=self.VBT[:, c, kc0 + s_ * 8:kc0 + s_ * 8 + 8], identity=self.identb[:])
                    return ins
                S.op("pe", tr2, reads=["identb"], writes=[pk2])
                S.op("dve", lambda e, ps2=ps2, c=c, h8=h8: e.tensor_copy(out=self.VnBaug[0:8, h8 * 8:h8 * 8 + 8, 2 * c:2 * c + 2, 0:64],
                                                                        in_=ps2[0:8, 0:512].bitcast(BF16).rearrange("p (s h d) -> p s h d", s=8, h=2)),
                     reads=[pk2], writes=["VnBaug"])
        for i in range(2):
            S.op("dve", lambda e, i=i: e.memset(self.PTs[i][:, 384:408], 0.0), writes=["PTs%d" % i])
        pso, psok = self.nps(hold=True)
        pend = [None]
        for s_ in range(16):
            rk, rkk = self.rawK[s_ % 2], "rawK%d" % (s_ % 2)
            S.dma("pool", lambda e, rk=rk, s_=s_: e.dma_start(out=rk, in_=self.cdk[s_].rearrange("(n p) f -> p n f", p=128)), writes=[rkk], sem=rkk)
            S.dma("pool", lambda e, s_=s_: e.dma_start(out=self.rawV, in_=self.cdv[s_].rearrange("(n p) f -> p n f", p=128)), writes=["rawV"], sem="rawV")
            for c in range(2):
                for h8 in range(2):
                    ps, pk = self.nps()

                    def tr(e, ps=ps, c=c, rk=rk, h8=h8):
                        ins = None
                        for i in range(8):
                            ins = e.transpose(out=self.psb(ps, i), in_=rk[:, h8 * 8 + i, c * 128:(c + 1) * 128], identity=self.identb[:])
                        return ins
                    S.op("pe", tr, reads=[rkk, "identb"], writes=[pk])
                    eng = "act" if h8 == 0 else "dve"
                    if eng == "act":
                        S.op("act", lambda e, ps=ps, c=c, h8=h8: e.activation(out=self.CKB[:, c, h8 * 1024:(h8 + 1) * 1024], in_=ps[:, 0:512].bitcast(BF16),
                                                                             func=AF.Copy), reads=[pk], writes=["CKB"])
                    else:
                        S.op("dve", lambda e, ps=ps, c=c, h8=h8: e.tensor_copy(out=self.CKB[:, c, h8 * 1024:(h8 + 1) * 1024], in_=ps[:, 0:512].bitcast(BF16)),
                             reads=[pk], writes=["CKB"])
            if pend[0] is not None:
                pend[0]()
                pend[0] = None
            S.op("dve", lambda e: e.tensor_copy(out=self.CVBaug[:, :, :, 0:64], in_=self.rawV.rearrange("p n (h d) -> p n h d", h=4)),
                 reads=["rawV"], writes=["CVBaug"])
            for k in range(4):
                hp = slice((k % 2) * 64, (k % 2) * 64 + 64)
                ch = k // 2
                ps, pk = self.nps()

                def sc(e, ps=ps, hp=hp, ch=ch, s_=s_):
                    q = self.QB[hp, ch:6:2, c0 + s_ * 8:c0 + s_ * 8 + 8]
                    for n in range(16):
                        e.matmul(ps[:, n * 24:(n + 1) * 24].rearrange("p (g t) -> p g t", g=3), lhsT=self.CKB[hp, ch, n * 128:(n + 1) * 128], rhs=q,
                                 start=True, stop=True)
                    return e.matmul(ps[0:8, 384:408].rearrange("p (g t) -> p g t", g=3), lhsT=self.KB[hp, ch, kc0 + s_ * 8:kc0 + s_ * 8 + 8], rhs=q,
                                    start=True, stop=True)
                S.op("pe", sc, reads=["CKB"], writes=[pk], drain=True)
                pi = self.PTi % 2
                self.PTi += 1
                PT, ptk = self.PTs[pi], "PTs%d" % pi
                Pq, pqk = self.Psum_s[pi], "Pq%d" % pi
                S.op("act", lambda e, ps=ps, PT=PT: e.activation(out=PT[:, 0:384], in_=ps[:, 0:384], func=AF.Exp, scale=ATTN_SCALE), reads=[pk], writes=[ptk])
                S.op("act", lambda e, ps=ps, PT=PT: e.activation(out=PT[0:8, 384:408], in_=ps[0:8, 384:408], func=AF.Exp, scale=ATTN_SCALE),
                     reads=[pk], writes=[ptk])
                S.op("dve", lambda e, PT=PT: e.tensor_tensor(out=PT[:, 0:408], in0=PT[:, 0:408], in1=self.maskS[:, 0:408], op=ALU.mult),
                     reads=[ptk, "maskS"], writes=[ptk])
                def gsum(e, PT=PT, Pq=Pq):
                    with self.nc.allow_low_precision("3-term sum feeding a bf16 matmul operand"):
                        return e.tensor_reduce(out=Pq[:, 0:17, :], in_=PT[:, 0:408].rearrange("p (n g t) -> p n t g", g=3, t=8),
                                               axis=mybir.AxisListType.X, op=ALU.add)
                S.op("dve", gsum, reads=[ptk], writes=[pqk])

                def pv(e, Pq=Pq, k=k, s_=s_):
                    o = pso[0:65, k * 128 + s_ * 8:k * 128 + s_ * 8 + 8]
                    for n in range(16):
                        e.matmul(o, lhsT=self.CVBaug[:, n, k, :], rhs=Pq[:, n, :], start=(n == 0), stop=False)
                    return e.matmul(o, lhsT=self.VnBaug[0:8, s_, k, :], rhs=Pq[0:8, 16, :], start=False, stop=True)
                if pend[0] is not None:
                    pend[0]()
                pend[0] = (lambda pv=pv, pqk=pqk: S.op("pe", pv, reads=[pqk, "CVBaug", "VnBaug"], writes=[psok]))
        pend[0]()
        self.unhold(psok)
        ot, otk = self.ot[0], "ot0"
        S.op("act", lambda e: e.activation(out=ot[0:65, 0:512], in_=pso[0:65, 0:512], func=AF.Copy), reads=[psok], writes=[otk])
        for k in range(4):
            hp = slice((k % 2) * 64, (k % 2) * 64 + 64)
            self._normB(ot[:, k * 128:(k + 1) * 128], otk, hp, k // 2, c0, 128)

    def phase2(self):
        self.half_ok = False
        self.carve2()
        if P2 >= 1:
            self.attn_A()
        if P2 >= 2 and not SKIP_AS:
            self.attn_A_sample()
        if P2 >= 3:
            self.attn_B()
        self.S.barrier()
        if P2 >= 4:
            self.attn_B_sample()


    def carve3(self):
        qb = self.QB[:, :, :].rearrange("p c t -> p (c t)")
        qa = self.QA[:, :, :].rearrange("p c t -> p (c t)")
        self.wslot = [qb[:, 0:self.WCAP], qb[:, self.WCAP:2 * self.WCAP], qa[:, 0:self.WCAP]]
        kb = self.KB[:, :, :].rearrange("p c t -> p (c t)")
        vb = self.VBT[:, :, :].rearrange("p c t -> p (c t)")
        self.MK = kb[:, 0:1024].rearrange("p (h t) -> p h t", h=4)
        self.MV = kb[:, 1024:2048].rearrange("p (n f) -> p n f", n=2)
        self.cmraw = kb[:, 2048:6144].rearrange("p (s n f) -> p s n f", s=4, n=2)
        self.CMK = vb[:, 0:4096].rearrange("p (s h t) -> p s h t", s=4, h=4)
        self.CMV = vb[:, 4096:8192].rearrange("p (s n f) -> p s n f", s=4, n=2)
        self.m1 = self.hT[:, 0:8, :]
        self.m2 = self.hT[:, 8:16, :]
        self.qm = self.hT[:, 16:20, :]
        self.om = self.uT[:, 0:4, :]
        self.PT3 = self.hT[:, 20:22, :].rearrange("p c t -> p (c t)")
        self.yT = self.hT[:, :, :].rearrange("p c t -> p (c t)")[:, 0:8 * TBMAX * 2].bitcast(F32).rearrange("p (c t) -> p c t", c=8)
        yr = self.arena[:, self.o_ropet:self.o_ropet + 5120]
        self.ystg = [yr[:, 0:2048].bitcast(F32), yr[:, 2048:4096].bitcast(F32)]
        self.yjunk = yr[:, 4096:5120]
        self.grow = qa[:, 5632:7680].bitcast(F32)
        self.fstat = qa[:, 7680:7696].bitcast(F32)

    def mem_kv3(self):
        S = self.S
        TB = 256
        self.load_x(self.xm, 0, TB, "m")
        self.rmsnorm(TB, 3)
        for wi, wn in enumerate(("mk", "mv")):
            def epi(j, ps, pk, wi=wi):
                t1, k1 = self.ntmp()
                S.op("act", lambda e: e.activation(out=t1[:, 0:TB], in_=ps[:, 0:TB], func=AF.Copy), reads=[pk], writes=[k1])
                if wi == 0 and MKV:
                    S.op("dve", lambda e: e.tensor_copy(out=self.MK[:, j, :], in_=t1[:, 0:TB]), reads=[k1], writes=["MK"])
                ps2, pk2 = self.nps()

                def tr(e):
                    ins = None
                    for t in range(2):
                        ins = e.transpose(out=ps2[:, t * 128:(t + 1) * 128], in_=t1[:, t * 128:(t + 1) * 128], identity=self.ident[:])
                    return ins
                S.op("pe", tr, reads=[k1, "ident"], writes=[pk2])
                i = self.uid % 2
                self.uid += 1
                st, sk = self.stg[i], "stg%d" % i
                S.op("dve", lambda e: e.tensor_copy(out=st[:, 0:TB], in_=ps2[:, 0:TB]), reads=[pk2], writes=[sk])
                if wi == 1 and MKV:
                    S.op("act", lambda e: e.activation(out=self.MV[:, :, j * 128:(j + 1) * 128],
                                                       in_=st[:, 0:TB].rearrange("p (n c) -> p n c", c=128), func=AF.Copy),
                         reads=[sk], writes=["MV"])
                oc = wi * 512 + j * 128
                S.dma("pool", lambda e: e.dma_start(out=self.o_mem[:, oc:oc + 128].rearrange("(n p) c -> p n c", p=128),
                                                  in_=st[:, 0:TB].rearrange("p (n c) -> p n c", c=128)),
                      reads=[sk], sem="okv%d" % i)
            self.proj(wn, 8, self.uT, ["uT"], TB, 0, 512, epi)

    def cross_prompt(self, TBp):
        S = self.S
        for h in range(4):
            ps, pk = self.nps()

            def sc(e, ps=ps, h=h):
                ins = None
                for n in range(2):
                    ins = e.matmul(ps[:, n * 512:n * 512 + TBp], lhsT=self.MK[:, h, n * 128:(n + 1) * 128], rhs=self.qm[:, h, 0:TBp], start=True, stop=True)
                return ins
            S.op("pe", sc, reads=["MK", "hT"], writes=[pk])
            PT = self.PT3
            S.op("act", lambda e, ps=ps: e.activation(out=PT[:, 0:1024].rearrange("p (n t) -> p n t", n=2)[:, :, 0:TBp],
                                                      in_=ps[:, :].rearrange("p (n t) -> p n t", n=2)[:, :, 0:TBp], func=AF.Exp, scale=MEM_SCALE),
                 reads=[pk], writes=["PT3"])
            po, pok = self.nps()

            def pv(e, po=po, h=h):
                ins = None
                for n in range(2):
                    e.matmul(po[:, 0:TBp], lhsT=self.MV[:, n, h * 128:(h + 1) * 128], rhs=PT[:, n * 512:n * 512 + TBp], start=(n == 0), stop=(n == 1))
                for n in range(2):
                    ins = e.matmul(po[:, 512:512 + TBp], lhsT=self.ones1[:], rhs=PT[:, n * 512:n * 512 + TBp], start=(n == 0), stop=(n == 1))
                return ins
            S.op("pe", pv, reads=["MV", "PT3", "ones1"], writes=[pok])
            t1, k1 = self.ntmp()
            S.op("dve", lambda e, po=po, t1=t1: e.reciprocal(out=t1[:, 0:TBp], in_=po[:, 512:512 + TBp]), reads=[pok], writes=[k1])
            S.op("dve", lambda e, po=po, t1=t1, h=h: e.tensor_tensor(out=self.om[:, h, 0:TBp], in0=po[:, 0:TBp], in1=t1[:, 0:TBp], op=ALU.mult),
                 reads=[pok, k1], writes=["uT"])

    def cross_sample(self, b0):
        S = self.S
        PT = self.PT3
        for sg in range(4):
            S.dma("pool", lambda e, sg=sg: e.dma_start(out=self.cmraw, in_=self.cmk[sg * 4:(sg + 1) * 4].rearrange("s (n p) f -> p s n f", p=128)),
                  writes=["cmraw"], sem="cmr")
            S.dma("pool", lambda e, sg=sg: e.dma_start(out=self.CMV, in_=self.cmv[sg * 4:(sg + 1) * 4].rearrange("s (n p) f -> p s n f", p=128)),
                  writes=["CMV"], sem="cmv")
            for si in range(4):
                for half in range(2):
                    pass
                ps, pk = self.nps()

                def tr(e, ps=ps, si=si):
                    ins = None
                    for h in range(4):
                        for n in range(2):
                            ins = e.transpose(out=self.psb(ps, h * 2 + n), in_=self.cmraw[:, si, n, h * 128:(h + 1) * 128], identity=self.identb[:])
                    return ins
                S.op("pe", tr, reads=["cmraw", "identb"], writes=[pk])
                S.op("act", lambda e, ps=ps, si=si: e.activation(out=self.CMK[:, si, :, :], in_=ps[:, 0:512].bitcast(BF16).rearrange("p (h t) -> p h t", h=4),
                                                               func=AF.Copy), reads=[pk], writes=["CMK"])
            ps, pk = self.nps()

            def sc(e, ps=ps, sg=sg):
                ins = None
                for n in range(2):
                    for si in range(4):
                        for h in range(4):
                            c = b0 + (sg * 4 + si) * 8
                            o = n * 128 + si * 32 + h * 8
                            ins = e.matmul(ps[:, o:o + 8], lhsT=self.CMK[:, si, h, n * 128:(n + 1) * 128], rhs=self.qm[:, h, c:c + 8], start=True, stop=True)
                return ins
            S.op("pe", sc, reads=["CMK", "hT"], writes=[pk])
            S.op("act", lambda e, ps=ps: e.activation(out=PT[:, 0:256], in_=ps[:, 0:256], func=AF.Exp, scale=MEM_SCALE), reads=[pk], writes=["PT3"])
            po, pok = self.nps()

            def pv(e, po=po):
                ins = None
                for si in range(4):
                    for h in range(4):
                        o = si * 32 + h * 8
                        for n in range(2):
                            ins = e.matmul(po[:, o:o + 8], lhsT=self.CMV[:, si, n, h * 128:(h + 1) * 128], rhs=PT[:, n * 128 + o:n * 128 + o + 8],
                                           start=(n == 0), stop=(n == 1))
                for n in range(2):
                    ins = e.matmul(po[:, 512:640], lhsT=self.ones1[:], rhs=PT[:, n * 128:(n + 1) * 128], start=(n == 0), stop=(n == 1))
                return ins
            S.op("pe", pv, reads=["CMV", "PT3", "ones1"], writes=[pok])
            t1, k1 = self.ntmp()
            S.op("dve", lambda e, po=po, t1=t1: e.reciprocal(out=t1[:, 0:128], in_=po[:, 512:640]), reads=[pok], writes=[k1])
            c = b0 + sg * 32
            S.op("dve", lambda e, po=po, t1=t1, c=c: e.tensor_tensor(
                out=self.om[:, 0:4, c:c + 32].rearrange("p h (s t) -> p h s t", s=4),
                in0=po[:, 0:128].rearrange("p (s h t) -> p h s t", s=4, h=4),
                in1=t1[:, 0:128].rearrange("p (s h t) -> p h s t", s=4, h=4), op=ALU.mult),
                reads=[pok, k1], writes=["uT"])

    def phase3(self):
        S = self.S
        self.carve3()
        self.half_ok = False
        S.dma("sp", lambda e: e.dma_start(out=self.grow[:, :], in_=self.gfin_d[:, :]), writes=["grow"], sem="c0")
        self.mem_kv3()
        if P3 < 2:
            return
        for bi, (t0, TB) in enumerate(OWN_BLOCKS):
            self.half_ok = TB <= 512
            S.dma("sp", lambda e, t0=t0, TB=TB: e.dma_start(out=self.xT[:, :, 0:TB], in_=self.hs[:, :, t0:t0 + TB]), writes=["xT"], sem="hsin")
            nfin = self.norm_begin(TB, 1)
            rs = self.rstd
            def epi_ga(j, ps, pk, TB=TB):
                S.op("dve", lambda e: e.tensor_tensor(out=self.m1[:, j, 0:TB], in0=ps[:, 0:TB], in1=rs[:, 0:TB], op=ALU.mult),
                     reads=[pk, "rstd"], writes=["hT"])
                S.op("act", lambda e: e.activation(out=self.m1[:, j, 0:TB], in_=self.m1[:, j, 0:TB], func=AF.Sigmoid), reads=["hT"], writes=["hT"])
            self.proj("gate", 8, self.uT, ["uT"], TB, 0, 1024, epi_ga, after_first=nfin)

            def epi_ba(j, ps, pk, TB=TB):
                S.op("dve", lambda e: e.tensor_tensor(out=self.m1[:, j, 0:TB], in0=ps[:, 0:TB], in1=self.m1[:, j, 0:TB], op=ALU.mult),
                     reads=[pk, "hT"], writes=["hT"])
            self.proj("ba", 4, self.OA[:, :, t0:t0 + TB], ["OAB"], TB, 0, 1024, epi_ba)

            def epi_gb(j, ps, pk, TB=TB):
                S.op("dve", lambda e: e.tensor_tensor(out=self.m2[:, j, 0:TB], in0=ps[:, 0:TB], in1=rs[:, 0:TB], op=ALU.mult),
                     reads=[pk, "rstd"], writes=["hT"])
                S.op("act", lambda e: e.activation(out=self.m2[:, j, 0:TB], in_=self.m2[:, j, 0:TB], func=AF.Sigmoid), reads=["hT"], writes=["hT"])
            self.proj("gate", 8, self.uT, ["uT"], TB, 1024, 1024, epi_gb)

            def epi_bb(j, ps, pk, TB=TB):
                t1, k1 = self.ntmp()
                S.op("dve", lambda e: e.tensor_tensor(out=t1[:, 0:TB], in0=ps[:, 0:TB], in1=self.m2[:, j, 0:TB], op=ALU.mult),
                     reads=[pk, "hT"], writes=[k1])
                S.op("dve", lambda e: e.tensor_tensor(out=self.m1[:, j, 0:TB], in0=t1[:, 0:TB], in1=self.m1[:, j, 0:TB], op=ALU.add),
                     reads=[k1, "hT"], writes=["hT"])
            self.proj("bb", 2, self.OB[:, :, t0:t0 + TB], ["OAB"], TB, 0, 1024, epi_bb)

            def epi_add(j, ps, pk, TB=TB):
                S.op("dve", lambda e: e.tensor_tensor(out=self.xT[:, j, 0:TB], in0=ps[:, 0:TB], in1=self.xT[:, j, 0:TB], op=ALU.add),
                     reads=[pk, "xT"], writes=["xT"])
            self.proj("out", 8, self.m1, ["hT"], TB, 0, 1024, epi_add)
            if P3 < 3:
                continue
            nfin2 = self.norm_begin(TB, 2)

            def epi_q(j, ps, pk, TB=TB):
                S.op("dve", lambda e: e.tensor_tensor(out=self.qm[:, j, 0:TB], in0=ps[:, 0:TB], in1=rs[:, 0:TB], op=ALU.mult),
                     reads=[pk, "rstd"], writes=["hT"])
            self.proj("mq", 8, self.uT, ["uT"], TB, 0, 512, epi_q, after_first=nfin2)
            self.cross_prompt(512)
            if P3 < 4:
                continue
            if TB > 512:
                self.cross_sample(512)
            self.proj("mo", 4, self.om, ["uT"], TB, 0, 1024, epi_add)
            if P3 < 5:
                continue
            self.ffn(TB, "g2", "u2", "d2", self.norm_begin(TB, 4))
            if P3 < 6:
                continue
            for t in range(TB // 128):
                ps, pk = self.nps()

                def tr(e, ps=ps, t=t):
                    ins = None
                    for c in range(8):
                        ins = e.transpose(out=ps[:, c * 128:(c + 1) * 128], in_=self.xT[:, c, t * 128:(t + 1) * 128], identity=self.ident[:])
                    return ins
                S.op("pe", tr, reads=["xT", "ident"], writes=[pk])
                i = t % 2
                st, sk = self.ystg[i], "ystg%d" % i
                ss = self.fstat[:, 2 * i:2 * i + 1]
                fk = "fstat%d" % i
                S.op("dve", lambda e, ss=ss: e.memset(ss, 0.0), writes=[fk])
                S.op("act", lambda e, ps=ps, ss=ss: e.activation(out=self.yjunk[:, :], in_=ps[:, :], func=AF.Square, accum_out=ss),
                     reads=[pk, fk], writes=[fk, "yjunk", "stg0", "stg1"])
                S.op("act", lambda e, ss=ss: e.activation(out=ss, in_=ss, func=AF.Sqrt, bias=self.epsb[:, 0:1], scale=1.0 / 1024),
                     reads=[fk, "consts"], writes=[fk])
                S.op("dve", lambda e, ss=ss: e.reciprocal(out=ss, in_=ss), reads=[fk], writes=[fk])
                S.op("dve", lambda e, ps=ps, st=st, ss=ss: e.scalar_tensor_tensor(out=st[:, :], in0=ps[:, :], scalar=ss, in1=self.grow[:, :],
                                                                                op0=ALU.mult, op1=ALU.mult),
                     reads=[pk, fk, "grow"], writes=[sk, "stg0", "stg1"] if i == 1 else [sk])
                r0 = t0 + t * 128
                S.dma("pool", lambda e, st=st, r0=r0: e.dma_start(out=self.y[r0:r0 + 128, :], in_=st[:, :]), reads=[sk], sem="yo%d" % i)

    def build(self):
        self.consts()
        self.phase1()
        self.wfence()
        self.S.barrier()
        if STAGE >= 2:
            self.phase2()
            self.S.barrier()
        if STAGE >= 3:
            self.phase3()


def build_program():
    nc0 = bass.Bass("TRN2", target_bir_lowering=False)
    with ExitStack() as es0:
        p0 = Prog(nc0, es0, Sch(nc0, es0, dummy=True))
        p0.alloc()
        p0.build()
        jobs = p0.wjobs
    nc = bass.Bass("TRN2", target_bir_lowering=False)
    with ExitStack() as es:
        S = Sch(nc, es)
        p = Prog(nc, es, S, wjobs=jobs)
        p.alloc()
        p.build()
        assert p.wi == len(jobs)
        S.finalize()
    return nc


def _rope_tables(pos):
    half = 8
    inv_freq = np.power(np.float32(500000.0), -np.arange(half, dtype=np.float32) / np.float32(half)).astype(np.float32)
    ang = pos.astype(np.float32)[:, None] * inv_freq[None, :]
    cos = np.cos(ang).astype(np.float32)
    sin = np.sin(ang).astype(np.float32)
    T = pos.shape[0]
    tab = np.zeros((2, 128, T), np.float32)
    tab[0] = 1.0
    for hh in range(2):
        b = hh * 64
        tab[0, b:b + 8] = cos.T
        tab[0, b + 8:b + 16] = cos.T
        tab[1, b:b + 8] = -sin.T
        tab[1, b + 8:b + 16] = sin.T
    return tab


def _swap_cols(wcols):
    w = wcols.reshape(wcols.shape[0], -1, 64).copy()
    a = w[:, :, 0:8].copy()
    w[:, :, 0:8] = w[:, :, 8:16]
    w[:, :, 8:16] = a
    return w.reshape(wcols.shape)


def _interleave(z, s):
    K = z.shape[0]
    n = z.shape[1] // 128
    return np.stack([z.reshape(K, n, 128), s.reshape(K, n, 128)], axis=2).reshape(K, 2 * n * 128)


def _mask_s():
    mk = np.zeros((128, 408), np.float32)
    for g, d in enumerate((1, 4, 16)):
        for t in range(8):
            for n in range(16):
                pos = n * 128 + np.arange(128)
                diff = 2048 + t - pos
                ok = (diff % d == 0) & (diff // d >= 1) & (diff // d <= 128)
                mk[:, n * 24 + g * 8 + t] = ok
            for tp in range(8):
                diff = t - tp
                mk[tp, 384 + g * 8 + t] = float(diff >= 0 and diff % d == 0 and diff // d <= 128)
    return mk


MASK_S = _mask_s()


def make_in_maps(inp):
    f = lambda k: np.ascontiguousarray(np.asarray(inp[k], dtype=np.float32))
    x_prompt, x_sample = f("x_prompt"), f("x_sample")
    w_in = f("w_in")[0]
    qa, ka, va = w_in[:, 0:512], w_in[:, 512:640], w_in[:, 640:768]
    qb, kb, vb = w_in[:, 768:1536], w_in[:, 1536:1792], w_in[:, 1792:2048]
    gate = w_in[:, 2048:4096]
    perm = [h for c in range(4) for h in (c, c + 4)]
    qa_p = qa.reshape(D, 8, 64)[:, perm, :].reshape(D, 512)
    w_q = np.ascontiguousarray(np.concatenate([_interleave(qa_p, _swap_cols(qa_p)), _interleave(qb, _swap_cols(qb))], axis=1))
    w_kv = np.ascontiguousarray(np.concatenate([_interleave(kb, _swap_cols(kb)), vb, va, ka, _swap_cols(ka)], axis=1))
    w_ba = np.ascontiguousarray(f("w_branch_a")[0].reshape(8, 64, D)[perm].reshape(512, D))
    gn = np.stack([f("ffn1_norm")[0], f("mix_norm")[0], f("mem_q_norm")[0], f("mem_kv_norm")[0], f("ffn2_norm")[0],
                   f("final_norm")], 0)
    gn = np.ascontiguousarray(gn.reshape(6, 8, 128).transpose(2, 0, 1).reshape(128, 48))
    sink = f("attn_sink")[0]
    gfin_row = np.ascontiguousarray(np.broadcast_to(f("final_norm")[None, :], (128, 1024)))
    sinkb = np.ascontiguousarray(np.repeat(sink.reshape(2, 4, 1), 128, axis=2).reshape(2, 512))
    shared = {
        "w_g1": f("ffn1_w_gate")[0], "w_u1": f("ffn1_w_up")[0], "w_d1": f("ffn1_w_down")[0],
        "w_g2": f("ffn2_w_gate")[0], "w_u2": f("ffn2_w_up")[0], "w_d2": f("ffn2_w_down")[0],
        "w_q": w_q, "w_kv": w_kv, "w_gate": np.ascontiguousarray(gate), "w_ba": w_ba, "w_bb": f("w_branch_b")[0],
        "w_out": f("w_out")[0], "w_mq": f("w_mem_q")[0], "w_mk": f("w_mem_k")[0], "w_mv": f("w_mem_v")[0],
        "w_mo": f("w_mem_o")[0], "gn": gn, "ident": np.eye(128, dtype=np.float32), "sinkb": sinkb,
    }
    jj = np.arange(128)[:, None]
    ii = np.arange(128)[None, :]
    csk, csv = f("cache_swa_k")[0], f("cache_swa_v")[0]
    cdk, cdv = f("cache_dil_k")[0], f("cache_dil_v")[0]
    cmk, cmv = f("cache_mem_k")[0], f("cache_mem_v")[0]
    mem_prompt = f("mem_prompt")
    in_maps = []
    for c in range(8):
        n, ch = c // 4, c % 4
        m = dict(shared)
        xo = np.concatenate([x_prompt[n, ch * 2048:(ch + 1) * 2048], x_sample[c * 16:(c + 1) * 16].reshape(128, D)], 0)
        m["xo"] = np.ascontiguousarray(xo)
        m["xh"] = np.ascontiguousarray(x_prompt[n, (ch - 1) * 2048:ch * 2048]) if ch > 0 else np.zeros((2048, D), np.float32)
        m["xm"] = np.ascontiguousarray(mem_prompt[n])
        pos_o = np.concatenate([ch * 2048 + np.arange(2048), np.tile(16384 + np.arange(8), 16)])
        m["rope_o"] = _rope_tables(pos_o)
        m["rope_h"] = _rope_tables(np.maximum((ch - 1) * 2048 + np.arange(2048), 0))
        masks = np.zeros((128, 8, 128), np.float32)
        masks[:, 0] = (jj <= ii)
        masks[:, 1] = (jj >= ii)
        masks[:, 2] = (jj <= ii)
        masks[:, 3] = (jj >= ii) * (1.0 if ch > 0 else 0.0)
        masks[:, 4, 0:8] = (jj >= ii)[:, 0:8]
        masks[0:8, 5, 0:8] = (jj <= ii)[0:8, 0:8]
        m["masks"] = masks
        m["masks_s"] = MASK_S
        m["gfin_row"] = gfin_row
        sl = slice(c * 16, (c + 1) * 16)
        m["cswk"] = np.ascontiguousarray(csk[sl].reshape(16, 128, 128))
        m["cswv"] = np.ascontiguousarray(csv[sl].reshape(16, 128, 128))
        m["cdk"] = np.ascontiguousarray(cdk[sl].reshape(16, 2048, 256))
        m["cdv"] = np.ascontiguousarray(cdv[sl].reshape(16, 2048, 256))
        m["cmk"] = np.ascontiguousarray(cmk[sl].reshape(16, 256, 512))
        m["cmv"] = np.ascontiguousarray(cmv[sl].reshape(16, 256, 512))
        in_maps.append(m)
    return in_maps


def kernel(**inp):
    in_maps = make_in_maps(inp)
    nc = build_program()
    res = run_bass_kernel_spmd(nc, in_maps, core_ids=list(range(8)))
    R = res.results
    D = 1024
    y_prompt = np.zeros((2, 8192, D), np.float32)
    y_sample = np.zeros((128, 8, D), np.float32)
    swa_k_p = np.zeros((1, 2, 128, 2, 64), np.float32)
    swa_v_p = np.zeros((1, 2, 128, 2, 64), np.float32)
    dil_k_p = np.zeros((1, 2, 2048, 4, 64), np.float32)
    dil_v_p = np.zeros((1, 2, 2048, 4, 64), np.float32)
    mem_k_p = np.zeros((1, 2, 256, 4, 128), np.float32)
    mem_v_p = np.zeros((1, 2, 256, 4, 128), np.float32)
    swa_k_s = np.zeros((1, 128, 8, 2, 64), np.float32)
    swa_v_s = np.zeros((1, 128, 8, 2, 64), np.float32)
    dil_k_s = np.zeros((1, 128, 8, 4, 64), np.float32)
    dil_v_s = np.zeros((1, 128, 8, 4, 64), np.float32)
    for c in range(8):
        n, ch = c // 4, c % 4
        y, okv, om = R[c]["y"], R[c]["o_kv"], R[c]["o_mem"]
        y_prompt[n, ch * 2048:(ch + 1) * 2048] = y[0:2048]
        y_sample[c * 16:(c + 1) * 16] = y[2048:].reshape(16, 8, D)
        s = okv[2048:]
        swa_k_s[0, c * 16:(c + 1) * 16] = s[:, 0:128].reshape(16, 8, 2, 64)
        dil_k_s[0, c * 16:(c + 1) * 16] = s[:, 128:384].reshape(16, 8, 4, 64)
        swa_v_s[0, c * 16:(c + 1) * 16] = s[:, 384:512].reshape(16, 8, 2, 64)
        dil_v_s[0, c * 16:(c + 1) * 16] = s[:, 512:768].reshape(16, 8, 4, 64)
        if ch == 3:
            p = okv[0:2048]
            swa_k_p[0, n] = p[1920:, 0:128].reshape(128, 2, 64)
            swa_v_p[0, n] = p[1920:, 384:512].reshape(128, 2, 64)
            dil_k_p[0, n] = p[:, 128:384].reshape(2048, 4, 64)
            dil_v_p[0, n] = p[:, 512:768].reshape(2048, 4, 64)
        if ch == 0:
            mem_k_p[0, n] = om[:, 0:512].reshape(256, 4, 128)
            mem_v_p[0, n] = om[:, 512:1024].reshape(256, 4, 128)
    return (y_prompt, y_sample, swa_k_p, swa_v_p, dil_k_p, dil_v_p, mem_k_p, mem_v_p, swa_k_s, swa_v_s, dil_k_s, dil_v_s)
```

```python
from contextlib import ExitStack
import numpy as np
import concourse.bass as bass
import concourse.mybir as mybir
from concourse.bass_utils import run_bass_kernel_spmd

F32 = mybir.dt.float32
BF16 = mybir.dt.bfloat16
AF = mybir.ActivationFunctionType
ALU = mybir.AluOpType

P2 = 9
P2S = 9
P2X = 9
P2Y = 0
PBG = 3
HALF_W = 0
MKV = 1
P3 = 9
PBX = 9
SKIP_AS = False
PBN = 1
STAGE = 9

D = 1024
DFF = 2816
NT = 17
T_OWN = NT * 128
T_HALO = 2048
TBMAX = 640
OWN_BLOCKS = [(0, 512), (512, 512), (1024, 512), (1536, 640)]
HALO_BLOCKS = [(0, 512), (512, 512), (1024, 512), (1536, 512)]
ATTN_SCALE = 0.125
MEM_SCALE = 128 ** -0.5
EPS = 1e-6
ENGS = ("pe", "act", "dve", "pool", "sp")


class Sch:
    def __init__(self, nc, es, dummy=False):
        self.nc, self.es, self.dummy = nc, es, dummy
        self.cnt = {e: 0 for e in ENGS}
        self.ops = {e: [] for e in ENGS}
        self.seen = {e: {} for e in ENGS}
        self.res = {}
        self.dsem = {}
        if not dummy:
            self.sem = {e: es.enter_context(nc.semaphore("sem_" + e)) for e in ENGS}

    def dma_sem(self, name):
        if name not in self.dsem:
            h = None if self.dummy else self.es.enter_context(self.nc.semaphore("d_" + name))
            self.dsem[name] = [h, 0]
        return name

    @staticmethod
    def _flat(keys):
        out = []
        for k in keys:
            if isinstance(k, (tuple, list)):
                out.extend(Sch._flat(k))
            else:
                out.append(k)
        return out

    def _collect(self, eng, reads, writes):
        deps = {}

        def add(d):
            if d is not None and deps.get(d[0], 0) < d[1]:
                deps[d[0]] = d[1]
        for k in reads:
            r = self.res.get(k)
            if r is not None:
                add(r[0])
        for k in writes:
            r = self.res.get(k)
            if r is not None:
                add(r[0])
                for d in r[1]:
                    add(d)
        waits = []
        seen = self.seen[eng]
        for k, v in deps.items():
            if k == eng and eng in ("pe", "sp", "pool"):
                continue
            if seen.get(k, 0) >= v:
                continue
            seen[k] = v
            waits.append((k, v))
        return waits

    def _record(self, token, reads, writes):
        for k in reads:
            self.res.setdefault(k, [None, []])[1].append(token)
        for k in writes:
            self.res[k] = [token, []]

    def op(self, eng, fn, reads=(), writes=(), drain=False):
        if self.dummy:
            return
        reads, writes = self._flat(reads), self._flat(writes)
        waits = self._collect(eng, reads, writes)
        if drain and self.cnt[eng] > 0:
            waits.append((eng, self.cnt[eng]))
        self.cnt[eng] += 1
        self.ops[eng].append((waits, fn, ("e", eng)))
        self._record((eng, self.cnt[eng]), reads, writes)

    def dma(self, queue, fn, reads=(), writes=(), sem=None):
        if self.dummy:
            return
        reads, writes = self._flat(reads), self._flat(writes)
        self.dma_sem(sem)
        waits = self._collect(queue, reads, writes)
        d = self.dsem[sem]
        d[1] += 16
        self.ops[queue].append((waits, fn, ("d", sem)))
        self._record(("D:" + sem, d[1]), reads, writes)

    def barrier(self):
        if self.dummy:
            return
        allv = [(e, self.cnt[e]) for e in ENGS if self.cnt[e] > 0]
        allv += [("D:" + n, c) for n, (h, c) in self.dsem.items() if c > 0]
        for e in ENGS:
            waits = []
            for k, v in allv:
                if k == e or self.seen[e].get(k, 0) >= v:
                    continue
                self.seen[e][k] = v
                waits.append((k, v))
            if waits:
                self.ops[e].append((waits, None, None))
        self.res = {}

    def _semh(self, k):
        return self.dsem[k[2:]][0] if k.startswith("D:") else self.sem[k]

    def finalize(self):
        fin = [(e, self.cnt[e]) for e in ENGS if e != "sp" and self.cnt[e] > 0]
        fin += [("D:" + n, c) for n, (h, c) in self.dsem.items() if c > 0]
        for k, v in fin:
            assert v < 65000, (k, v)

        def run(e, eng):
            for waits, fn, inc in self.ops[e]:
                for k, v in waits:
                    eng.wait_ge(self._semh(k), v)
                if fn is None:
                    continue
                ins = fn(eng)
                if inc[0] == "e":
                    ins.then_inc(self.sem[inc[1]], 1)
                else:
                    ins.then_inc(self.dsem[inc[1]][0], 16)
            if e == "sp":
                for k, v in fin:
                    eng.wait_ge(self._semh(k), v)

        with self.nc.Block() as block:
            @block.tensor
            def _(eng):
                run("pe", eng)

            @block.scalar
            def _(eng):
                run("act", eng)

            @block.vector
            def _(eng):
                run("dve", eng)

            @block.gpsimd
            def _(eng):
                run("pool", eng)

            @block.sync
            def _(eng):
                run("sp", eng)


def nslices(n, step=512):
    return [(a, min(step, n - a)) for a in range(0, n, step)]


class Prog:
    def __init__(self, nc, es, S, wjobs=None):
        self.nc, self.es, self.S = nc, es, S
        self.recording = wjobs is None
        self.wjobs = [] if wjobs is None else wjobs
        self.wi = 0
        self.wloaded = 0
        self.psi = 0
        self.tmpi = 0
        self.uid = 0
        self.held = set()
        self.half_ok = False
        self.PTi = 0
        self.wscr = {}
        self.next_x = None
        self.mid_hook = None

    def alloc(self):
        nc, es = self.nc, self.es
        dt = lambda n, s, d, k="ExternalInput": nc.dram_tensor(n, s, d, kind=k).ap()
        self.xo = dt("xo", [T_OWN, D], F32)
        self.xh = dt("xh", [T_HALO, D], F32)
        self.xm = dt("xm", [256, D], F32)
        self.rope_o = dt("rope_o", [2, 128, T_OWN], F32)
        self.rope_h = dt("rope_h", [2, 128, T_HALO], F32)
        self.gn = dt("gn", [128, 48], F32)
        self.ident_d = dt("ident", [128, 128], F32)
        self.masks_d = dt("masks", [128, 8, 128], F32)
        self.sinkb = dt("sinkb", [2, 512], F32)
        self.maskS_d = dt("masks_s", [128, 408], F32)
        self.gfin_d = dt("gfin_row", [128, 1024], F32)
        self.w = {}
        for n, s in [("g1", [D, DFF]), ("u1", [D, DFF]), ("d1", [DFF, D]), ("g2", [D, DFF]), ("u2", [D, DFF]),
                     ("d2", [DFF, D]), ("q", [D, 2560]), ("kv", [D, 1152]), ("gate", [D, 2048]),
                     ("ba", [512, D]), ("bb", [256, D]), ("out", [D, D]), ("mq", [D, 512]), ("mk", [D, 512]),
                     ("mv", [D, 512]), ("mo", [512, D])]:
            self.w[n] = dt("w_" + n, s, F32)
        self.cswk = dt("cswk", [16, 128, 128], F32)
        self.cswv = dt("cswv", [16, 128, 128], F32)
        self.cdk = dt("cdk", [16, 2048, 256], F32)
        self.cdv = dt("cdv", [16, 2048, 256], F32)
        self.cmk = dt("cmk", [16, 256, 512], F32)
        self.cmv = dt("cmv", [16, 256, 512], F32)
        self.y = dt("y", [T_OWN, D], F32, "ExternalOutput")
        self.o_kv = dt("o_kv", [T_OWN, 768], F32, "ExternalOutput")
        self.o_mem = dt("o_mem", [256, 1024], F32, "ExternalOutput")
        self.hs = dt("hs", [128, 8, T_OWN], F32, "Internal")

        sb = lambda n, s, d: es.enter_context(nc.sbuf_tensor(n, s, d))
        self.ident = sb("identf", [128, 128], F32)
        self.identb = sb("identb", [128, 128], BF16)
        self.ones = sb("ones", [128, 128], BF16)
        self.ones1 = sb("ones1", [128, 128], BF16)
        self.onesf = sb("onesf", [128, 64], F32)
        self.epsb = sb("epsb", [128, 1], F32)
        self.gnt = sb("gnt", [128, 48], F32)
        self.masks = sb("masksb", [128, 8, 128], BF16)
        self.esink = sb("esink", [128, 2, 512], F32)
        self.QA = sb("QA", [128, 4, T_OWN], BF16)
        self.QB = sb("QB", [128, 6, T_OWN], BF16)
        self.KA = sb("KA", [128, 128 + T_OWN], BF16)
        self.KB = sb("KB", [128, 2, T_HALO + T_OWN], BF16)
        self.VAT = sb("VAT", [128, 128 + T_OWN], BF16)
        self.VBT = sb("VBT", [128, 2, T_HALO + T_OWN], BF16)
        self.ARENA = 57000
        self.arena = sb("arena", [128, self.ARENA], BF16)
        self.ps = [es.enter_context(nc.psum_tensor("ps%d" % i, [128, 1024], F32)) for i in range(4)]
        o = 0
        def carve(nelem_bf16, dtype=BF16):
            nonlocal o
            v = self.arena[:, o:o + nelem_bf16]
            o += nelem_bf16
            return v.bitcast(F32) if dtype == F32 else v
        self.xT = carve(8 * TBMAX * 2, F32).rearrange("p (c t) -> p c t", c=8)
        self.masksf = self.xT[:, 0:2, 0:512].rearrange("p c (m t) -> p c m t", m=4)
        self.uT = carve(8 * TBMAX).rearrange("p (c t) -> p c t", c=8)
        self.hT = carve(22 * TBMAX).rearrange("p (c t) -> p c t", c=22)
        self.tmp = [carve(TBMAX * 2, F32) for _ in range(3)]
        self.o_ropet = o
        self.ropet = carve(2 * TBMAX * 2, F32).rearrange("p (c t) -> p c t", c=2)
        self.stg = [carve(TBMAX * 2, F32) for _ in range(2)]
        self.NW = 3
        self.WCAP = 5632
        self.wslot = [carve(self.WCAP) for _ in range(self.NW)]
        self.o_p13 = o
        self.rstd = carve(TBMAX * 2, F32)
        assert o <= self.ARENA, o
        self.xstage = self.hT.rearrange("p c t -> p (c t)")[:, 0:5 * 2048].bitcast(F32).rearrange("p (n d) -> p n d", n=5)
        self.sq = self.hT

    def nps(self, hold=False, half=False):
        if half and self.half_ok:
            while True:
                b = self.psi % 8
                self.psi += 1
                if ("bk%d" % b) not in self.held:
                    break
            return self.ps[b // 2][:, (b % 2) * 512:(b % 2) * 512 + 512], ("bk%d" % b,)
        while True:
            if self.psi % 2:
                self.psi += 1
            b = self.psi % 8
            self.psi += 2
            if ("bk%d" % b) not in self.held and ("bk%d" % (b + 1)) not in self.held:
                break
        key = ("bk%d" % b, "bk%d" % (b + 1))
        if hold:
            self.held.update(key)
        return self.ps[b // 2], key

    def unhold(self, key):
        for k in key:
            self.held.discard(k)

    def ntmp(self):
        i = self.tmpi % 3
        self.tmpi += 1
        return self.tmp[i], "tmp%d" % i

    def key(self, s):
        self.uid += 1
        return "%s#%d" % (s, self.uid)

    def wget(self, *parts):
        desc = tuple(parts)
        if not self.recording:
            i = self.wi
            assert self.wjobs[i] == desc, (i, self.wjobs[i], desc)
            self.wi += 1
            while (self.wloaded < len(self.wjobs) and self.wloaded < i + self.NW
                   and self.wjobs[self.wloaded] != ("FENCE",)):
                self._wload(self.wloaded)
                self.wloaded += 1
            s = i % self.NW
        else:
            self.wjobs.append(desc)
            s = 0
        return self._wviews(s, desc), "w%d" % s

    def wfence(self):
        if self.recording:
            self.wjobs.append(("FENCE",))
            return
        assert self.wjobs[self.wi] == ("FENCE",) and self.wloaded == self.wi
        self.wi += 1
        self.wloaded = self.wi

    def _wviews(self, s, desc):
        views, o = [], 0
        for (name, kc0, kcn, c0, ncols) in desc:
            views.append(self.wslot[s][:, o:o + kcn * ncols].rearrange("p (k f) -> p k f", k=kcn))
            o += kcn * ncols
        assert o <= self.WCAP
        return views

    def _wload(self, j):
        desc = self.wjobs[j]
        s = j % self.NW
        n = sum(kcn * ncols for (_, _, kcn, _, ncols) in desc)
        reuse = sum(1 for d in self.wjobs if d == desc) > 1
        if desc in self.wscr:
            scr, skey = self.wscr[desc]
            dst = self.wslot[s][:, 0:n]
            self.S.dma("sp", lambda e, dst=dst, scr=scr: e.dma_start(out=dst, in_=scr[:, :]), reads=[skey], writes=["w%d" % s], sem="wh%d" % s)
            return
        for dst, (name, kc0, kcn, c0, ncols) in zip(self._wviews(s, desc), desc):
            src = self.w[name][kc0 * 128:(kc0 + kcn) * 128, c0:c0 + ncols].rearrange("(k p) f -> p k f", p=128)
            self.S.dma("pool", lambda e, dst=dst, src=src: e.dma_start(out=dst, in_=src), writes=["w%d" % s], sem="w%d" % s)
        if reuse:
            scr = self.nc.dram_tensor("wscr%d" % len(self.wscr), [128, n], BF16, kind="Internal").ap()
            skey = "wscr%d" % len(self.wscr)
            self.wscr[desc] = (scr, skey)
            srcv = self.wslot[s][:, 0:n]
            self.S.dma("sp", lambda e, scr=scr, srcv=srcv: e.dma_start(out=scr[:, :], in_=srcv), reads=["w%d" % s], writes=[skey], sem="ws%d" % s)

    def proj(self, wname, KC, act, actkeys, TB, c0, ncols, epi, after_first=None, after_group0=None):
        S = self.S
        gw = 512 if KC <= 8 else 256
        for g0 in range(0, ncols, gw):
            gn = min(gw, ncols - g0)
            (wv,), wkey = self.wget((wname, 0, KC, c0 + g0, gn))
            for j in range(gn // 128):
                ps, pk = self.nps(half=True)

                def mm(e, wv=wv, j=j, ps=ps):
                    ins = None
                    for (a, n) in nslices(TB):
                        for k in range(KC):
                            ins = e.matmul(ps[:, a:a + n], lhsT=wv[:, k, j * 128:(j + 1) * 128], rhs=act[:, k, a:a + n],
                                           start=(k == 0), stop=(k == KC - 1))
                    return ins
                S.op("pe", mm, reads=[wkey] + list(actkeys), writes=[pk])
                if after_first is not None:
                    after_first()
                    after_first = None
                epi((g0 // 128) + j, ps, pk)
            if after_group0 is not None:
                after_group0()
                after_group0 = None

    def load_x_dma(self, src, r0, TB):
        nt = TB // 128
        xs = self.xstage
        self.S.dma("sp", lambda e: e.dma_start(out=xs[:, 0:nt, :], in_=src[r0:r0 + TB, :].rearrange("(n p) d -> p n d", p=128)),
                   writes=["hT"], sem="xin")

    def load_x(self, src, r0, TB, tag, dma=True):
        S = self.S
        nt = TB // 128
        xs = self.xstage
        if dma:
            self.load_x_dma(src, r0, TB)
        for t in range(nt):
            ps, pk = self.nps()

            def tr(e, t=t, ps=ps):
                ins = None
                for c in range(8):
                    ins = e.transpose(out=ps[:, c * 128:(c + 1) * 128], in_=xs[:, t, c * 128:(c + 1) * 128], identity=self.ident[:])
                return ins
            S.op("pe", tr, reads=["hT", "ident"], writes=[pk])
            eng = "dve" if t % 2 == 0 else "act"
            if eng == "dve":
                S.op("dve", lambda e, t=t, ps=ps: e.tensor_copy(out=self.xT[:, :, t * 128:(t + 1) * 128],
                                                                 in_=ps[:].rearrange("p (c t) -> p c t", c=8)),
                     reads=[pk], writes=["xT"])
            else:
                S.op("act", lambda e, t=t, ps=ps: e.activation(out=self.xT[:, :, t * 128:(t + 1) * 128],
                                                                in_=ps[:].rearrange("p (c t) -> p c t", c=8), func=AF.Copy),
                     reads=[pk], writes=["xT"])

    def rmsnorm(self, TB, gi, out=None, outkey="uT", out32=None):
        S = self.S
        out = self.uT if out is None else out
        S.op("act", lambda e: e.activation(out=self.sq[:, 0:8, 0:TB], in_=self.xT[:, :, 0:TB], func=AF.Square),
             reads=["xT"], writes=["hT"])
        ps, pk = self.nps()

        def mm(e):
            ins = None
            for (a, n) in nslices(TB):
                for c in range(8):
                    ins = e.matmul(ps[:, a:a + n], lhsT=self.ones[:], rhs=self.sq[:, c, a:a + n], start=(c == 0), stop=(c == 7))
            return ins
        S.op("pe", mm, reads=["hT", "ones"], writes=[pk])
        rs, rk = self.ntmp()
        S.op("act", lambda e: e.activation(out=rs[:, 0:TB], in_=ps[:, 0:TB], func=AF.Sqrt, bias=self.epsb[:, 0:1], scale=1.0),
             reads=[pk, "consts"], writes=[rk])
        S.op("dve", lambda e: e.reciprocal(out=rs[:, 0:TB], in_=rs[:, 0:TB]), reads=[rk], writes=[rk])
        for c in range(8):
            S.op("dve", lambda e, c=c: e.scalar_tensor_tensor(out=out[:, c, 0:TB], in0=self.xT[:, c, 0:TB],
                                                             scalar=self.gnt[:, gi * 8 + c:gi * 8 + c + 1], in1=rs[:, 0:TB],
                                                             op0=ALU.mult, op1=ALU.mult),
                 reads=["xT", rk, "consts"], writes=[outkey])

    def norm_begin(self, TB, gi):
        S = self.S
        for c in range(8):
            sc_ap = self.gnt[:, gi * 8 + c:gi * 8 + c + 1]
            if c < 2:
                S.op("act", lambda e, c=c, sc_ap=sc_ap: e.activation(out=self.uT[:, c, 0:TB], in_=self.xT[:, c, 0:TB], func=AF.Copy, scale=sc_ap),
                     reads=["xT", "consts"], writes=["uT"])
            else:
                S.op("dve", lambda e, c=c, sc_ap=sc_ap: e.tensor_scalar(out=self.uT[:, c, 0:TB], in0=self.xT[:, c, 0:TB], scalar1=sc_ap, scalar2=None,
                                                                       op0=ALU.mult),
                     reads=["xT", "consts"], writes=["uT"])
        S.op("act", lambda e: e.activation(out=self.sq[:, 0:8, 0:TB], in_=self.xT[:, :, 0:TB], func=AF.Square),
             reads=["xT"], writes=["hT"])
        state = {"done": False}

        def finish():
            if state["done"]:
                return
            state["done"] = True
            ps, pk = self.nps(half=True)

            def mm(e):
                ins = None
                for (a, n) in nslices(TB):
                    for c in range(8):
                        ins = e.matmul(ps[:, a:a + n], lhsT=self.ones[:], rhs=self.sq[:, c, a:a + n], start=(c == 0), stop=(c == 7))
                return ins
            S.op("pe", mm, reads=["hT", "ones"], writes=[pk])
            S.op("act", lambda e: e.activation(out=self.rstd[:, 0:TB], in_=ps[:, 0:TB], func=AF.Sqrt, bias=self.epsb[:, 0:1], scale=1.0),
                 reads=[pk, "consts"], writes=["rstd"])
            S.op("dve", lambda e: e.reciprocal(out=self.rstd[:, 0:TB], in_=self.rstd[:, 0:TB]), reads=["rstd"], writes=["rstd"])
        return finish

    def ffn(self, TB, wg, wu, wd, nfin=None):
        S = self.S
        rs = self.rstd
        for g0 in range(0, DFF, 256):
            gn = 256
            (wgv, wuv), gk = self.wget((wg, 0, 8, g0, gn), (wu, 0, 8, g0, gn))
            uk = gk
            for j in range(gn // 128):
                f = g0 // 128 + j
                psg, pgk = self.nps(half=True)
                psu, puk = self.nps(half=True)

                def mm(e, wv, ps, j=j):
                    ins = None
                    for (a, n) in nslices(TB):
                        for k in range(8):
                            ins = e.matmul(ps[:, a:a + n], lhsT=wv[:, k, j * 128:(j + 1) * 128], rhs=self.uT[:, k, a:a + n],
                                           start=(k == 0), stop=(k == 7))
                    return ins
                S.op("pe", lambda e, wv=wgv, ps=psg, j=j: mm(e, wv, ps, j), reads=[gk, "uT"], writes=[pgk])
                S.op("pe", lambda e, wv=wuv, ps=psu, j=j: mm(e, wv, ps, j), reads=[uk, "uT"], writes=[puk])
                sg, sk = self.ntmp()
                if nfin is None:
                    S.op("act", lambda e, ps=psg, sg=sg: e.activation(out=sg[:, 0:TB], in_=ps[:, 0:TB], func=AF.Silu),
                         reads=[pgk], writes=[sk])
                else:
                    nfin()
                    S.op("dve", lambda e, ps=psg, sg=sg: e.tensor_tensor(out=sg[:, 0:TB], in0=ps[:, 0:TB], in1=rs[:, 0:TB], op=ALU.mult),
                         reads=[pgk, "rstd"], writes=[sk])
                    S.op("act", lambda e, sg=sg: e.activation(out=sg[:, 0:TB], in_=sg[:, 0:TB], func=AF.Silu), reads=[sk], writes=[sk])
                    S.op("dve", lambda e, sg=sg: e.tensor_tensor(out=sg[:, 0:TB], in0=sg[:, 0:TB], in1=rs[:, 0:TB], op=ALU.mult),
                         reads=[sk, "rstd"], writes=[sk])
                S.op("dve", lambda e, ps=psu, sg=sg, f=f: e.tensor_tensor(out=self.hT[:, f, 0:TB], in0=ps[:, 0:TB], in1=sg[:, 0:TB],
                                                                         op=ALU.mult),
                     reads=[puk, sk], writes=["hT"])

        def epi(j, ps, pk):
            S.op("dve", lambda e: e.scalar_tensor_tensor(out=self.xT[:, j, 0:TB], in0=ps[:, 0:TB], scalar=0.5,
                                                         in1=self.xT[:, j, 0:TB], op0=ALU.mult, op1=ALU.add),
                 reads=[pk, "xT"], writes=["xT"])
        self.proj(wd, 22, self.hT, ["hT"], TB, 0, D, epi)

    def consts(self):
        S = self.S
        S.dma("sp", lambda e: e.dma_start(out=self.ident[:], in_=self.ident_d[:, :]), writes=["ident"], sem="c0")
        S.dma("sp", lambda e: e.dma_start(out=self.gnt[:], in_=self.gn[:, :]), writes=["consts"], sem="c1")
        S.dma("sp", lambda e: e.dma_start(out=self.masksf, in_=self.masks_d[:, :, :].rearrange("p (c m) t -> p c m t", c=2)), writes=["xT"], sem="c2")
        S.dma("sp", lambda e: e.dma_start(out=self.esink[64:65, :, :], in_=self.sinkb[:, :].rearrange("(o g) n -> o g n", o=1)),
              writes=["esink"], sem="c3")
        S.op("dve", lambda e: e.memset(self.ones[:], 1.0 / 1024), writes=["ones"])
        S.op("dve", lambda e: e.memset(self.ones1[:], 1.0), writes=["ones1"])
        S.op("dve", lambda e: e.memset(self.onesf[:], 1.0), writes=["onesf"])
        S.op("dve", lambda e: e.memset(self.epsb[:], EPS), writes=["consts"])
        S.op("dve", lambda e: e.tensor_copy(out=self.identb[:], in_=self.ident[:]), reads=["ident"], writes=["identb"])
        S.op("dve", lambda e: e.tensor_copy(out=self.masks[:].rearrange("p (c m) t -> p c m t", c=2), in_=self.masksf), reads=["xT"], writes=["masks"])
        S.op("act", lambda e: e.activation(out=self.esink[64:65, :, :], in_=self.esink[64:65, :, :], func=AF.Exp),
             reads=["esink"], writes=["esink"])

    def qkv(self, TB, own, t0, full_kv, nfin):
        S = self.S
        rs = self.rstd

        def hook():
            nfin()
            for i in range(2):
                S.op("dve", lambda e, i=i: e.tensor_tensor(out=self.ropet[:, i, 0:TB], in0=self.ropet[:, i, 0:TB], in1=rs[:, 0:TB], op=ALU.mult),
                     reads=["ropet", "rstd"], writes=["ropet"])
            if self.next_x is not None:
                self.load_x_dma(*self.next_x)
                self.next_x = None
        ropesrc = self.rope_o if own else self.rope_h
        S.dma("sp", lambda e: e.dma_start(out=self.ropet[:, :, 0:TB], in_=ropesrc[:, :, t0:t0 + TB].rearrange("c p t -> p c t")),
              writes=["ropet"], sem="rope")
        kcol = (T_HALO + t0) if own else t0
        acol = 128 + t0

        def finish(val, vk, dest, emit_out, ocol):
            if dest is not None:
                S.op("act", lambda e: e.activation(out=dest, in_=val[:, 0:TB], func=AF.Copy), reads=[vk], writes=["qkv"])
            if emit_out:
                nt = TB // 128
                ps, pk = self.nps()

                def tr(e):
                    ins = None
                    for t in range(nt):
                        ins = e.transpose(out=ps[:, t * 128:(t + 1) * 128], in_=val[:, t * 128:(t + 1) * 128], identity=self.ident[:])
                    return ins
                S.op("pe", tr, reads=[vk, "ident"], writes=[pk])
                i = self.uid % 2
                self.uid += 1
                st, sk = self.stg[i], "stg%d" % i
                S.op("dve", lambda e: e.tensor_copy(out=st[:, 0:TB], in_=ps[:, 0:TB]), reads=[pk], writes=[sk])
                S.dma("pool", lambda e: e.dma_start(out=self.o_kv[t0:t0 + TB, ocol:ocol + 128].rearrange("(n p) c -> p n c", p=128),
                                                  in_=st[:, 0:TB].rearrange("p (n c) -> p n c", c=128)),
                      reads=[sk], sem="okv%d" % i)

        pend = {}

        def epi_factory(dests):
            def epi(j, ps, pk):
                kind, dest, emit_out, ocol = dests[j]
                if kind == "z":
                    t1, k1 = self.ntmp()
                    S.op("dve", lambda e: e.tensor_tensor(out=t1[:, 0:TB], in0=ps[:, 0:TB], in1=self.ropet[:, 0, 0:TB], op=ALU.mult),
                         reads=[pk, "ropet"], writes=[k1])
                    pend["z"] = (t1, k1, dest, emit_out, ocol)
                elif kind == "s":
                    t1, k1, dest, emit_out, ocol = pend.pop("z")
                    t2, k2 = self.ntmp()
                    S.op("dve", lambda e: e.tensor_tensor(out=t2[:, 0:TB], in0=ps[:, 0:TB], in1=self.ropet[:, 1, 0:TB], op=ALU.mult),
                         reads=[pk, "ropet"], writes=[k2])
                    S.op("dve", lambda e: e.tensor_tensor(out=t1[:, 0:TB], in0=t1[:, 0:TB], in1=t2[:, 0:TB], op=ALU.add),
                         reads=[k1, k2], writes=[k1])
                    finish(t1, k1, dest, emit_out, ocol)
                else:
                    t1, k1 = self.ntmp()
                    S.op("dve", lambda e: e.tensor_tensor(out=t1[:, 0:TB], in0=ps[:, 0:TB], in1=rs[:, 0:TB], op=ALU.mult),
                         reads=[pk, "rstd"], writes=[k1])
                    finish(t1, k1, dest, emit_out, ocol)
            return epi

        if own:
            d = []
            for c in range(4):
                d += [("z", self.QA[:, c, t0:t0 + TB], False, 0), ("s", None, False, 0)]
            for c in range(6):
                d += [("z", self.QB[:, c, t0:t0 + TB], False, 0), ("s", None, False, 0)]
            self.proj("q", 8, self.uT, ["uT"], TB, 0, 2560, epi_factory(d), after_first=hook)
        d = [("z", self.KB[:, 0, kcol:kcol + TB], own, 128), ("s", None, False, 0),
             ("z", self.KB[:, 1, kcol:kcol + TB], own, 256), ("s", None, False, 0),
             ("v", self.VBT[:, 0, kcol:kcol + TB], own, 512), ("v", self.VBT[:, 1, kcol:kcol + TB], own, 640)]
        ncols = 768
        if full_kv:
            if own:
                d += [("v", self.VAT[:, acol:acol + TB], True, 384), ("z", self.KA[:, acol:acol + TB], True, 0), ("s", None, False, 0)]
            else:
                d += [("v", None, False, 0), ("z", None, False, 0), ("s", None, False, 0)]
            ncols = 1152
        if full_kv and not own:
            base = epi_factory(d)

            def epi2(j, ps, pk):
                if j < 6:
                    return base(j, ps, pk)
                lo = TB - 128
                if j == 6:
                    S.op("dve", lambda e: e.tensor_tensor(out=self.VAT[:, 0:128], in0=ps[:, lo:TB], in1=rs[:, lo:TB], op=ALU.mult),
                         reads=[pk, "rstd"], writes=["qkv"])
                elif j == 7:
                    t1, k1 = self.ntmp()
                    S.op("dve", lambda e: e.tensor_tensor(out=t1[:, 0:TB], in0=ps[:, 0:TB], in1=self.ropet[:, 0, 0:TB], op=ALU.mult),
                         reads=[pk, "ropet"], writes=[k1])
                    pend["z"] = (t1, k1)
                else:
                    t1, k1 = pend.pop("z")
                    t2, k2 = self.ntmp()
                    S.op("dve", lambda e: e.tensor_tensor(out=t2[:, 0:TB], in0=ps[:, 0:TB], in1=self.ropet[:, 1, 0:TB], op=ALU.mult),
                         reads=[pk, "ropet"], writes=[k2])
                    S.op("dve", lambda e: e.tensor_tensor(out=self.KA[:, 0:128], in0=t1[:, lo:TB], in1=t2[:, lo:TB], op=ALU.add),
                         reads=[k1, k2], writes=["qkv"])
            self.proj("kv", 8, self.uT, ["uT"], TB, 0, ncols, epi2, after_first=hook, after_group0=self.mid_hook)
        else:
            self.proj("kv", 8, self.uT, ["uT"], TB, 0, ncols, epi_factory(d), after_first=(None if own else hook), after_group0=self.mid_hook)
        self.mid_hook = None

    def phase1(self):
        S = self.S
        blocks = [(self.xh, t0, TB) for (t0, TB) in HALO_BLOCKS] + [(self.xo, t0, TB) for (t0, TB) in OWN_BLOCKS]
        self.load_x_dma(*blocks[0])
        self.load_x(*blocks[0], "b0", dma=False)

        def mk_mid(nb):
            return (lambda: self.load_x(*nb, "nx", dma=False)) if nb is not None else None
        for bi, (t0, TB) in enumerate(HALO_BLOCKS):
            self.half_ok = TB <= 512
            self.next_x = blocks[bi + 1]
            self.mid_hook = mk_mid(blocks[bi + 1])
            self.ffn(TB, "g1", "u1", "d1", self.norm_begin(TB, 0))
            self.qkv(TB, False, t0, bi == len(HALO_BLOCKS) - 1, self.norm_begin(TB, 1))
        for bi, (t0, TB) in enumerate(OWN_BLOCKS):
            self.half_ok = TB <= 512
            self.next_x = blocks[4 + bi + 1] if bi + 1 < len(OWN_BLOCKS) else None
            self.mid_hook = mk_mid(self.next_x)
            self.ffn(TB, "g1", "u1", "d1", self.norm_begin(TB, 0))
            S.dma("sp", lambda e, t0=t0, TB=TB: e.dma_start(out=self.hs[:, :, t0:t0 + TB], in_=self.xT[:, :, 0:TB]),
                  reads=["xT"], writes=["hs"], sem="hs")
            self.qkv(TB, True, t0, True, self.norm_begin(TB, 1))

    def mem_kv(self):
        S = self.S
        TB = 256
        self.load_x(self.xm, 0, TB, "m")
        self.rmsnorm(TB, 3)
        for wi, wn in enumerate(("mk", "mv")):
            def epi(j, ps, pk, wi=wi):
                t1, k1 = self.ntmp()
                S.op("act", lambda e: e.activation(out=t1[:, 0:TB], in_=ps[:, 0:TB], func=AF.Copy), reads=[pk], writes=[k1])
                ps2, pk2 = self.nps()

                def tr(e):
                    ins = None
                    for t in range(2):
                        ins = e.transpose(out=ps2[:, t * 128:(t + 1) * 128], in_=t1[:, t * 128:(t + 1) * 128], identity=self.ident[:])
                    return ins
                S.op("pe", tr, reads=[k1, "ident"], writes=[pk2])
                i = self.uid % 2
                self.uid += 1
                st, sk = self.stg[i], "stg%d" % i
                S.op("dve", lambda e: e.tensor_copy(out=st[:, 0:TB], in_=ps2[:, 0:TB]), reads=[pk2], writes=[sk])
                oc = wi * 512 + j * 128
                S.dma("pool", lambda e: e.dma_start(out=self.o_mem[:, oc:oc + 128].rearrange("(n p) c -> p n c", p=128),
                                                  in_=st[:, 0:TB].rearrange("p (n c) -> p n c", c=128)),
                      reads=[sk], sem="okv%d" % i)
            self.proj(wn, 8, self.uT, ["uT"], TB, 0, 512, epi)


    def psb(self, ps, i):
        return ps[:, i * 64:(i + 1) * 64].bitcast(BF16)

    def carve2(self):
        o = 0
        A = self.arena

        def carve(n, dtype=BF16):
            nonlocal o
            v = A[:, o:o + n]
            o += (n + 15) // 16 * 16
            return v.bitcast(F32) if dtype == F32 else v
        self.PT = [carve(1024) for _ in range(2)]
        AS = 68
        self.VAaug = [carve(2 * AS).rearrange("p (h d) -> p h d", h=2)[:, :, 0:65] for _ in range(4)]
        self.NVB = 10
        self.VBaug = [carve(4 * AS).rearrange("p (h d) -> p h d", h=4)[:, :, 0:65] for _ in range(self.NVB)]
        self.ot = [carve(1024, F32) for _ in range(2)]
        self.rr = carve(4096, F32)
        self.maskS = carve(408)
        self.o2_fixed = o
        self.accB = carve(4 * 2048 * 2, F32).rearrange("p (k t) -> p k t", k=4)
        self.craw_a = carve(2048).rearrange("p (s f) -> p s f", s=16)
        self.CKA = carve(2048).rearrange("p (s f) -> p s f", s=16)
        self.CVAaug = carve(16 * 2 * AS).rearrange("p (s h d) -> p s h d", s=16, h=2)[:, :, :, 0:65]
        self.VnAaug = carve(16 * 2 * AS).rearrange("p (s h d) -> p s h d", s=16, h=2)[:, :, :, 0:65]
        assert o <= 38400, o
        o = self.o2_fixed
        self.rawK = [carve(4096).rearrange("p (n f) -> p n f", n=16) for _ in range(2)]
        self.rawV = carve(4096).rearrange("p (n f) -> p n f", n=16)
        self.CKB = carve(4096).rearrange("p (c t) -> p c t", c=2)
        self.CVBaug = carve(16 * 4 * AS).rearrange("p (n h d) -> p n h d", n=16, h=4)[:, :, :, 0:65]
        self.VnBaug = carve(16 * 4 * AS).rearrange("p (s h d) -> p s h d", s=16, h=4)[:, :, :, 0:65]
        self.PTs = [carve(408) for _ in range(2)]
        self.Psum_s = [carve(136).rearrange("p (n t) -> p n t", t=8) for _ in range(2)]
        assert o <= 38400, o
        ow = 38400
        self.OA = A[:, ow:ow + 4 * T_OWN].rearrange("p (c t) -> p c t", c=4)
        self.OB = A[:, ow + 4 * T_OWN:ow + 6 * T_OWN].rearrange("p (c t) -> p c t", c=2)

    def attn_A(self):
        S = self.S
        vslot = {}
        vcount = [0]

        def vtileA(kt):
            if kt in vslot:
                return vslot[kt]
            i = vcount[0] % 4
            vcount[0] += 1
            for k in [k for k, v in vslot.items() if v[2] == i]:
                del vslot[k]
            va, key = self.VAaug[i], "VAaug%d" % i
            ps, pk = self.nps()
            col = 128 + kt * 128
            S.op("pe", lambda e: e.transpose(out=self.psb(ps, 0), in_=self.VAT[:, col:col + 128], identity=self.identb[:]),
                 reads=["identb"], writes=[pk])
            S.op("dve", lambda e: e.tensor_copy(out=va[:, :, 0:64], in_=self.psb(ps, 0).rearrange("p (h d) -> p h d", h=2)),
                 reads=[pk], writes=[key])
            vslot[kt] = (va, key, i)
            return vslot[kt]
        for i in range(4):
            S.op("dve", lambda e, i=i: e.memset(self.VAaug[i][:, :, 64:65], 1.0), writes=["VAaug%d" % i])
        pending = None
        pend3 = [None]
        rri = [0]

        def stage2(pd):
            PT, ptk, vc, vp, g, hp, q0 = pd
            po, pok = self.nps()

            def pv(e):
                e.matmul(po[0:65, 0:512], lhsT=vc[0][:, g, :], rhs=PT[:, 0:512], start=True, stop=False)
                return e.matmul(po[0:65, 0:512], lhsT=vp[0][:, g, :], rhs=PT[:, 512:1024], start=False, stop=True)
            S.op("pe", pv, reads=[ptk, vc[1], vp[1]], writes=[pok])
            oi = rri[0] % 2
            rri[0] += 1
            ot, otk = self.ot[oi], "ot%d" % oi
            rr, rrk = self.rr[:, oi * 512:(oi + 1) * 512], "rrA%d" % oi
            S.op("act", lambda e: e.activation(out=ot[0:65, 0:512], in_=po[0:65, 0:512], func=AF.Copy), reads=[pok], writes=[otk])
            S.op("dve", lambda e: e.tensor_tensor(out=rr[64:65, :], in0=ot[64:65, 0:512], in1=self.esink[64:65, g, 0:512], op=ALU.add),
                 reads=[otk, "esink"], writes=[rrk])
            S.op("act", lambda e: e.activation(out=rr[64:65, :], in_=rr[64:65, :], func=AF.Ln), reads=[rrk], writes=[rrk])
            S.op("act", lambda e: e.activation(out=rr[64:65, :], in_=rr[64:65, :], func=AF.Exp, scale=-1.0), reads=[rrk], writes=[rrk])
            if pend3[0] is not None:
                pend3[0]()

            def st3():
                ps, pk = self.nps()
                S.op("pe", lambda e: e.matmul(ps[0:64, 0:512], lhsT=self.onesf[64:65, 0:64], rhs=rr[64:65, :], start=True, stop=True),
                     reads=[rrk, "onesf"], writes=[pk], drain=True)
                S.op("dve", lambda e: e.tensor_tensor(out=self.OA[hp, 0:4, q0:q0 + 128], in0=ot[0:64, 0:512].rearrange("p (h q) -> p h q", h=4),
                                                     in1=ps[0:64, 0:512].rearrange("p (h q) -> p h q", h=4), op=ALU.mult),
                     reads=[pk, otk], writes=["OAB"])
            pend3[0] = st3

        for qt in range(16):
            vc = vtileA(qt)
            vp = vtileA(qt - 1)
            for g in range(2):
                hp = slice(g * 64, (g + 1) * 64)
                ps, pk = self.nps()
                q0 = qt * 128

                def sc(e, ps=ps, hp=hp, q0=q0, qt=qt):
                    r4 = lambda ap: ap.rearrange("p (h q) -> p h q", h=4)
                    e.matmul(r4(ps[:, 0:512]), lhsT=self.KA[hp, 128 + q0:128 + q0 + 128], rhs=self.QA[hp, 0:4, q0:q0 + 128], start=True, stop=True)
                    return e.matmul(r4(ps[:, 512:1024]), lhsT=self.KA[hp, q0:q0 + 128], rhs=self.QA[hp, 0:4, q0:q0 + 128], start=True, stop=True)
                S.op("pe", sc, writes=[pk], drain=True)
                pi = self.PTi % 2
                self.PTi += 1
                PT, ptk = self.PT[pi], "PT%d" % pi
                S.op("act", lambda e, ps=ps, PT=PT: e.activation(out=PT[:, :], in_=ps[:, :], func=AF.Exp, scale=ATTN_SCALE), reads=[pk], writes=[ptk])
                m0 = 2 if qt == 0 else 0
                S.op("dve", lambda e, PT=PT, m0=m0: e.tensor_tensor(
                    out=PT[:, :].rearrange("p (m h q) -> p m h q", m=2, h=4), in0=PT[:, :].rearrange("p (m h q) -> p m h q", m=2, h=4),
                    in1=self.masks[:, m0:m0 + 2, :].unsqueeze(2).to_broadcast([128, 2, 4, 128]), op=ALU.mult),
                    reads=[ptk, "masks"], writes=[ptk])
                if pending is not None:
                    stage2(pending)
                pending = (PT, ptk, vc, vp, g, hp, q0)
        stage2(pending)
        pend3[0]()

    def _normA(self, ot, otk, g, hp, q0):
        S = self.S
        rr = self.rr
        n = 512
        S.op("dve", lambda e: e.tensor_tensor(out=rr[64:65, 0:n], in0=ot[64:65, 0:n], in1=self.esink[64:65, g, 0:n], op=ALU.add),
             reads=[otk, "esink"], writes=["rr", "rrA0", "rrA1"])
        S.op("dve", lambda e: e.reciprocal(out=rr[64:65, 0:n], in_=rr[64:65, 0:n]), reads=["rr"], writes=["rr", "rrA0", "rrA1"])
        ps, pk = self.nps()
        S.op("pe", lambda e: e.matmul(ps[0:64, 0:n], lhsT=self.onesf[64:65, 0:64], rhs=rr[64:65, 0:n], start=True, stop=True),
             reads=["rr", "onesf"], writes=[pk], drain=True)
        S.op("dve", lambda e: e.tensor_tensor(out=self.OA[hp, 0:4, q0:q0 + 128], in0=ot[0:64, 0:n].rearrange("p (h q) -> p h q", h=4),
                                             in1=ps[0:64, 0:n].rearrange("p (h q) -> p h q", h=4), op=ALU.mult),
             reads=[pk, otk], writes=["OAB"])

    def _normA_s(self, ot, otk, g, hp, c0):
        S = self.S
        rr = self.rr
        n = 512
        v4 = lambda ap: ap.rearrange("p (s h t) -> p s h t", s=16, h=4)
        S.op("dve", lambda e: e.tensor_tensor(out=v4(rr[64:65, 0:n]), in0=v4(ot[64:65, 0:n]),
                                             in1=self.esink[64:65, g, :].rearrange("p (h q) -> p h q", h=4)[:, :, 0:8].unsqueeze(1).to_broadcast([1, 16, 4, 8]),
                                             op=ALU.add),
             reads=[otk, "esink"], writes=["rr", "rrA0", "rrA1"])
        S.op("act", lambda e: e.activation(out=rr[64:65, 0:n], in_=rr[64:65, 0:n], func=AF.Ln), reads=["rr"], writes=["rr", "rrA0", "rrA1"])
        S.op("act", lambda e: e.activation(out=rr[64:65, 0:n], in_=rr[64:65, 0:n], func=AF.Exp, scale=-1.0), reads=["rr"], writes=["rr", "rrA0", "rrA1"])
        ps, pk = self.nps()
        S.op("pe", lambda e: e.matmul(ps[0:64, 0:n], lhsT=self.onesf[64:65, 0:64], rhs=rr[64:65, 0:n], start=True, stop=True),
             reads=["rr", "onesf"], writes=[pk], drain=True)
        p4 = lambda ap: ap.rearrange("p (s h t) -> p h s t", s=16, h=4)
        S.op("dve", lambda e: e.tensor_tensor(out=self.OA[hp, 0:4, c0:c0 + 128].rearrange("p h (s t) -> p h s t", s=16),
                                             in0=p4(ot[0:64, 0:n]), in1=p4(ps[0:64, 0:n]), op=ALU.mult),
             reads=[pk, otk], writes=["OAB"])

    def attn_A_sample(self):
        S = self.S
        c0 = 2048
        S.dma("pool", lambda e: e.dma_start(out=self.craw_a, in_=self.cswk.rearrange("s p f -> p s f")), writes=["craw_a"], sem="ca0")
        S.op("dve", lambda e: e.memset(self.CVAaug[:, :, :, 64:65], 1.0), writes=["CVAaug"])
        S.op("dve", lambda e: e.memset(self.VnAaug[:, :, :, 64:65], 1.0), writes=["VnAaug"])
        for h in range(2):
            S.dma("pool", lambda e, h=h: e.dma_start(out=self.CVAaug[:, :, h, 0:64], in_=self.cswv[:, :, h * 64:(h + 1) * 64].rearrange("s p d -> p s d")),
                  writes=["CVAaug"], sem="ca1")
        if P2S < 2:
            return
        ps, pk = self.nps()

        def tr(e):
            ins = None
            for s_ in range(16):
                ins = e.transpose(out=self.psb(ps, s_), in_=self.craw_a[:, s_, :], identity=self.identb[:])
            return ins
        for q4 in range(4):
            psq, pkq = self.nps()

            def trq(e, psq=psq, q4=q4):
                ins = None
                for i in range(4):
                    ins = e.transpose(out=self.psb(psq, i), in_=self.craw_a[:, q4 * 4 + i, :], identity=self.identb[:])
                return ins
            S.op("pe", trq, reads=["craw_a", "identb"], writes=[pkq])
            S.op("dve", lambda e, psq=psq, q4=q4: e.tensor_copy(out=self.CKA[:, q4 * 4:q4 * 4 + 4, :],
                                                              in_=psq[:, 0:256].bitcast(BF16).rearrange("p (s f) -> p s f", s=4)),
                 reads=[pkq], writes=["CKA"])
        if P2S < 3:
            return
        for h8 in range(2):
            ps2, pk2 = self.nps()

            def tr2(e, ps2=ps2, h8=h8):
                ins = None
                for i in range(8):
                    s_ = h8 * 8 + i
                    ins = e.transpose(out=self.psb(ps2, i)[0:8, :], in_=self.VAT[:, 128 + c0 + s_ * 8:128 + c0 + s_ * 8 + 8], identity=self.identb[:])
                return ins
            S.op("pe", tr2, reads=["identb"], writes=[pk2])
            S.op("dve", lambda e, ps2=ps2, h8=h8: e.tensor_copy(out=self.VnAaug[0:8, h8 * 8:h8 * 8 + 8, :, 0:64],
                                                               in_=ps2[0:8, 0:512].bitcast(BF16).rearrange("p (s h d) -> p s h d", s=8, h=2)),
                 reads=[pk2], writes=["VnAaug"])
        if P2S < 4:
            return
        for g in range(2):
            hp = slice(g * 64, (g + 1) * 64)
            ps, pk = self.nps()

            def sc(e, ps=ps, hp=hp):
                ins = None
                for s_ in range(16):
                    q = self.QA[hp, 0:4, c0 + s_ * 8:c0 + s_ * 8 + 8]
                    r4 = lambda ap: ap.rearrange("p (h q) -> p h q", h=4)
                    lh = self.CKA[hp, s_, :] if P2Y == 0 else self.KA[hp, 0:128]
                    if P2Y in (4, 5):
                        if s_ < 2:
                            ins = e.matmul(ps[:, s_ * 512:(s_ + 1) * 512].rearrange("p (h q) -> p h q", h=4), lhsT=self.KA[hp, 128:256],
                                           rhs=self.QA[hp, 0:4, 0:128], start=True, stop=True)
                        continue
                    if P2Y == 2:
                        q = self.QA[hp, 0, 0:32]
                        ins = e.matmul(ps[:, s_ * 32:(s_ + 1) * 32], lhsT=lh, rhs=q, start=True, stop=True)
                    else:
                        ins = e.matmul(r4(ps[:, s_ * 32:(s_ + 1) * 32]), lhsT=lh, rhs=q, start=True, stop=True)
                    if P2X >= 1:
                        ins = e.matmul(r4(ps[0:8, 512 + s_ * 32:512 + (s_ + 1) * 32]), lhsT=self.KA[hp, 128 + c0 + s_ * 8:128 + c0 + s_ * 8 + 8], rhs=q,
                                       start=True, stop=True)
                return ins
            S.op("pe", sc, reads=["CKA"], writes=[pk], drain=True)
            pi = self.uid % 2
            self.uid += 1
            PT, ptk = self.PT[pi], "PT%d" % pi
            if P2X < 2:
                continue
            S.op("act", lambda e, ps=ps, PT=PT: e.activation(out=PT[:, 0:512], in_=ps[:, 0:512], func=AF.Exp, scale=ATTN_SCALE), reads=[pk], writes=[ptk])
            S.op("act", lambda e, ps=ps, PT=PT: e.activation(out=PT[0:8, 512:1024], in_=ps[0:8, 512:1024], func=AF.Exp, scale=ATTN_SCALE),
                 reads=[pk], writes=[ptk])
            S.op("dve", lambda e, PT=PT: e.tensor_tensor(out=PT[:, 0:512].rearrange("p (a t) -> p a t", t=8),
                                                         in0=PT[:, 0:512].rearrange("p (a t) -> p a t", t=8),
                                                         in1=self.masks[:, 4, 0:8].unsqueeze(1).to_broadcast([128, 64, 8]), op=ALU.mult),
                 reads=[ptk, "masks"], writes=[ptk])
            S.op("dve", lambda e, PT=PT: e.tensor_tensor(out=PT[0:8, 512:1024].rearrange("p (a t) -> p a t", t=8),
                                                         in0=PT[0:8, 512:1024].rearrange("p (a t) -> p a t", t=8),
                                                         in1=self.masks[0:8, 5, 0:8].unsqueeze(1).to_broadcast([8, 64, 8]), op=ALU.mult),
                 reads=[ptk, "masks"], writes=[ptk])
            if P2S < 5:
                continue
            po, pok = self.nps()

            def pv(e, po=po, PT=PT, g=g):
                ins = None
                for s_ in range(16):
                    o = po[0:65, s_ * 32:(s_ + 1) * 32]
                    e.matmul(o, lhsT=self.CVAaug[:, s_, g, :], rhs=PT[:, s_ * 32:(s_ + 1) * 32], start=True, stop=False)
                    ins = e.matmul(o, lhsT=self.VnAaug[0:8, s_, g, :], rhs=PT[0:8, 512 + s_ * 32:512 + (s_ + 1) * 32], start=False, stop=True)
                return ins
            S.op("pe", pv, reads=[ptk, "CVAaug", "VnAaug"], writes=[pok])
            oi = self.uid % 2
            self.uid += 1
            ot, otk = self.ot[oi], "ot%d" % oi
            S.op("act", lambda e, po=po, ot=ot: e.activation(out=ot[0:65, 0:512], in_=po[0:65, 0:512], func=AF.Copy), reads=[pok], writes=[otk])
            self._normA_s(ot, otk, g, hp, c0)

    def attn_B(self):
        S = self.S
        vslot = {}
        vcount = [0]
        for i in range(self.NVB):
            S.op("dve", lambda e, i=i: e.memset(self.VBaug[i][:, :, 64:65], 1.0), writes=["VBaug%d" % i])

        def cols(start, step):
            c = T_HALO + start
            return slice(c, c + 127 * step + 1, step)

        def vtileB(start, step, protect):
            kk = (start, step)
            if kk in vslot:
                return vslot[kk]
            while True:
                i = vcount[0] % self.NVB
                vcount[0] += 1
                owner = [k for k, v in vslot.items() if v[2] == i]
                if owner and owner[0] in protect:
                    continue
                for k in owner:
                    del vslot[k]
                break
            va, key = self.VBaug[i], "VBaug%d" % i
            ps, pk = self.nps()
            cs = cols(start, step)

            def tr(e):
                e.transpose(out=self.psb(ps, 0), in_=self.VBT[:, 0, cs], identity=self.identb[:])
                return e.transpose(out=self.psb(ps, 1), in_=self.VBT[:, 1, cs], identity=self.identb[:])
            S.op("pe", tr, reads=["identb"], writes=[pk])
            S.op("dve", lambda e: e.tensor_copy(out=va[:, :, 0:64], in_=ps[:, 0:128].bitcast(BF16).rearrange("p (h d) -> p h d", h=4)),
                 reads=[pk], writes=[key])
            vslot[kk] = (va, key, i, kk)
            return vslot[kk]

        pending = [None]

        def stage2(pd, half):
            PT, ptk, vc, vp, ti, ctx = pd
            if ctx["po"] is None:
                ctx["po"] = (self.nps(hold=True), self.nps(hold=True))
            (po0, pok0), (po1, pok1) = ctx["po"]

            def pv(e, ks):
                ins = None
                for k in ks:
                    po = (po0, po1)[k // 2]
                    o = po[0:65, (k % 2) * 512 + ti * 128:(k % 2) * 512 + ti * 128 + 128]
                    e.matmul(o, lhsT=vc[0][:, k, :], rhs=PT[:, k * 256:k * 256 + 128], start=True, stop=False)
                    ins = e.matmul(o, lhsT=vp[0][:, k, :], rhs=PT[:, k * 256 + 128:k * 256 + 256], start=False, stop=True)
                return ins
            if half == 0:
                S.op("pe", lambda e: pv(e, (0, 1)), reads=[ptk, vc[1], vp[1]], writes=[pok0])
                return
            S.op("pe", lambda e: pv(e, (2, 3)), reads=[ptk, vc[1], vp[1]], writes=[pok1])
            if ti == 3:
                self.unhold(pok0)
                self.unhold(pok1)
                g, d, accf = ctx["g"], ctx["d"], ctx["accf"]
                for k in range(4):
                    po, pok = ((po0, pok0), (po1, pok1))[k // 2]
                    src = po[0:65, (k % 2) * 512:(k % 2) * 512 + 512]
                    dst = accf(k)
                    if d == 16:
                        src = src.rearrange("p (r j) -> p r j", r=4)
                    if g == 0:
                        S.op("act", lambda e, src=src, dst=dst: e.activation(out=dst, in_=src, func=AF.Copy), reads=[pok], writes=["accB"])
                    else:
                        S.op("dve", lambda e, src=src, dst=dst: e.tensor_tensor(out=dst, in0=src, in1=dst, op=ALU.add), reads=[pok, "accB"], writes=["accB"])

        for g, d in enumerate((1, 4, 16)):
            if d == 1:
                batches = [[(128 * (b0 + i), 1) for i in range(4)] for b0 in range(0, 16, 4)]
                accv = [lambda k, b=b: self.accB[0:65, k, b * 512:(b + 1) * 512] for b in range(4)]
            elif d == 4:
                batches = [[(r + 512 * b, 4) for b in range(4)] for r in range(4)]
                accv = [lambda k, r=r: self.accB[0:65, k, r:r + 2045:4] for r in range(4)]
            else:
                batches = [[(r0 + i, 16) for i in range(4)] for r0 in range(0, 16, 4)]
                accv = [lambda k, r0=r0: self.accB[0:65, k, :].rearrange("p (j r) -> p r j", r=16)[:, r0:r0 + 4, :] for r0 in range(0, 16, 4)]
            for bi, batch in enumerate(batches):
                need = [(st, sp) for (st, sp) in batch] + [(st - 128 * d, sp) for (st, sp) in batch]
                ctx = {"po": None, "g": g, "d": d, "accf": accv[bi]}
                for ti, (st, sp) in enumerate(batch):
                    prot = list(need)
                    if pending[0] is not None:
                        prot += [pending[0][2][3], pending[0][3][3]]
                    vc = vtileB(st, sp, prot)
                    vp = vtileB(st - 128 * d, sp, prot)
                    ps, pk = self.nps()
                    cq, cp = cols(st, sp), cols(st - 128 * d, sp)

                    def sc(e, ks, ps=ps, cq=cq, cp=cp, g=g):
                        ins = None
                        for k in ks:
                            hp = slice((k % 2) * 64, (k % 2) * 64 + 64)
                            q = self.QB[hp, g * 2 + k // 2, cq.start - T_HALO:cq.stop - T_HALO:cq.step]
                            e.matmul(ps[:, k * 256:k * 256 + 128], lhsT=self.KB[hp, k // 2, cq], rhs=q, start=True, stop=True)
                            ins = e.matmul(ps[:, k * 256 + 128:k * 256 + 256], lhsT=self.KB[hp, k // 2, cp], rhs=q, start=True, stop=True)
                        return ins
                    S.op("pe", lambda e, sc=sc: sc(e, (0, 2)), writes=[pk], drain=(pending[0] is None))
                    if pending[0] is not None:
                        stage2(pending[0], 0)
                    S.op("pe", lambda e, sc=sc: sc(e, (1, 3)), writes=[pk], drain=(pending[0] is None))
                    if pending[0] is not None:
                        stage2(pending[0], 1)
                    pi = self.PTi % 2
                    self.PTi += 1
                    PT, ptk = self.PT[pi], "PT%d" % pi
                    S.op("act", lambda e, ps=ps, PT=PT: e.activation(out=PT[:, :], in_=ps[:, :], func=AF.Exp, scale=ATTN_SCALE), reads=[pk], writes=[ptk])
                    m0 = 2 if st - 128 * d < 0 else 0
                    S.op("dve", lambda e, PT=PT, m0=m0: e.tensor_tensor(
                        out=PT[:, :].rearrange("p (k m q) -> p k m q", k=4, m=2), in0=PT[:, :].rearrange("p (k m q) -> p k m q", k=4, m=2),
                        in1=self.masks[:, m0:m0 + 2, :].unsqueeze(1).to_broadcast([128, 4, 2, 128]), op=ALU.mult),
                        reads=[ptk, "masks"], writes=[ptk])
                    pending[0] = (PT, ptk, vc, vp, ti, ctx)
        stage2(pending[0], 0)
        stage2(pending[0], 1)
        for k in range(4):
            S.op("act", lambda e, k=k: e.activation(out=self.accB[64:65, k, :], in_=self.accB[64:65, k, :], func=AF.Ln), reads=["accB"], writes=["accB"])
            S.op("act", lambda e, k=k: e.activation(out=self.accB[64:65, k, :], in_=self.accB[64:65, k, :], func=AF.Exp, scale=-1.0),
                 reads=["accB"], writes=["accB"])
        for k in range(4):
            hp = slice((k % 2) * 64, (k % 2) * 64 + 64)
            acc = self.accB[:, k, :]
            for (a, m) in nslices(2048):
                ps, pk = self.nps()
                S.op("pe", lambda e, ps=ps, a=a, m=m, acc=acc: e.matmul(ps[0:64, 0:m], lhsT=self.onesf[64:65, 0:64], rhs=acc[64:65, a:a + m],
                                                                      start=True, stop=True),
                     reads=["accB", "onesf"], writes=[pk], drain=True)
                S.op("dve", lambda e, ps=ps, a=a, m=m, acc=acc, hp=hp, k=k: e.tensor_tensor(out=self.OB[hp, k // 2, a:a + m], in0=acc[0:64, a:a + m],
                                                                                          in1=ps[0:64, 0:m], op=ALU.mult),
                     reads=[pk, "accB"], writes=["OAB"])

    def _normB(self, acc, acck, hp, chunk, c0, n):
        S = self.S
        rr = self.rr
        S.op("act", lambda e: e.activation(out=rr[64:65, 0:n], in_=acc[64:65, 0:n], func=AF.Ln), reads=[acck], writes=["rr", "rrA0", "rrA1"])
        S.op("act", lambda e: e.activation(out=rr[64:65, 0:n], in_=rr[64:65, 0:n], func=AF.Exp, scale=-1.0), reads=["rr"], writes=["rr", "rrA0", "rrA1"])
        for (a, m) in nslices(n):
            ps, pk = self.nps()
            S.op("pe", lambda e, ps=ps, a=a, m=m: e.matmul(ps[0:64, 0:m], lhsT=self.onesf[64:65, 0:64], rhs=rr[64:65, a:a + m], start=True, stop=True),
                 reads=["rr", "onesf"], writes=[pk], drain=True)
            S.op("dve", lambda e, ps=ps, a=a, m=m: e.tensor_tensor(out=self.OB[hp, chunk, c0 + a:c0 + a + m], in0=acc[0:64, a:a + m],
                                                                 in1=ps[0:64, 0:m], op=ALU.mult),
                 reads=[pk, acck], writes=["OAB"])

    def attn_B_sample(self):
        S = self.S
        c0 = 2048
        kc0 = T_HALO + c0
        S.dma("sp", lambda e: e.dma_start(out=self.ot[1][:, 0:408], in_=self.maskS_d[:, :]), writes=["ot1"], sem="ms")
        S.op("dve", lambda e: e.tensor_copy(out=self.maskS[:, 0:408], in_=self.ot[1][:, 0:408]), reads=["ot1"], writes=["maskS"])
        S.op("dve", lambda e: e.memset(self.CVBaug[:, :, :, 64:65], 1.0), writes=["CVBaug"])
        S.op("dve", lambda e: e.memset(self.VnBaug[:, :, :, 64:65], 1.0), writes=["VnBaug"])
        for c in range(2):
            for h8 in range(2):
                ps2, pk2 = self.nps()

                def tr2(e, ps2=ps2, c=c, h8=h8):
                    ins = None
                    for i in range(8):
                        s_ = h8 * 8 + i
                        ins = e.transpose(out=self.psb(ps2, i)[0:8, :], in_=self.VBT[:, c, kc0 + s_ * 8:kc0 + s_ * 8 + 8], identity=self.identb[:])
                    return ins
                S.op("pe", tr2, reads=["identb"], writes=[pk2])
                S.op("dve", lambda e, ps2=ps2, c=c, h8=h8: e.tensor_copy(out=self.VnBaug[0:8, h8 * 8:h8 * 8 + 8, 2 * c:2 * c + 2, 0:64],
                                                                        in_=ps2[0:8, 0:512].bitcast(BF16).rearrange("p (s h d) -> p s h d", s=8, h=2)),
                     reads=[pk2], writes=["VnBaug"])
        for i in range(2):
            S.op("dve", lambda e, i=i: e.memset(self.PTs[i][:, 384:408], 0.0), writes=["PTs%d" % i])
        pso, psok = self.nps(hold=True)
        pend = [None]
        for s_ in range(16):
            rk, rkk = self.rawK[s_ % 2], "rawK%d" % (s_ % 2)
            S.dma("pool", lambda e, rk=rk, s_=s_: e.dma_start(out=rk, in_=self.cdk[s_].rearrange("(n p) f -> p n f", p=128)), writes=[rkk], sem=rkk)
            S.dma("pool", lambda e, s_=s_: e.dma_start(out=self.rawV, in_=self.cdv[s_].rearrange("(n p) f -> p n f", p=128)), writes=["rawV"], sem="rawV")
            for c in range(2):
                for h8 in range(2):
                    ps, pk = self.nps()

                    def tr(e, ps=ps, c=c, rk=rk, h8=h8):
                        ins = None
                        for i in range(8):
                            ins = e.transpose(out=self.psb(ps, i), in_=rk[:, h8 * 8 + i, c * 128:(c + 1) * 128], identity=self.identb[:])
                        return ins
                    S.op("pe", tr, reads=[rkk, "identb"], writes=[pk])
                    eng = "act" if h8 == 0 else "dve"
                    if eng == "act":
                        S.op("act", lambda e, ps=ps, c=c, h8=h8: e.activation(out=self.CKB[:, c, h8 * 1024:(h8 + 1) * 1024], in_=ps[:, 0:512].bitcast(BF16),
                                                                             func=AF.Copy), reads=[pk], writes=["CKB"])
                    else:
                        S.op("dve", lambda e, ps=ps, c=c, h8=h8: e.tensor_copy(out=self.CKB[:, c, h8 * 1024:(h8 + 1) * 1024], in_=ps[:, 0:512].bitcast(BF16)),
                             reads=[pk], writes=["CKB"])
            if pend[0] is not None:
                pend[0]()
                pend[0] = None
            S.op("dve", lambda e: e.tensor_copy(out=self.CVBaug[:, :, :, 0:64], in_=self.rawV.rearrange("p n (h d) -> p n h d", h=4)),
                 reads=["rawV"], writes=["CVBaug"])
            for k in range(4):
                hp = slice((k % 2) * 64, (k % 2) * 64 + 64)
                ch = k // 2
                ps, pk = self.nps()

                def sc(e, ps=ps, hp=hp, ch=ch, s_=s_):
                    q = self.QB[hp, ch:6:2, c0 + s_ * 8:c0 + s_ * 8 + 8]
                    for n in range(16):
                        e.matmul(ps[:, n * 24:(n + 1) * 24].rearrange("p (g t) -> p g t", g=3), lhsT=self.CKB[hp, ch, n * 128:(n + 1) * 128], rhs=q,
                                 start=True, stop=True)
                    return e.matmul(ps[0:8, 384:408].rearrange("p (g t) -> p g t", g=3), lhsT=self.KB[hp, ch, kc0 + s_ * 8:kc0 + s_ * 8 + 8], rhs=q,
                                    start=True, stop=True)
                S.op("pe", sc, reads=["CKB"], writes=[pk])
                pi = self.PTi % 2
                self.PTi += 1
                PT, ptk = self.PTs[pi], "PTs%d" % pi
                Pq, pqk = self.Psum_s[pi], "Pq%d" % pi
                S.op("act", lambda e, ps=ps, PT=PT: e.activation(out=PT[:, 0:384], in_=ps[:, 0:384], func=AF.Exp, scale=ATTN_SCALE), reads=[pk], writes=[ptk])
                S.op("act", lambda e, ps=ps, PT=PT: e.activation(out=PT[0:8, 384:408], in_=ps[0:8, 384:408], func=AF.Exp, scale=ATTN_SCALE),
                     reads=[pk], writes=[ptk])
                S.op("dve", lambda e, PT=PT: e.tensor_tensor(out=PT[:, 0:408], in0=PT[:, 0:408], in1=self.maskS[:, 0:408], op=ALU.mult),
                     reads=[ptk, "maskS"], writes=[ptk])
                def gsum(e, PT=PT, Pq=Pq):
                    with self.nc.allow_low_precision("3-term sum feeding a bf16 matmul operand"):
                        return e.tensor_reduce(out=Pq[:, 0:17, :], in_=PT[:, 0:408].rearrange("p (n g t) -> p n t g", g=3, t=8),
                                               axis=mybir.AxisListType.X, op=ALU.add)
                S.op("dve", gsum, reads=[ptk], writes=[pqk])

                def pv(e, Pq=Pq, k=k, s_=s_):
                    o = pso[0:65, k * 128 + s_ * 8:k * 128 + s_ * 8 + 8]
                    for n in range(16):
                        e.matmul(o, lhsT=self.CVBaug[:, n, k, :], rhs=Pq[:, n, :], start=(n == 0), stop=False)
                    return e.matmul(o, lhsT=self.VnBaug[0:8, s_, k, :], rhs=Pq[0:8, 16, :], start=False, stop=True)
                if pend[0] is not None:
                    pend[0]()
                pend[0] = (lambda pv=pv, pqk=pqk: S.op("pe", pv, reads=[pqk, "CVBaug", "VnBaug"], writes=[psok]))
        pend[0]()
        self.unhold(psok)
        ot, otk = self.ot[0], "ot0"
        S.op("act", lambda e: e.activation(out=ot[0:65, 0:512], in_=pso[0:65, 0:512], func=AF.Copy), reads=[psok], writes=[otk])
        for k in range(4):
            hp = slice((k % 2) * 64, (k % 2) * 64 + 64)
            self._normB(ot[:, k * 128:(k + 1) * 128], otk, hp, k // 2, c0, 128)

    def phase2(self):
        self.half_ok = False
        self.carve2()
        if P2 >= 1:
            self.attn_A()
        if P2 >= 2 and not SKIP_AS:
            self.attn_A_sample()
        if P2 >= 3:
            self.attn_B()
        self.S.barrier()
        if P2 >= 4:
            self.attn_B_sample()


    def carve3(self):
        qb = self.QB[:, :, :].rearrange("p c t -> p (c t)")
        qa = self.QA[:, :, :].rearrange("p c t -> p (c t)")
        self.wslot = [qb[:, 0:self.WCAP], qb[:, self.WCAP:2 * self.WCAP], qa[:, 0:self.WCAP]]
        kb = self.KB[:, :, :].rearrange("p c t -> p (c t)")
        vb = self.VBT[:, :, :].rearrange("p c t -> p (c t)")
        self.MK = kb[:, 0:1024].rearrange("p (h t) -> p h t", h=4)
        self.MV = kb[:, 1024:2048].rearrange("p (n f) -> p n f", n=2)
        self.cmraw = kb[:, 2048:6144].rearrange("p (s n f) -> p s n f", s=4, n=2)
        self.CMK = vb[:, 0:4096].rearrange("p (s h t) -> p s h t", s=4, h=4)
        self.CMV = vb[:, 4096:8192].rearrange("p (s n f) -> p s n f", s=4, n=2)
        self.m1 = self.hT[:, 0:8, :]
        self.m2 = self.hT[:, 8:16, :]
        self.qm = self.hT[:, 16:20, :]
        self.om = self.uT[:, 0:4, :]
        self.PT3 = self.hT[:, 20:22, :].rearrange("p c t -> p (c t)")
        self.yT = self.hT[:, :, :].rearrange("p c t -> p (c t)")[:, 0:8 * TBMAX * 2].bitcast(F32).rearrange("p (c t) -> p c t", c=8)
        yr = self.arena[:, self.o_ropet:self.o_ropet + 5120]
        self.ystg = [yr[:, 0:2048].bitcast(F32), yr[:, 2048:4096].bitcast(F32)]
        self.yjunk = yr[:, 4096:5120]
        self.grow = qa[:, 5632:7680].bitcast(F32)
        self.fstat = qa[:, 7680:7696].bitcast(F32)

    def mem_kv3(self):
        S = self.S
        TB = 256
        self.load_x(self.xm, 0, TB, "m")
        self.rmsnorm(TB, 3)
        for wi, wn in enumerate(("mk", "mv")):
            def epi(j, ps, pk, wi=wi):
                t1, k1 = self.ntmp()
                S.op("act", lambda e: e.activation(out=t1[:, 0:TB], in_=ps[:, 0:TB], func=AF.Copy), reads=[pk], writes=[k1])
                if wi == 0 and MKV:
                    S.op("dve", lambda e: e.tensor_copy(out=self.MK[:, j, :], in_=t1[:, 0:TB]), reads=[k1], writes=["MK"])
                ps2, pk2 = self.nps()

                def tr(e):
                    ins = None
                    for t in range(2):
                        ins = e.transpose(out=ps2[:, t * 128:(t + 1) * 128], in_=t1[:, t * 128:(t + 1) * 128], identity=self.ident[:])
                    return ins
                S.op("pe", tr, reads=[k1, "ident"], writes=[pk2])
                i = self.uid % 2
                self.uid += 1
                st, sk = self.stg[i], "stg%d" % i
                S.op("dve", lambda e: e.tensor_copy(out=st[:, 0:TB], in_=ps2[:, 0:TB]), reads=[pk2], writes=[sk])
                if wi == 1 and MKV:
                    S.op("act", lambda e: e.activation(out=self.MV[:, :, j * 128:(j + 1) * 128],
                                                       in_=st[:, 0:TB].rearrange("p (n c) -> p n c", c=128), func=AF.Copy),
                         reads=[sk], writes=["MV"])
                oc = wi * 512 + j * 128
                S.dma("pool", lambda e: e.dma_start(out=self.o_mem[:, oc:oc + 128].rearrange("(n p) c -> p n c", p=128),
                                                  in_=st[:, 0:TB].rearrange("p (n c) -> p n c", c=128)),
                      reads=[sk], sem="okv%d" % i)
            self.proj(wn, 8, self.uT, ["uT"], TB, 0, 512, epi)

    def cross_prompt(self, TBp):
        S = self.S
        for h in range(4):
            ps, pk = self.nps()

            def sc(e, ps=ps, h=h):
                ins = None
                for n in range(2):
                    ins = e.matmul(ps[:, n * 512:n * 512 + TBp], lhsT=self.MK[:, h, n * 128:(n + 1) * 128], rhs=self.qm[:, h, 0:TBp], start=True, stop=True)
                return ins
            S.op("pe", sc, reads=["MK", "hT"], writes=[pk])
            PT = self.PT3
            S.op("act", lambda e, ps=ps: e.activation(out=PT[:, 0:1024].rearrange("p (n t) -> p n t", n=2)[:, :, 0:TBp],
                                                      in_=ps[:, :].rearrange("p (n t) -> p n t", n=2)[:, :, 0:TBp], func=AF.Exp, scale=MEM_SCALE),
                 reads=[pk], writes=["PT3"])
            po, pok = self.nps()

            def pv(e, po=po, h=h):
                ins = None
                for n in range(2):
                    e.matmul(po[:, 0:TBp], lhsT=self.MV[:, n, h * 128:(h + 1) * 128], rhs=PT[:, n * 512:n * 512 + TBp], start=(n == 0), stop=(n == 1))
                for n in range(2):
                    ins = e.matmul(po[:, 512:512 + TBp], lhsT=self.ones1[:], rhs=PT[:, n * 512:n * 512 + TBp], start=(n == 0), stop=(n == 1))
                return ins
            S.op("pe", pv, reads=["MV", "PT3", "ones1"], writes=[pok])
            t1, k1 = self.ntmp()
            S.op("dve", lambda e, po=po, t1=t1: e.reciprocal(out=t1[:, 0:TBp], in_=po[:, 512:512 + TBp]), reads=[pok], writes=[k1])
            S.op("dve", lambda e, po=po, t1=t1, h=h: e.tensor_tensor(out=self.om[:, h, 0:TBp], in0=po[:, 0:TBp], in1=t1[:, 0:TBp], op=ALU.mult),
                 reads=[pok, k1], writes=["uT"])

    def cross_sample(self, b0):
        S = self.S
        PT = self.PT3
        for sg in range(4):
            S.dma("pool", lambda e, sg=sg: e.dma_start(out=self.cmraw, in_=self.cmk[sg * 4:(sg + 1) * 4].rearrange("s (n p) f -> p s n f", p=128)),
                  writes=["cmraw"], sem="cmr")
            S.dma("pool", lambda e, sg=sg: e.dma_start(out=self.CMV, in_=self.cmv[sg * 4:(sg + 1) * 4].rearrange("s (n p) f -> p s n f", p=128)),
                  writes=["CMV"], sem="cmv")
            for si in range(4):
                for half in range(2):
                    pass
                ps, pk = self.nps()

                def tr(e, ps=ps, si=si):
                    ins = None
                    for h in range(4):
                        for n in range(2):
                            ins = e.transpose(out=self.psb(ps, h * 2 + n), in_=self.cmraw[:, si, n, h * 128:(h + 1) * 128], identity=self.identb[:])
                    return ins
                S.op("pe", tr, reads=["cmraw", "identb"], writes=[pk])
                S.op("act", lambda e, ps=ps, si=si: e.activation(out=self.CMK[:, si, :, :], in_=ps[:, 0:512].bitcast(BF16).rearrange("p (h t) -> p h t", h=4),
                                                               func=AF.Copy), reads=[pk], writes=["CMK"])
            ps, pk = self.nps()

            def sc(e, ps=ps, sg=sg):
                ins = None
                for n in range(2):
                    for si in range(4):
                        for h in range(4):
                            c = b0 + (sg * 4 + si) * 8
                            o = n * 128 + si * 32 + h * 8
                            ins = e.matmul(ps[:, o:o + 8], lhsT=self.CMK[:, si, h, n * 128:(n + 1) * 128], rhs=self.qm[:, h, c:c + 8], start=True, stop=True)
                return ins
            S.op("pe", sc, reads=["CMK", "hT"], writes=[pk])
            S.op("act", lambda e, ps=ps: e.activation(out=PT[:, 0:256], in_=ps[:, 0:256], func=AF.Exp, scale=MEM_SCALE), reads=[pk], writes=["PT3"])
            po, pok = self.nps()

            def pv(e, po=po):
                ins = None
                for si in range(4):
                    for h in range(4):
                        o = si * 32 + h * 8
                        for n in range(2):
                            ins = e.matmul(po[:, o:o + 8], lhsT=self.CMV[:, si, n, h * 128:(h + 1) * 128], rhs=PT[:, n * 128 + o:n * 128 + o + 8],
                                           start=(n == 0), stop=(n == 1))
                for n in range(2):
                    ins = e.matmul(po[:, 512:640], lhsT=self.ones1[:], rhs=PT[:, n * 128:(n + 1) * 128], start=(n == 0), stop=(n == 1))
                return ins
            S.op("pe", pv, reads=["CMV", "PT3", "ones1"], writes=[pok])
            t1, k1 = self.ntmp()
            S.op("dve", lambda e, po=po, t1=t1: e.reciprocal(out=t1[:, 0:128], in_=po[:, 512:640]), reads=[pok], writes=[k1])
            c = b0 + sg * 32
            S.op("dve", lambda e, po=po, t1=t1, c=c: e.tensor_tensor(
                out=self.om[:, 0:4, c:c + 32].rearrange("p h (s t) -> p h s t", s=4),
                in0=po[:, 0:128].rearrange("p (s h t) -> p h s t", s=4, h=4),
                in1=t1[:, 0:128].rearrange("p (s h t) -> p h s t", s=4, h=4), op=ALU.mult),
                reads=[pok, k1], writes=["uT"])

    def phase3(self):
        S = self.S
        self.carve3()
        self.half_ok = False
        S.dma("sp", lambda e: e.dma_start(out=self.grow[:, :], in_=self.gfin_d[:, :]), writes=["grow"], sem="c0")
        self.mem_kv3()
        if P3 < 2:
            return
        for bi, (t0, TB) in enumerate(OWN_BLOCKS):
            self.half_ok = TB <= 512
            S.dma("sp", lambda e, t0=t0, TB=TB: e.dma_start(out=self.xT[:, :, 0:TB], in_=self.hs[:, :, t0:t0 + TB]), writes=["xT"], sem="hsin")
            nfin = self.norm_begin(TB, 1)
            rs = self.rstd
            def epi_ga(j, ps, pk, TB=TB):
                S.op("dve", lambda e: e.tensor_tensor(out=self.m1[:, j, 0:TB], in0=ps[:, 0:TB], in1=rs[:, 0:TB], op=ALU.mult),
                     reads=[pk, "rstd"], writes=["hT"])
                S.op("act", lambda e: e.activation(out=self.m1[:, j, 0:TB], in_=self.m1[:, j, 0:TB], func=AF.Sigmoid), reads=["hT"], writes=["hT"])
            self.proj("gate", 8, self.uT, ["uT"], TB, 0, 1024, epi_ga, after_first=nfin)

            def epi_ba(j, ps, pk, TB=TB):
                S.op("dve", lambda e: e.tensor_tensor(out=self.m1[:, j, 0:TB], in0=ps[:, 0:TB], in1=self.m1[:, j, 0:TB], op=ALU.mult),
                     reads=[pk, "hT"], writes=["hT"])
            self.proj("ba", 4, self.OA[:, :, t0:t0 + TB], ["OAB"], TB, 0, 1024, epi_ba)

            def epi_gb(j, ps, pk, TB=TB):
                S.op("dve", lambda e: e.tensor_tensor(out=self.m2[:, j, 0:TB], in0=ps[:, 0:TB], in1=rs[:, 0:TB], op=ALU.mult),
                     reads=[pk, "rstd"], writes=["hT"])
                S.op("act", lambda e: e.activation(out=self.m2[:, j, 0:TB], in_=self.m2[:, j, 0:TB], func=AF.Sigmoid), reads=["hT"], writes=["hT"])
            self.proj("gate", 8, self.uT, ["uT"], TB, 1024, 1024, epi_gb)

            def epi_bb(j, ps, pk, TB=TB):
                t1, k1 = self.ntmp()
                S.op("dve", lambda e: e.tensor_tensor(out=t1[:, 0:TB], in0=ps[:, 0:TB], in1=self.m2[:, j, 0:TB], op=ALU.mult),
                     reads=[pk, "hT"], writes=[k1])
                S.op("dve", lambda e: e.tensor_tensor(out=self.m1[:, j, 0:TB], in0=t1[:, 0:TB], in1=self.m1[:, j, 0:TB], op=ALU.add),
                     reads=[k1, "hT"], writes=["hT"])
            self.proj("bb", 2, self.OB[:, :, t0:t0 + TB], ["OAB"], TB, 0, 1024, epi_bb)

            def epi_add(j, ps, pk, TB=TB):
                S.op("dve", lambda e: e.tensor_tensor(out=self.xT[:, j, 0:TB], in0=ps[:, 0:TB], in1=self.xT[:, j, 0:TB], op=ALU.add),
                     reads=[pk, "xT"], writes=["xT"])
            self.proj("out", 8, self.m1, ["hT"], TB, 0, 1024, epi_add)
            if P3 < 3:
                continue
            nfin2 = self.norm_begin(TB, 2)

            def epi_q(j, ps, pk, TB=TB):
                S.op("dve", lambda e: e.tensor_tensor(out=self.qm[:, j, 0:TB], in0=ps[:, 0:TB], in1=rs[:, 0:TB], op=ALU.mult),
                     reads=[pk, "rstd"], writes=["hT"])
            self.proj("mq", 8, self.uT, ["uT"], TB, 0, 512, epi_q, after_first=nfin2)
            self.cross_prompt(512)
            if P3 < 4:
                continue
            if TB > 512:
                self.cross_sample(512)
            self.proj("mo", 4, self.om, ["uT"], TB, 0, 1024, epi_add)
            if P3 < 5:
                continue
            self.ffn(TB, "g2", "u2", "d2", self.norm_begin(TB, 4))
            if P3 < 6:
                continue
            for t in range(TB // 128):
                ps, pk = self.nps()

                def tr(e, ps=ps, t=t):
                    ins = None
                    for c in range(8):
                        ins = e.transpose(out=ps[:, c * 128:(c + 1) * 128], in_=self.xT[:, c, t * 128:(t + 1) * 128], identity=self.ident[:])
                    return ins
                S.op("pe", tr, reads=["xT", "ident"], writes=[pk])
                i = t % 2
                st, sk = self.ystg[i], "ystg%d" % i
                ss = self.fstat[:, 2 * i:2 * i + 1]
                fk = "fstat%d" % i
                S.op("dve", lambda e, ss=ss: e.memset(ss, 0.0), writes=[fk])
                S.op("act", lambda e, ps=ps, ss=ss: e.activation(out=self.yjunk[:, :], in_=ps[:, :], func=AF.Square, accum_out=ss),
                     reads=[pk, fk], writes=[fk, "yjunk", "stg0", "stg1"])
                S.op("act", lambda e, ss=ss: e.activation(out=ss, in_=ss, func=AF.Sqrt, bias=self.epsb[:, 0:1], scale=1.0 / 1024),
                     reads=[fk, "consts"], writes=[fk])
                S.op("dve", lambda e, ss=ss: e.reciprocal(out=ss, in_=ss), reads=[fk], writes=[fk])
                S.op("dve", lambda e, ps=ps, st=st, ss=ss: e.scalar_tensor_tensor(out=st[:, :], in0=ps[:, :], scalar=ss, in1=self.grow[:, :],
                                                                                op0=ALU.mult, op1=ALU.mult),
                     reads=[pk, fk, "grow"], writes=[sk, "stg0", "stg1"] if i == 1 else [sk])
                r0 = t0 + t * 128
                S.dma("pool", lambda e, st=st, r0=r0: e.dma_start(out=self.y[r0:r0 + 128, :], in_=st[:, :]), reads=[sk], sem="yo%d" % i)

    def build(self):
        self.consts()
        self.phase1()
        self.wfence()
        self.S.barrier()
        if STAGE >= 2:
            self.phase2()
            self.S.barrier()
        if STAGE >= 3:
            self.phase3()


def build_program():
    nc0 = bass.Bass("TRN2", target_bir_lowering=False)
    with ExitStack() as es0:
        p0 = Prog(nc0, es0, Sch(nc0, es0, dummy=True))
        p0.alloc()
        p0.build()
        jobs = p0.wjobs
    nc = bass.Bass("TRN2", target_bir_lowering=False)
    with ExitStack() as es:
        S = Sch(nc, es)
        p = Prog(nc, es, S, wjobs=jobs)
        p.alloc()
        p.build()
        assert p.wi == len(jobs)
        S.finalize()
    return nc


def _rope_tables(pos):
    half = 8
    inv_freq = np.power(np.float32(500000.0), -np.arange(half, dtype=np.float32) / np.float32(half)).astype(np.float32)
    ang = pos.astype(np.float32)[:, None] * inv_freq[None, :]
    cos = np.cos(ang).astype(np.float32)
    sin = np.sin(ang).astype(np.float32)
    T = pos.shape[0]
    tab = np.zeros((2, 128, T), np.float32)
    tab[0] = 1.0
    for hh in range(2):
        b = hh * 64
        tab[0, b:b + 8] = cos.T
        tab[0, b + 8:b + 16] = cos.T
        tab[1, b:b + 8] = -sin.T
        tab[1, b + 8:b + 16] = sin.T
    return tab


def _swap_cols(wcols):
    w = wcols.reshape(wcols.shape[0], -1, 64).copy()
    a = w[:, :, 0:8].copy()
    w[:, :, 0:8] = w[:, :, 8:16]
    w[:, :, 8:16] = a
    return w.reshape(wcols.shape)


def _interleave(z, s):
    K = z.shape[0]
    n = z.shape[1] // 128
    return np.stack([z.reshape(K, n, 128), s.reshape(K, n, 128)], axis=2).reshape(K, 2 * n * 128)


def _mask_s():
    mk = np.zeros((128, 408), np.float32)
    for g, d in enumerate((1, 4, 16)):
        for t in range(8):
            for n in range(16):
                pos = n * 128 + np.arange(128)
                diff = 2048 + t - pos
                ok = (diff % d == 0) & (diff // d >= 1) & (diff // d <= 128)
                mk[:, n * 24 + g * 8 + t] = ok
            for tp in range(8):
                diff = t - tp
                mk[tp, 384 + g * 8 + t] = float(diff >= 0 and diff % d == 0 and diff // d <= 128)
    return mk


MASK_S = _mask_s()


def make_in_maps(inp):
    f = lambda k: np.ascontiguousarray(np.asarray(inp[k], dtype=np.float32))
    x_prompt, x_sample = f("x_prompt"), f("x_sample")
    w_in = f("w_in")[0]
    qa, ka, va = w_in[:, 0:512], w_in[:, 512:640], w_in[:, 640:768]
    qb, kb, vb = w_in[:, 768:1536], w_in[:, 1536:1792], w_in[:, 1792:2048]
    gate = w_in[:, 2048:4096]
    perm = [h for c in range(4) for h in (c, c + 4)]
    qa_p = qa.reshape(D, 8, 64)[:, perm, :].reshape(D, 512)
    w_q = np.ascontiguousarray(np.concatenate([_interleave(qa_p, _swap_cols(qa_p)), _interleave(qb, _swap_cols(qb))], axis=1))
    w_kv = np.ascontiguousarray(np.concatenate([_interleave(kb, _swap_cols(kb)), vb, va, ka, _swap_cols(ka)], axis=1))
    w_ba = np.ascontiguousarray(f("w_branch_a")[0].reshape(8, 64, D)[perm].reshape(512, D))
    gn = np.stack([f("ffn1_norm")[0], f("mix_norm")[0], f("mem_q_norm")[0], f("mem_kv_norm")[0], f("ffn2_norm")[0],
                   f("final_norm")], 0)
    gn = np.ascontiguousarray(gn.reshape(6, 8, 128).transpose(2, 0, 1).reshape(128, 48))
    sink = f("attn_sink")[0]
    gfin_row = np.ascontiguousarray(np.broadcast_to(f("final_norm")[None, :], (128, 1024)))
    sinkb = np.ascontiguousarray(np.repeat(sink.reshape(2, 4, 1), 128, axis=2).reshape(2, 512))
    shared = {
        "w_g1": f("ffn1_w_gate")[0], "w_u1": f("ffn1_w_up")[0], "w_d1": f("ffn1_w_down")[0],
        "w_g2": f("ffn2_w_gate")[0], "w_u2": f("ffn2_w_up")[0], "w_d2": f("ffn2_w_down")[0],
        "w_q": w_q, "w_kv": w_kv, "w_gate": np.ascontiguousarray(gate), "w_ba": w_ba, "w_bb": f("w_branch_b")[0],
        "w_out": f("w_out")[0], "w_mq": f("w_mem_q")[0], "w_mk": f("w_mem_k")[0], "w_mv": f("w_mem_v")[0],
        "w_mo": f("w_mem_o")[0], "gn": gn, "ident": np.eye(128, dtype=np.float32), "sinkb": sinkb,
    }
    jj = np.arange(128)[:, None]
    ii = np.arange(128)[None, :]
    csk, csv = f("cache_swa_k")[0], f("cache_swa_v")[0]
    cdk, cdv = f("cache_dil_k")[0], f("cache_dil_v")[0]
    cmk, cmv = f("cache_mem_k")[0], f("cache_mem_v")[0]
    mem_prompt = f("mem_prompt")
    in_maps = []
    for c in range(8):
        n, ch = c // 4, c % 4
        m = dict(shared)
        xo = np.concatenate([x_prompt[n, ch * 2048:(ch + 1) * 2048], x_sample[c * 16:(c + 1) * 16].reshape(128, D)], 0)
        m["xo"] = np.ascontiguousarray(xo)
        m["xh"] = np.ascontiguousarray(x_prompt[n, (ch - 1) * 2048:ch * 2048]) if ch > 0 else np.zeros((2048, D), np.float32)
        m["xm"] = np.ascontiguousarray(mem_prompt[n])
        pos_o = np.concatenate([ch * 2048 + np.arange(2048), np.tile(16384 + np.arange(8), 16)])
        m["rope_o"] = _rope_tables(pos_o)
        m["rope_h"] = _rope_tables(np.maximum((ch - 1) * 2048 + np.arange(2048), 0))
        masks = np.zeros((128, 8, 128), np.float32)
        masks[:, 0] = (jj <= ii)
        masks[:, 1] = (jj >= ii)
        masks[:, 2] = (jj <= ii)
        masks[:, 3] = (jj >= ii) * (1.0 if ch > 0 else 0.0)
        masks[:, 4, 0:8] = (jj >= ii)[:, 0:8]
        masks[0:8, 5, 0:8] = (jj <= ii)[0:8, 0:8]
        m["masks"] = masks
        m["masks_s"] = MASK_S
        m["gfin_row"] = gfin_row
        sl = slice(c * 16, (c + 1) * 16)
        m["cswk"] = np.ascontiguousarray(csk[sl].reshape(16, 128, 128))
        m["cswv"] = np.ascontiguousarray(csv[sl].reshape(16, 128, 128))
        m["cdk"] = np.ascontiguousarray(cdk[sl].reshape(16, 2048, 256))
        m["cdv"] = np.ascontiguousarray(cdv[sl].reshape(16, 2048, 256))
        m["cmk"] = np.ascontiguousarray(cmk[sl].reshape(16, 256, 512))
        m["cmv"] = np.ascontiguousarray(cmv[sl].reshape(16, 256, 512))
        in_maps.append(m)
    return in_maps


def kernel(**inp):
    in_maps = make_in_maps(inp)
    nc = build_program()
    res = run_bass_kernel_spmd(nc, in_maps, core_ids=list(range(8)))
    R = res.results
    D = 1024
    y_prompt = np.zeros((2, 8192, D), np.float32)
    y_sample = np.zeros((128, 8, D), np.float32)
    swa_k_p = np.zeros((1, 2, 128, 2, 64), np.float32)
    swa_v_p = np.zeros((1, 2, 128, 2, 64), np.float32)
    dil_k_p = np.zeros((1, 2, 2048, 4, 64), np.float32)
    dil_v_p = np.zeros((1, 2, 2048, 4, 64), np.float32)
    mem_k_p = np.zeros((1, 2, 256, 4, 128), np.float32)
    mem_v_p = np.zeros((1, 2, 256, 4, 128), np.float32)
    swa_k_s = np.zeros((1, 128, 8, 2, 64), np.float32)
    swa_v_s = np.zeros((1, 128, 8, 2, 64), np.float32)
    dil_k_s = np.zeros((1, 128, 8, 4, 64), np.float32)
    dil_v_s = np.zeros((1, 128, 8, 4, 64), np.float32)
    for c in range(8):
        n, ch = c // 4, c % 4
        y, okv, om = R[c]["y"], R[c]["o_kv"], R[c]["o_mem"]
        y_prompt[n, ch * 2048:(ch + 1) * 2048] = y[0:2048]
        y_sample[c * 16:(c + 1) * 16] = y[2048:].reshape(16, 8, D)
        s = okv[2048:]
        swa_k_s[0, c * 16:(c + 1) * 16] = s[:, 0:128].reshape(16, 8, 2, 64)
        dil_k_s[0, c * 16:(c + 1) * 16] = s[:, 128:384].reshape(16, 8, 4, 64)
        swa_v_s[0, c * 16:(c + 1) * 16] = s[:, 384:512].reshape(16, 8, 2, 64)
        dil_v_s[0, c * 16:(c + 1) * 16] = s[:, 512:768].reshape(16, 8, 4, 64)
        if ch == 3:
            p = okv[0:2048]
            swa_k_p[0, n] = p[1920:, 0:128].reshape(128, 2, 64)
            swa_v_p[0, n] = p[1920:, 384:512].reshape(128, 2, 64)
            dil_k_p[0, n] = p[:, 128:384].reshape(2048, 4, 64)
            dil_v_p[0, n] = p[:, 512:768].reshape(2048, 4, 64)
        if ch == 0:
            mem_k_p[0, n] = om[:, 0:512].reshape(256, 4, 128)
            mem_v_p[0, n] = om[:, 512:1024].reshape(256, 4, 128)
    return (y_prompt, y_sample, swa_k_p, swa_v_p, dil_k_p, dil_v_p, mem_k_p, mem_v_p, swa_k_s, swa_v_s, dil_k_s, dil_v_s)
```
